# Optimizing a Trainium2 kernel written in Bass

```python
import math
import jax, jax.numpy as jnp
from jax import lax
import numpy as np

D_MODEL = 1024
BATCH = 1
SEQ = 16384
DEPTH = 4

HEAD_DIM = 64
GRID_W = 64
NORM_EPS = 1e-6
NA_HEADS = 4
NA_KH = 8
NA_KW = 16
SWA_Q_HEADS = 8
SWA_KV_HEADS = 2
SWA_WINDOW = 128
SWA_BLOCK = 128
ROPE_THETA = 10000.0
GDN_HEADS = 4
GDN_DK = 64
GDN_DV = 64
GDN_CONV = 3
GDN_CHUNK = 64
D_FF = 2816
FFN_CONV = 3

NA_WIDTH = NA_HEADS * HEAD_DIM
SWA_Q_WIDTH = SWA_Q_HEADS * HEAD_DIM
SWA_KV_WIDTH = SWA_KV_HEADS * HEAD_DIM
GDN_QK_WIDTH = GDN_HEADS * GDN_DK
GDN_V_WIDTH = GDN_HEADS * GDN_DV
N_BRANCH = 3
IN_SPLITS = (NA_WIDTH, NA_WIDTH, NA_WIDTH,
             SWA_Q_WIDTH, SWA_KV_WIDTH, SWA_KV_WIDTH,
             2 * GDN_QK_WIDTH + GDN_V_WIDTH, GDN_V_WIDTH, 2 * GDN_HEADS, 2 * GDN_HEADS,
             N_BRANCH * D_MODEL)
IN_WIDTH = 3 * NA_WIDTH + SWA_Q_WIDTH + 2 * SWA_KV_WIDTH + 2 * GDN_QK_WIDTH + 2 * GDN_V_WIDTH + 4 * GDN_HEADS + N_BRANCH * D_MODEL

kernel_name = "hybrid_na_swa_gdn_encoder"


def rms_norm(x, g):
    xf = x.astype(jnp.float32)
    y = xf * lax.rsqrt(jnp.mean(xf * xf, axis=-1, keepdims=True) + NORM_EPS)
    return (y * g.astype(jnp.float32)).astype(x.dtype)


def l2_norm(x):
    xf = x.astype(jnp.float32)
    return (xf * lax.rsqrt(jnp.sum(xf * xf, axis=-1, keepdims=True) + NORM_EPS)).astype(x.dtype)


def split_heads(t, n_heads):
    return t.reshape(t.shape[0], t.shape[1], n_heads, -1)


def depthwise_conv_centred(x, w):
    k_size = w.shape[0]
    half = k_size // 2
    length = x.shape[1]
    xp = jnp.pad(x, ((0, 0), (half, half), (0, 0)))
    y = xp[:, 0:length] * w[0]
    for i in range(1, k_size):
        y = y + xp[:, i:i + length] * w[i]
    return y


def rope_tables(length):
    inv = 1.0 / (ROPE_THETA ** (jnp.arange(0, HEAD_DIM, 2, dtype=jnp.float32) / HEAD_DIM))
    ang = jnp.arange(length, dtype=jnp.float32)[:, None] * inv[None, :]
    return jnp.cos(ang), jnp.sin(ang)


def apply_rope(x, cos, sin):
    xf = x.astype(jnp.float32)
    half = HEAD_DIM // 2
    x1, x2 = xf[..., :half], xf[..., half:]
    c, s = cos[None, :, None, :], sin[None, :, None, :]
    return jnp.concatenate([x1 * c - x2 * s, x2 * c + x1 * s], axis=-1).astype(x.dtype)


def neighborhood_attention_2d(q, k, v, rpb):
    B, L, H, d = q.shape
    rows = L // GRID_W
    kh = min(NA_KH, rows)
    r = jnp.arange(rows)
    row_start = jnp.clip(r - kh // 2, 0, rows - kh)
    row_idx = row_start[:, None] + jnp.arange(kh)
    c = jnp.arange(GRID_W)
    col_start = jnp.clip(c - NA_KW // 2, 0, GRID_W - NA_KW)
    col_in = (c[None, :] >= col_start[:, None]) & (c[None, :] < col_start[:, None] + NA_KW)
    qg = q.reshape(B, rows, GRID_W, H, d)
    kg = k.reshape(B, rows, GRID_W, H, d)[:, row_idx]
    vg = v.reshape(B, rows, GRID_W, H, d)[:, row_idx]
    s = jnp.einsum('brqhd,brjkhd->brhqjk', qg, kg).astype(jnp.float32) * (HEAD_DIM ** -0.5)
    dr = row_idx - r[:, None] + (NA_KH - 1)
    dc = jnp.clip(c[None, :] - c[:, None] + (NA_KW - 1), 0, 2 * NA_KW - 2)
    bias = rpb.astype(jnp.float32)[:, dr[:, None, :, None], dc[None, :, None, :]]
    s = jnp.where(col_in[:, None, :], s + jnp.moveaxis(bias, 0, 1), -jnp.inf)
    p = jax.nn.softmax(s, axis=(-2, -1)).astype(v.dtype)
    return jnp.einsum('brhqjk,brjkhd->brqhd', p, vg).reshape(B, L, H, d)


def sliding_window_gqa(q, k, v, sink):
    B, L, Hq, d = q.shape
    Hkv = k.shape[2]
    G = Hq // Hkv
    blk = SWA_BLOCK
    nb = L // blk
    def key_blocks(t):
        tp = jnp.pad(t, ((0, 0), (blk, blk), (0, 0), (0, 0))).reshape(B, nb + 2, blk, Hkv, d)
        return jnp.concatenate([tp[:, :-2], tp[:, 1:-1], tp[:, 2:]], axis=2)
    kb, vb = key_blocks(k), key_blocks(v)
    qb = q.reshape(B, nb, blk, Hkv, G, d)
    s = jnp.einsum('bnqhgd,bnkhd->bnhgqk', qb, kb).astype(jnp.float32) * (HEAD_DIM ** -0.5)
    i = jnp.arange(blk)[:, None]
    j = jnp.arange(3 * blk)[None, :]
    kpos = jnp.arange(nb)[:, None, None] * blk - blk + j
    valid = (jnp.abs(j - blk - i) <= SWA_WINDOW) & (kpos >= 0) & (kpos < L)
    s = jnp.where(valid[None, :, None, None], s, -jnp.inf)
    sink_col = jnp.broadcast_to(sink.astype(jnp.float32).reshape(Hkv, G)[None, None, :, :, None, None],
                                s.shape[:-1] + (1,))
    p = jax.nn.softmax(jnp.concatenate([s, sink_col], axis=-1), axis=-1)[..., :-1].astype(v.dtype)
    return jnp.einsum('bnhgqk,bnkhd->bnqhgd', p, vb).reshape(B, L, Hq, d)


def gated_delta_rule_chunked(q, k, v, beta, g):
    out_dtype = v.dtype
    B, L, H, dk = q.shape
    dv = v.shape[-1]
    C = GDN_CHUNK
    n = L // C
    def chunks(t):
        t = jnp.moveaxis(t.astype(jnp.float32), 2, 1)
        return t.reshape((B, H, n, C) + t.shape[3:])
    q, k, v, beta, g = chunks(q), chunks(k), chunks(v), chunks(beta), chunks(g)
    g = jnp.cumsum(g, axis=-1)
    incl = jnp.tril(jnp.ones((C, C), dtype=bool))
    strict = jnp.tril(jnp.ones((C, C), dtype=bool), -1)
    decay = jnp.exp(jnp.where(incl, g[..., :, None] - g[..., None, :], -jnp.inf))
    k_beta = k * beta[..., None]
    lower = jnp.where(strict, jnp.einsum('bhnid,bhnjd->bhnij', k_beta, k) * decay, 0.0)
    rhs = jnp.concatenate([v * beta[..., None], k_beta * jnp.exp(g)[..., None]], axis=-1)
    sol = lax.linalg.triangular_solve(lower + jnp.eye(C, dtype=jnp.float32), rhs,
                                      left_side=True, lower=True, unit_diagonal=True)
    u, w = sol[..., :dv], sol[..., dv:]
    qk = jnp.einsum('bhnid,bhnjd->bhnij', q, k) * decay
    q_dec = q * jnp.exp(g)[..., None]
    k_tail = k * jnp.exp(g[..., -1:] - g)[..., None]
    chunk_dec = jnp.exp(g[..., -1])
    def step(S, xs):
        qk_c, q_c, w_c, u_c, k_c, d_c = xs
        v_new = u_c - jnp.einsum('bhck,bhkv->bhcv', w_c, S)
        o_c = jnp.einsum('bhck,bhkv->bhcv', q_c, S) + jnp.einsum('bhij,bhjv->bhiv', qk_c, v_new)
        S = S * d_c[..., None, None] + jnp.einsum('bhck,bhcv->bhkv', k_c, v_new)
        return S, o_c
    xs = tuple(jnp.moveaxis(t, 2, 0) for t in (qk, q_dec, w, u, k_tail, chunk_dec))
    _, o = lax.scan(step, jnp.zeros((B, H, dk, dv), jnp.float32), xs)
    o = jnp.moveaxis(o, 0, 2).reshape(B, H, L, dv)
    return jnp.moveaxis(o, 1, 2).astype(out_dtype)


def gated_deltanet_bidir(qkv, z, beta_logit, a_logit, conv_w, a_log, dt_bias, norm_g):
    B, L, _ = qkv.shape
    qkv = jax.nn.silu(depthwise_conv_centred(qkv, conv_w))
    q, k, v = jnp.split(qkv, [GDN_QK_WIDTH, 2 * GDN_QK_WIDTH], axis=-1)
    q = l2_norm(q.reshape(B, L, GDN_HEADS, GDN_DK)) * (GDN_DK ** -0.5)
    k = l2_norm(k.reshape(B, L, GDN_HEADS, GDN_DK))
    v = v.reshape(B, L, GDN_HEADS, GDN_DV)
    beta = jax.nn.sigmoid(beta_logit.astype(jnp.float32)).reshape(B, L, 2, GDN_HEADS)
    g = -jnp.exp(a_log.astype(jnp.float32)) * jax.nn.softplus(
        a_logit.astype(jnp.float32).reshape(B, L, 2, GDN_HEADS) + dt_bias.astype(jnp.float32))
    flip = lambda t: jnp.flip(t, axis=1)
    o_fwd = gated_delta_rule_chunked(q, k, v, beta[:, :, 0], g[:, :, 0])
    o_bwd = flip(gated_delta_rule_chunked(flip(q), flip(k), flip(v), flip(beta[:, :, 1]), flip(g[:, :, 1])))
    o = rms_norm(o_fwd + o_bwd, norm_g) * jax.nn.silu(z.reshape(B, L, GDN_HEADS, GDN_DV))
    return o.reshape(B, L, GDN_V_WIDTH)


def setup_inputs(seed: int = 0) -> dict:
    key = jax.random.key(seed)
    ks = jax.random.split(key, 20)
    f32 = jnp.float32
    nrm = lambda k, shape, scale: jax.random.normal(k, shape, f32) * scale
    gain = lambda k, shape: 1.0 + 0.02 * jax.random.normal(k, shape, f32)
    dt = jnp.exp(jax.random.uniform(ks[8], (DEPTH, 2, GDN_HEADS), f32, math.log(1e-3), math.log(1e-1)))
    return {
        "x": jax.random.normal(ks[0], (BATCH, SEQ, D_MODEL), f32),
        "attn_norm": gain(ks[1], (DEPTH, D_MODEL)),
        "w_in": nrm(ks[2], (DEPTH, D_MODEL, IN_WIDTH), D_MODEL ** -0.5),
        "qk_norm": gain(ks[3], (DEPTH, 4, HEAD_DIM)),
        "na_rpb": nrm(ks[4], (DEPTH, NA_HEADS, 2 * NA_KH - 1, 2 * NA_KW - 1), 0.1),
        "swa_sink": nrm(ks[5], (DEPTH, SWA_Q_HEADS), 0.5),
        "gdn_conv_w": nrm(ks[6], (DEPTH, GDN_CONV, 2 * GDN_QK_WIDTH + GDN_V_WIDTH), GDN_CONV ** -0.5),
        "gdn_a_log": jnp.log(jax.random.uniform(ks[7], (DEPTH, 2, GDN_HEADS), f32, 1.0, 16.0)),
        "gdn_dt_bias": dt + jnp.log(-jnp.expm1(-dt)),
        "gdn_norm": gain(ks[9], (DEPTH, GDN_DV)),
        "w_branch_na": nrm(ks[10], (DEPTH, NA_WIDTH, D_MODEL), NA_WIDTH ** -0.5),
        "w_branch_swa": nrm(ks[11], (DEPTH, SWA_Q_WIDTH, D_MODEL), SWA_Q_WIDTH ** -0.5),
        "w_branch_gdn": nrm(ks[12], (DEPTH, GDN_V_WIDTH, D_MODEL), GDN_V_WIDTH ** -0.5),
        "w_out": nrm(ks[13], (DEPTH, D_MODEL, D_MODEL), D_MODEL ** -0.5),
        "ffn_norm": gain(ks[14], (DEPTH, D_MODEL)),
        "w_up": nrm(ks[15], (DEPTH, D_MODEL, 2 * D_FF), D_MODEL ** -0.5),
        "ffn_conv_w": nrm(ks[16], (DEPTH, FFN_CONV, 2 * D_FF), FFN_CONV ** -0.5),
        "ffn_conv_b": nrm(ks[17], (DEPTH, 2 * D_FF), 0.02),
        "w_down": nrm(ks[18], (DEPTH, D_FF, D_MODEL), D_FF ** -0.5),
    }


def reference(x, attn_norm, w_in, qk_norm, na_rpb, swa_sink, gdn_conv_w, gdn_a_log, gdn_dt_bias, gdn_norm,
              w_branch_na, w_branch_swa, w_branch_gdn, w_out, ffn_norm, w_up, ffn_conv_w, ffn_conv_b, w_down):
    B, L, _ = x.shape
    cos, sin = rope_tables(L)
    splits = np.cumsum(IN_SPLITS)[:-1].tolist()
    for l in range(DEPTH):
        h = rms_norm(x, attn_norm[l])
        (qa, ka, va, qs, ks_, vs, qkv_c, z_c, beta_c, a_c, gate_c) = jnp.split(h @ w_in[l], splits, axis=-1)
        qa = rms_norm(split_heads(qa, NA_HEADS), qk_norm[l, 0])
        ka = rms_norm(split_heads(ka, NA_HEADS), qk_norm[l, 1])
        y_na = neighborhood_attention_2d(qa, ka, split_heads(va, NA_HEADS), na_rpb[l]).reshape(B, L, NA_WIDTH)
        qs = apply_rope(rms_norm(split_heads(qs, SWA_Q_HEADS), qk_norm[l, 2]), cos, sin)
        ks_ = apply_rope(rms_norm(split_heads(ks_, SWA_KV_HEADS), qk_norm[l, 3]), cos, sin)
        y_swa = sliding_window_gqa(qs, ks_, split_heads(vs, SWA_KV_HEADS), swa_sink[l]).reshape(B, L, SWA_Q_WIDTH)
        y_gdn = gated_deltanet_bidir(qkv_c, z_c, beta_c, a_c, gdn_conv_w[l], gdn_a_log[l], gdn_dt_bias[l], gdn_norm[l])
        g_na, g_swa, g_gdn = jnp.split(jax.nn.sigmoid(gate_c), N_BRANCH, axis=-1)
        merged = (g_na * (y_na @ w_branch_na[l]) + g_swa * (y_swa @ w_branch_swa[l])
                  + g_gdn * (y_gdn @ w_branch_gdn[l]))
        x = x + merged @ w_out[l]
        h = rms_norm(x, ffn_norm[l])
        u = depthwise_conv_centred(h @ w_up[l], ffn_conv_w[l]) + ffn_conv_b[l]
        a, b = jnp.split(u, 2, axis=-1)
        x = x + (jax.nn.silu(a) * b) @ w_down[l]
    return x
```

```python
from contextlib import ExitStack
import numpy as np
import concourse.bass as bass
import concourse.mybir as mybir

F32 = mybir.dt.float32
BF16 = mybir.dt.bfloat16
AF = mybir.ActivationFunctionType
ALU = mybir.AluOpType
NDMA = 24


class V:
    def __init__(self, buf, ap):
        self.buf = buf
        self.ap = ap


class St:
    def __init__(self):
        self.w = None
        self.r = {}
        self.excl = False


class Buf:
    def __init__(self, t, name, st=None):
        self.t = t
        self.name = name
        self.st = st if st is not None else St()

    def sub(self, ap, name):
        return Buf(ap, name, self.st)

    def __getitem__(self, idx):
        return V(self, self.t[idx])

    def v(self, ap):
        return V(self, ap)


class Prog:
    def __init__(self):
        self.nc = bass.Bass("TRN2", target_bir_lowering=False)
        self.es = ExitStack()
        nc = self.nc
        self.engs = {"pe": nc.tensor, "act": nc.scalar, "dve": nc.vector, "pool": nc.gpsimd, "sp": nc.sync}
        self.semh = {}
        for e in self.engs:
            self.semh[e] = self.es.enter_context(nc.semaphore("s_" + e))
        for i in range(NDMA):
            self.semh[("dma", i)] = self.es.enter_context(nc.semaphore("s_dma%d" % i))
        self.cnt = {e: 0 for e in self.engs}
        self.dcnt = [0] * NDMA
        self.drr = 0
        self.seen = {e: {} for e in self.engs}
        self.ops = {e: [] for e in self.engs}
        self.nbuf = 0

    def sbuf(self, shape, dt, name=None):
        self.nbuf += 1
        name = (name or "sb") + "_%d" % self.nbuf
        t = self.es.enter_context(self.nc.sbuf_tensor(name, list(shape), dt))
        return Buf(t, name)

    def psum(self, shape, dt=F32, name=None):
        self.nbuf += 1
        name = (name or "ps") + "_%d" % self.nbuf
        t = self.es.enter_context(self.nc.psum_tensor(name, list(shape), dt))
        b = Buf(t, name)
        b.st.excl = True
        return b

    def dram(self, name, shape, dt, kind):
        t = self.nc.dram_tensor(name, list(shape), dt, kind=kind)
        return Buf(t.ap(), name)

    def _waits(self, e, reads, writes):
        waits = {}
        seen = self.seen[e]

        def need(k, v):
            if k is None or v <= 0:
                return
            if e == "pe" and k == "pe":
                return
            if seen.get(k, 0) >= v:
                return
            if waits.get(k, 0) < v:
                waits[k] = v

        for b in reads:
            if b.w is not None:
                need(*b.w)
            if b.excl:
                for k, v in b.r.items():
                    if k != e:
                        need(k, v)
        for b in writes:
            if b.w is not None:
                need(*b.w)
            for k, v in b.r.items():
                need(k, v)
        for k, v in waits.items():
            seen[k] = v
        return list(waits.items())

    def op(self, e, fn, reads, writes):
        reads = [(x.buf if isinstance(x, V) else x).st for x in reads]
        writes = [(x.buf if isinstance(x, V) else x).st for x in writes]
        wl = self._waits(e, reads, writes)
        self.cnt[e] += 1
        tok = (e, self.cnt[e])
        semh = self.semh

        def emit(eng):
            for k, v in wl:
                eng.wait_ge(semh[k], v)
            fn(eng).then_inc(semh[e], 1)

        self.ops[e].append(emit)
        for b in writes:
            b.w = tok
            b.r = {}
        for b in reads:
            if b not in writes:
                b.r[e] = tok[1]

    def dma(self, q, out, in_, **kw):
        i = self.drr
        self.drr = (i + 1) % NDMA
        key = ("dma", i)
        prev = self.dcnt[i]
        self.dcnt[i] += 16
        tok = (key, self.dcnt[i])
        wl = self._waits(q, [in_.buf.st], [out.buf.st])
        if prev > 0 and self.seen[q].get(key, 0) < prev:
            wl.append((key, prev))
            self.seen[q][key] = prev
        semh = self.semh
        oa, ia = out.ap, in_.ap

        def emit(eng):
            for k, v in wl:
                eng.wait_ge(semh[k], v)
            eng.dma_start(out=oa, in_=ia, **kw).then_inc(semh[key], 16)

        self.ops[q].append(emit)
        out.buf.st.w = tok
        out.buf.st.r = {}
        if in_.buf.st is not out.buf.st:
            in_.buf.st.r[key] = tok[1]

    def finish(self):
        wl = [(e, c) for e, c in self.cnt.items() if c > 0]
        wl += [(("dma", i), c) for i, c in enumerate(self.dcnt) if c > 0]
        semh = self.semh

        def emit(eng):
            for k, v in wl:
                eng.wait_ge(semh[k], v)

        self.ops["sp"].append(emit)
        ops = self.ops
        with self.nc.Block() as block:
            @block.tensor
            def _(e):
                for f in ops["pe"]:
                    f(e)

            @block.scalar
            def _(e):
                for f in ops["act"]:
                    f(e)

            @block.vector
            def _(e):
                for f in ops["dve"]:
                    f(e)

            @block.gpsimd
            def _(e):
                for f in ops["pool"]:
                    f(e)

            @block.sync
            def _(e):
                for f in ops["sp"]:
                    f(e)
        self.es.close()
        return self.nc

    def mm(self, out, lhsT, rhs, start=True, stop=True):
        self.op("pe", lambda e: e.matmul(out.ap, lhsT.ap, rhs.ap, start=start, stop=stop),
                [lhsT, rhs] + ([] if start else [out]), [out])

    def act(self, out, in_, func, bias=None, scale=None, eng="act"):
        kw = {}
        rd = [in_]
        if bias is not None:
            if isinstance(bias, V):
                kw["bias"] = bias.ap
                rd.append(bias)
            else:
                kw["bias"] = bias
        if scale is not None:
            if isinstance(scale, V):
                kw["scale"] = scale.ap
                rd.append(scale)
            else:
                kw["scale"] = scale
        self.op(eng, lambda e: e.activation(out.ap, in_.ap, func, **kw), rd, [out])

    def tt(self, out, in0, in1, op, eng="dve"):
        self.op(eng, lambda e: e.tensor_tensor(out.ap, in0.ap, in1.ap, op), [in0, in1], [out])

    def ts(self, out, in0, s1, op0, s2=None, op1=None, eng="dve"):
        rd = [in0]
        a1 = s1.ap if isinstance(s1, V) else s1
        a2 = s2.ap if isinstance(s2, V) else s2
        if isinstance(s1, V):
            rd.append(s1)
        if isinstance(s2, V):
            rd.append(s2)
        if op1 is None:
            self.op(eng, lambda e: e.tensor_scalar(out.ap, in0.ap, a1, None, op0), rd, [out])
        else:
            self.op(eng, lambda e: e.tensor_scalar(out.ap, in0.ap, a1, a2, op0, op1), rd, [out])

    def stt(self, out, in0, s, in1, op0, op1, eng="dve"):
        rd = [in0, in1]
        a = s.ap if isinstance(s, V) else s
        if isinstance(s, V):
            rd.append(s)
        self.op(eng, lambda e: e.scalar_tensor_tensor(out.ap, in0.ap, a, in1.ap, op0, op1), rd, [out])

    def copy(self, out, in_, eng="dve"):
        if eng == "act":
            self.op("act", lambda e: e.copy(out.ap, in_.ap), [in_], [out])
        else:
            self.op(eng, lambda e: e.tensor_copy(out.ap, in_.ap), [in_], [out])

    def recip(self, out, in_, eng="dve"):
        self.op(eng, lambda e: e.reciprocal(out.ap, in_.ap), [in_], [out])

    def memset(self, out, val, eng="pool"):
        self.op(eng, lambda e: e.memset(out.ap, val), [], [out])


def _barrier(self):
    wl_all = [(e, c) for e, c in self.cnt.items() if c > 0]
    wl_all += [(("dma", i), c) for i, c in enumerate(self.dcnt) if c > 0]
    semh = self.semh
    for e in self.engs:
        wl = [(k, v) for k, v in wl_all if self.seen[e].get(k, 0) < v and not (k == e)]
        for k, v in wl:
            self.seen[e][k] = v

        def emit(eng, wl=wl):
            for k, v in wl:
                eng.wait_ge(semh[k], v)

        self.ops[e].append(emit)


Prog.barrier = _barrier

from contextlib import contextmanager


@contextmanager
def _scope(self):
    old = self.es
    self.es = ExitStack()
    try:
        yield
    finally:
        self.barrier()
        self.es.close()
        self.es = old


Prog.scope = _scope


TOK = 2048
HALO = 256
NL = TOK + 2 * HALO
D = 1024
EPS = 1e-6


def qknorm(p, C, ps_in, N, gain, out, rope=None):
    p.act(C["sq"][0:64, 0:N], ps_in, AF.Square)
    p.mm(C["psm"][0:64, 0:N], C["ones64"][:, :], C["sq"][0:64, 0:N])
    p.act(C["r"][0:64, 0:N], C["psm"][0:64, 0:N], AF.Sqrt, bias=EPS)
    p.recip(C["r"][0:64, 0:N], C["r"][0:64, 0:N])
    if rope is None:
        p.stt(out, ps_in, gain, C["r"][0:64, 0:N], ALU.mult, ALU.mult)
        return
    cosv, sinv = rope
    p.stt(C["qn"][0:64, 0:N], ps_in, gain, C["r"][0:64, 0:N], ALU.mult, ALU.mult)
    p.mm(C["psr"][0:64, 0:N], C["rotT"][:, :], C["qn"][0:64, 0:N])
    p.tt(C["t1"][0:64, 0:N], C["qn"][0:64, 0:N], cosv, ALU.mult, eng="pool")
    p.tt(C["t2"][0:64, 0:N], C["psr"][0:64, 0:N], sinv, ALU.mult)
    p.tt(out, C["t1"][0:64, 0:N], C["t2"][0:64, 0:N], ALU.add, eng="pool")


def build_k1():
    p = Prog()
    IN, OUT = "ExternalInput", "ExternalOutput"
    xT = p.dram("xT", [D, NL], F32, IN)
    gn = p.dram("gn", [128, 8], F32, IN)
    w = p.dram("w", [D, 2320], F32, IN)
    qkg = p.dram("qkg", [64, 4], F32, IN)
    cosT = p.dram("cosT", [64, NL], F32, IN)
    sinT = p.dram("sinT", [64, NL], F32, IN)
    ebt = p.dram("ebt", [64, 8 * 256], F32, IN)
    ebe = p.dram("ebe", [64, 7 * 12 * 256], F32, IN)
    cmask = p.dram("cmask", [64, 256], F32, IN)
    sinkb = p.dram("sinkb", [64, 8], F32, IN)
    gcw = p.dram("gcw", [64, 36], F32, IN)
    alog = p.dram("alog", [8, 1], F32, IN)
    dtb = p.dram("dtb", [8, 1], F32, IN)
    rotT_d = p.dram("rotT", [64, 64], F32, IN)
    mprev = p.dram("mprev", [128, 512], F32, IN)
    mnext = p.dram("mnext", [128, 512], F32, IN)
    mprev0 = p.dram("mprev0", [128, 512], F32, IN)
    mnextL = p.dram("mnextL", [128, 512], F32, IN)
    ynaT = p.dram("ynaT", [4, 64, TOK], F32, OUT)
    yswaT = p.dram("yswaT", [8, 64, TOK], F32, OUT)
    gqkv = p.dram("gqkv", [12, 64, TOK], F32, OUT)
    betaT = p.dram("betaT", [8, TOK], F32, OUT)
    gT = p.dram("gT", [8, TOK], F32, OUT)

    ps = [p.psum([128, 512], F32, "ps%d" % i) for i in range(8)]
    hT = p.sbuf([128, 8, NL], BF16, "hT")
    gn_sb = p.sbuf([128, 8], F32, "gn_sb")
    qkg_sb = p.sbuf([64, 4], F32, "qkg_sb")
    ones128 = p.sbuf([128, 128], BF16, "ones128")
    ones64 = p.sbuf([64, 64], BF16, "ones64")
    one64 = p.sbuf([128, 64], BF16, "one64")
    rot_f = p.sbuf([64, 64], F32, "rot_f")
    rotT = p.sbuf([64, 64], BF16, "rotTb")
    p.memset(ones128[:, :], 1.0 / D)
    p.memset(ones64[:, :], 1.0 / 64)
    p.memset(one64[:, :], 1.0)
    p.dma("sp", gn_sb[:, :], gn[:, :])
    p.dma("sp", qkg_sb[:, :], qkg[:, :])
    p.dma("sp", rot_f[:, :], rotT_d[:, :])
    p.copy(rotT[:, :], rot_f[:, :])

    with p.scope():
        xb = [p.sbuf([128, 8, 512], F32, "xb%d" % i) for i in range(2)]
        sq = p.sbuf([128, 8, 512], BF16, "sq1")
        rstd = p.sbuf([128, 512], F32, "rstd1")
        for b in range(NL // 512):
            X = xb[b % 2]
            c0 = 512 * b
            p.dma("sp", X[:, :, :], xT.v(xT.t[:, c0:c0 + 512].rearrange("(k p) t -> p k t", p=128)))
            for k in range(8):
                p.act(sq[:, k, :], X[:, k, :], AF.Square)
            for k in range(8):
                p.mm(ps[0][:, :], ones128[:, :], sq[:, k, :], start=(k == 0), stop=(k == 7))
            p.act(rstd[:, :], ps[0][:, :], AF.Sqrt, bias=EPS)
            p.recip(rstd[:, :], rstd[:, :])
            for k in range(8):
                p.stt(hT[:, k, c0:c0 + 512], X[:, k, :], gn_sb[:, k:k + 1], rstd[:, :], ALU.mult, ALU.mult)

    with p.scope():
        NB = 410
        wg = [p.sbuf([128, 784], BF16, "wg%d" % k) for k in range(8)]
        for k in range(8):
            p.dma("pool", wg[k][:, :], w[k * 128:(k + 1) * 128, 1536:2320])
        gcw_sb = p.sbuf([64, 36], F32, "gcw_sb")
        p.dma("sp", gcw_sb[:, :], gcw[:, :])
        prev2 = p.sbuf([64, 24], F32, "prev2")
        p.memset(prev2[:, :], 0.0)
        ub = [p.sbuf([64, NB + 2], F32, "ub%d" % i) for i in range(3)]
        Y = [p.sbuf([64, NB], F32, "Y%d" % i) for i in range(3)]
        sqg = p.sbuf([64, NB], BF16, "sqg")
        rg = p.sbuf([64, NB], F32, "rg")
        it = 0
        for b in range(5):
            c0 = HALO - 1 + NB * b
            i0 = 2 if b == 0 else 0
            for t in range(12):
                pu = ps[it % 2]
                U = ub[it % 3]
                y = Y[it % 3]
                it += 1
                for k in range(8):
                    p.mm(pu[0:64, 0:NB], wg[k][:, t * 64:(t + 1) * 64], hT[:, k, c0:c0 + NB], start=(k == 0), stop=(k == 7))
                p.copy(U[:, 0:2], prev2[:, 2 * t:2 * t + 2], eng="pool")
                p.act(U[:, 2:NB + 2], pu[0:64, 0:NB], AF.Copy)
                p.copy(prev2[:, 2 * t:2 * t + 2], U[:, NB:NB + 2], eng="pool")
                p.ts(y[:, :], U[:, 0:NB], gcw_sb[:, 3 * t:3 * t + 1], ALU.mult)
                p.stt(y[:, :], U[:, 1:NB + 1], gcw_sb[:, 3 * t + 1:3 * t + 2], y[:, :], ALU.mult, ALU.add)
                p.stt(y[:, :], U[:, 2:NB + 2], gcw_sb[:, 3 * t + 2:3 * t + 3], y[:, :], ALU.mult, ALU.add)
                p.act(y[:, :], y[:, :], AF.Silu)
                if t < 8:
                    p.act(sqg[:, :], y[:, :], AF.Square)
                    p.mm(ps[2][0:64, 0:NB], one64[0:64, :], sqg[:, :])
                    p.act(rg[:, :], ps[2][0:64, 0:NB], AF.Sqrt, bias=EPS)
                    p.recip(rg[:, :], rg[:, :])
                    if t < 4:
                        p.stt(y[:, :], y[:, :], 0.125, rg[:, :], ALU.mult, ALU.mult)
                    else:
                        p.tt(y[:, :], y[:, :], rg[:, :], ALU.mult)
                oc = NB * b - 2 + i0
                p.dma("sp", gqkv.v(gqkv.t[t, :, oc:oc + NB - i0]), y[:, i0:NB])
        alog_sb = p.sbuf([8, 1], F32, "alog_sb")
        dtb_sb = p.sbuf([8, 1], F32, "dtb_sb")
        negA = p.sbuf([8, 1], F32, "negA")
        p.dma("sp", alog_sb[:, :], alog[:, :])
        p.dma("sp", dtb_sb[:, :], dtb[:, :])
        p.act(negA[:, :], alog_sb[:, :], AF.Exp)
        p.ts(negA[:, :], negA[:, :], -1.0, ALU.mult)
        bo = [p.sbuf([8, 512], F32, "bo%d" % i) for i in range(2)]
        go = [p.sbuf([8, 512], F32, "go%d" % i) for i in range(2)]
        for b in range(4):
            c0 = HALO + 512 * b
            for k in range(8):
                p.mm(ps[3][0:8, :], wg[k][:, 768:776], hT[:, k, c0:c0 + 512], start=(k == 0), stop=(k == 7))
            for k in range(8):
                p.mm(ps[4][0:8, :], wg[k][:, 776:784], hT[:, k, c0:c0 + 512], start=(k == 0), stop=(k == 7))
            B_, G_ = bo[b % 2], go[b % 2]
            p.act(B_[:, :], ps[3][0:8, :], AF.Sigmoid)
            p.act(G_[:, :], ps[4][0:8, :], AF.Exp, bias=dtb_sb[:, 0:1])
            p.act(G_[:, :], G_[:, :], AF.Ln, bias=1.0)
            p.ts(G_[:, :], G_[:, :], negA[:, 0:1], ALU.mult)
            p.dma("sp", betaT[:, 512 * b:512 * b + 512], B_[:, :])
            p.dma("sp", gT[:, 512 * b:512 * b + 512], G_[:, :])

    def mk_scratch():
        C = dict(ones64=ones64, rotT=rotT)
        C["sq"] = p.sbuf([64, 512], BF16, "qsq")
        C["r"] = p.sbuf([64, 512], F32, "qr")
        C["qn"] = p.sbuf([64, 512], BF16, "qqn")
        C["t1"] = p.sbuf([64, 512], F32, "qt1")
        C["t2"] = p.sbuf([64, 512], F32, "qt2")
        C["psm"] = ps[6]
        C["psr"] = ps[7]
        return C

    with p.scope():
        C = mk_scratch()
        wn = [p.sbuf([128, 768], BF16, "wn%d" % k) for k in range(8)]
        for k in range(8):
            p.dma("pool", wn[k][:, :], w[k * 128:(k + 1) * 128, 0:768])
        qaT = [p.sbuf([64, TOK], BF16, "qaT%d" % h) for h in range(4)]
        kaT = [p.sbuf([64, NL], BF16, "kaT%d" % h) for h in range(4)]
        va = p.sbuf([64, 40, 256], BF16, "va")
        for h in range(4):
            for nb in range(4):
                c0 = HALO + 512 * nb
                pq = ps[nb % 2]
                for k in range(8):
                    p.mm(pq[0:64, :], wn[k][:, h * 64:(h + 1) * 64], hT[:, k, c0:c0 + 512], start=(k == 0), stop=(k == 7))
                qknorm(p, C, pq[0:64, :], 512, qkg_sb[:, 0:1], qaT[h][:, 512 * nb:512 * nb + 512])
            for nb in range(5):
                c0 = 512 * nb
                pq = ps[nb % 2]
                for k in range(8):
                    p.mm(pq[0:64, :], wn[k][:, 256 + h * 64:256 + (h + 1) * 64], hT[:, k, c0:c0 + 512], start=(k == 0), stop=(k == 7))
                qknorm(p, C, pq[0:64, :], 512, qkg_sb[:, 1:2], kaT[h][:, c0:c0 + 512])
        for rr in range(40):
            pv = ps[2 + rr % 2]
            for k in range(8):
                p.mm(pv[0:64, 0:256], hT[:, k, 64 * rr:64 * rr + 64], wn[k][:, 512:768], start=(k == 0), stop=(k == 7))
            p.copy(va[:, rr, :], pv[0:64, 0:256], eng=("act" if rr % 2 else "dve"))
        cm = p.sbuf([64, 256], F32, "cm")
        p.dma("sp", cm[:, :], cmask[:, :])
        tmpt = p.sbuf([64, 12 * 256], F32, "tmpt")
        EBs = p.sbuf([64, 8, 256], BF16, "EBs")
        EBe = p.sbuf([64, 7 * 12, 256], BF16, "EBe")
        p.dma("sp", tmpt[:, 0:2048], ebt[:, :])
        p.act(tmpt[:, 0:2048], tmpt[:, 0:2048], AF.Exp)
        for j in range(8):
            p.tt(EBs[:, j, :], tmpt[:, 256 * j:256 * j + 256], cm[:, :], ALU.mult)
        for e_ in range(7):
            p.dma("sp", tmpt[:, :], ebe[:, e_ * 3072:(e_ + 1) * 3072])
            p.act(tmpt[:, :], tmpt[:, :], AF.Exp)
            for s_ in range(12):
                p.tt(EBe[:, e_ * 12 + s_, :], tmpt[:, 256 * s_:256 * s_ + 256], cm[:, :], ALU.mult)
        E1 = [p.sbuf([64, 256], F32, "E1_%d" % i) for i in range(3)]
        E2 = [p.sbuf([64, 256], BF16, "E2_%d" % i) for i in range(3)]
        rden = p.sbuf([64, 256], F32, "rden")
        yst = [p.sbuf([64, 256], F32, "yst%d" % i) for i in range(3)]
        ei = 0
        for qr in range(32):
            if qr < 4:
                slots = [(kr, EBe[:, qr * 12 + (kr - qr), :]) for kr in range(qr, 12)]
            elif qr >= 29:
                slots = [(kr, EBe[:, (qr - 29 + 4) * 12 + (kr - 28), :]) for kr in range(28, qr + 8)]
            else:
                slots = [(qr + j, EBs[:, j, :]) for j in range(8)]
            pacc = ps[4 + qr % 2]
            q0 = 64 * qr
            for si, (kr, tab) in enumerate(slots):
                pst = ps[si % 2]
                for h in range(4):
                    p.mm(pst[0:64, h * 64:(h + 1) * 64], kaT[h][:, 64 * kr:64 * kr + 64], qaT[h][:, q0:q0 + 64])
                e1, e2 = E1[ei % 3], E2[ei % 3]
                ei += 1
                p.act(e1[:, :], pst[0:64, 0:256], AF.Exp, scale=0.125)
                p.tt(e2[:, :], e1[:, :], tab, ALU.mult, eng=("pool" if ei % 2 else "dve"))
                for h in range(4):
                    p.mm(pacc[0:64, h * 64:(h + 1) * 64], va[:, kr, h * 64:(h + 1) * 64], e2[:, h * 64:(h + 1) * 64],
                         start=(si == 0 and h == 0), stop=(si == len(slots) - 1))
                p.mm(pacc[0:64, 256:512], one64[0:64, :], e2[:, :], start=False, stop=(si == len(slots) - 1))
            ys = yst[qr % 3]
            p.recip(rden[:, :], pacc[0:64, 256:512])
            p.tt(ys[:, :], pacc[0:64, 0:256], rden[:, :], ALU.mult)
            p.dma("sp", ynaT.v(ynaT.t[:, :, q0:q0 + 64].rearrange("h d t -> d h t")),
                  ys.v(ys.t[:, :].rearrange("d (h t) -> d h t", h=4)))

    with p.scope():
        C = mk_scratch()
        wsw = [p.sbuf([128, 768], BF16, "wsw%d" % k) for k in range(8)]
        for k in range(8):
            p.dma("pool", wsw[k][:, :], w[k * 128:(k + 1) * 128, 768:1536])
        cs = p.sbuf([64, NL], F32, "cs")
        sn = p.sbuf([64, NL], F32, "sn")
        p.dma("sp", cs[:, :], cosT[:, :])
        p.dma("sp", sn[:, :], sinT[:, :])
        qsT = [p.sbuf([64, TOK], BF16, "qsT%d" % h) for h in range(8)]
        ksT = [p.sbuf([64, NL], BF16, "ksT%d" % g) for g in range(2)]
        vs = p.sbuf([128, 20, 128], BF16, "vs")
        for h in range(8):
            for nb in range(4):
                c0 = HALO + 512 * nb
                pq = ps[nb % 2]
                for k in range(8):
                    p.mm(pq[0:64, :], wsw[k][:, h * 64:(h + 1) * 64], hT[:, k, c0:c0 + 512], start=(k == 0), stop=(k == 7))
                qknorm(p, C, pq[0:64, :], 512, qkg_sb[:, 2:3], qsT[h][:, 512 * nb:512 * nb + 512],
                       rope=(cs[:, c0:c0 + 512], sn[:, c0:c0 + 512]))
        for g in range(2):
            for nb in range(5):
                c0 = 512 * nb
                pq = ps[nb % 2]
                for k in range(8):
                    p.mm(pq[0:64, :], wsw[k][:, 512 + g * 64:512 + (g + 1) * 64], hT[:, k, c0:c0 + 512], start=(k == 0), stop=(k == 7))
                qknorm(p, C, pq[0:64, :], 512, qkg_sb[:, 3:4], ksT[g][:, c0:c0 + 512],
                       rope=(cs[:, c0:c0 + 512], sn[:, c0:c0 + 512]))
        for tb in range(20):
            pv = ps[2 + tb % 2]
            for k in range(8):
                p.mm(pv[:, 0:128], hT[:, k, 128 * tb:128 * tb + 128], wsw[k][:, 640:768], start=(k == 0), stop=(k == 7))
            p.copy(vs[:, tb, :], pv[:, 0:128], eng=("act" if tb % 2 else "dve"))
        mtmp = p.sbuf([128, 512], F32, "mtmp")
        masks = {}
        for nm, src in (("prev", mprev), ("next", mnext), ("prev0", mprev0), ("nextL", mnextL)):
            m = p.sbuf([128, 512], BF16, "m_" + nm)
            p.dma("sp", mtmp[:, :], src[:, :])
            p.copy(m[:, :], mtmp[:, :])
            masks[nm] = m
        es = p.sbuf([64, 8], F32, "es")
        p.dma("sp", es[:, :], sinkb[:, :])
        p.act(es[:, :], es[:, :], AF.Exp)
        one_f = p.sbuf([64, 128], F32, "one_f")
        p.memset(one_f[:, :], 1.0)
        EsT = p.sbuf([64, 8, 128], F32, "EsT")
        for h in range(8):
            p.ts(EsT[:, h, :], one_f[:, :], es[:, h:h + 1], ALU.mult)
        E1 = [p.sbuf([128, 512], BF16, "S1_%d" % i) for i in range(3)]
        rden = p.sbuf([64, 512], F32, "srden")
        yst = [p.sbuf([64, 512], F32, "syst%d" % i) for i in range(3)]
        ei = 0
        for n in range(16):
            q0 = 128 * n
            for g in range(2):
                pnum = ps[2 + (2 * n + g) % 2]
                pden = ps[4 + (2 * n + g) % 2]
                for bi, kb in enumerate((-1, 0, 1)):
                    k0 = HALO + 128 * (n + kb)
                    pst = ps[bi % 2]
                    for hh in range(4):
                        p.mm(pst[:, hh * 128:(hh + 1) * 128], ksT[g][:, k0:k0 + 128], qsT[4 * g + hh][:, q0:q0 + 128])
                    e1 = E1[ei % 3]
                    ei += 1
                    p.act(e1[:, :], pst[:, :], AF.Exp, scale=0.125)
                    if kb == -1:
                        p.tt(e1[:, :], e1[:, :], masks["prev0" if n == 0 else "prev"][:, :], ALU.mult, eng="pool")
                    elif kb == 1:
                        p.tt(e1[:, :], e1[:, :], masks["nextL" if n == 15 else "next"][:, :], ALU.mult, eng="pool")
                    tb = (k0 // 128)
                    for hh in range(4):
                        p.mm(pnum[0:64, hh * 128:(hh + 1) * 128], vs[:, tb, g * 64:(g + 1) * 64], e1[:, hh * 128:(hh + 1) * 128],
                             start=(bi == 0 and hh == 0), stop=(bi == 2))
                    p.mm(pden[0:64, :], one64[:, :], e1[:, :], start=(bi == 0), stop=(bi == 2))
                ys = yst[(2 * n + g) % 3]
                p.tt(rden[:, :], pden[0:64, :], EsT.v(EsT.t[:, 4 * g:4 * g + 4, :].rearrange("d h t -> d (h t)")), ALU.add)
                p.recip(rden[:, :], rden[:, :])
                p.tt(ys[:, :], pnum[0:64, :], rden[:, :], ALU.mult)
                p.dma("sp", yswaT.v(yswaT.t[4 * g:4 * g + 4, :, q0:q0 + 128].rearrange("h d t -> d h t")),
                      ys.v(ys.t[:, :].rearrange("d (h t) -> d h t", h=4)))
    return p.finish()


L = 16384
NCP = 128
NGP = 32


def build_k2(NGP=NGP, stage=99):
    p = Prog()
    IN, OUT = "ExternalInput", "ExternalOutput"
    qT = p.dram("qT", [64, L], F32, IN)
    kT = p.dram("kT", [64, L], F32, IN)
    ktm = p.dram("ktm", [L, 64], F32, IN)
    vtm = p.dram("vtm", [L, 64], F32, IN)
    brow = p.dram("brow", [1, L], F32, IN)
    gcp = p.dram("gcp", [128, NCP], F32, IN)
    bcp = p.dram("bcp", [128, NCP], F32, IN)
    ident_d = p.dram("ident", [128, 128], F32, IN)
    trisel_d = p.dram("trisel", [128, 130], F32, IN)
    bones_d = p.dram("bones", [128, 128], F32, IN)
    MB_d = p.dram("MB", [128, 128], F32, IN)
    MBT_d = p.dram("MBT", [128, 128], F32, IN)
    SM_d = p.dram("SM", [128, 128], F32, IN)
    SMT_d = p.dram("SMT", [128, 128], F32, IN)
    o = p.dram("o", [L, 64], F32, OUT)

    def const(src, shape, nm):
        b = p.sbuf(shape, F32, nm)
        p.dma("sp", b[:, :], src[:, :])
        return b
    ident = const(ident_d, [128, 128], "ident")
    trisel = const(trisel_d, [128, 130], "trisel")
    bones = const(bones_d, [128, 128], "bones")
    MB = const(MB_d, [128, 128], "MB")
    MBT = const(MBT_d, [128, 128], "MBT")
    SM = const(SM_d, [128, 128], "SM")
    SMT = const(SMT_d, [128, 128], "SMT")
    g_all = const(gcp, [128, NCP], "g_all")
    b_all = const(bcp, [128, NCP], "b_all")
    ones_f = p.sbuf([128, 128], F32, "ones_f")
    p.memset(ones_f[:, :], 1.0)

    banks = [p.psum([128, 512], F32, "bank%d" % i) for i in range(8)]

    def sub(bi, a, b, nm, rows=128):
        return banks[bi].sub(banks[bi].t[0:rows, a:b], nm)
    psGB = sub(0, 0, 130, "psGB")
    psL = sub(1, 0, 128, "psL")
    psU = sub(1, 128, 256, "psU")
    psA = sub(2, 0, 128, "psA")
    psB = sub(2, 128, 256, "psB")
    psX = sub(3, 0, 128, "psX")
    psu = sub(4, 0, 64, "psu")
    psw = sub(4, 64, 192, "psw", 64)
    psq = sub(4, 192, 320, "psq")
    pst = sub(5, 0, 64, "pst")
    pso = sub(6, 0, 64, "pso", 64)
    psS = sub(7, 0, 64, "psS", 64)
    psbb = sub(3, 0, 512, "psbb", 64)
    ps_t = sub(0, 0, 128, "ps_t")
    ps_t2 = sub(1, 0, 128, "ps_t2")
    gc_all = p.sbuf([128, NCP], F32, "gc_all")
    ngc_all = p.sbuf([128, NCP], F32, "ngc_all")
    coefw = p.sbuf([128, NCP], F32, "coefw")
    coeft = p.sbuf([128, NCP], F32, "coeft")
    p.mm(ps_t[:, :], trisel[:, 0:128], g_all[:, :])
    p.copy(gc_all[:, :], ps_t[:, :])
    p.ts(ngc_all[:, :], gc_all[:, :], -1.0, ALU.mult)
    p.act(coefw[:, :], gc_all[:, :], AF.Exp)
    p.tt(coefw[:, :], coefw[:, :], b_all[:, :], ALU.mult)
    p.mm(ps_t2[:, :], bones[:, :], g_all[:, :])
    p.tt(coeft[:, :], ps_t2[:, :], gc_all[:, :], ALU.subtract)
    p.act(coeft[:, :], coeft[:, :], AF.Exp)

    if stage == 1:
        return p.finish()
    qg = [p.sbuf([64, 512], F32, "qg%d" % i) for i in range(2)]
    kg = [p.sbuf([64, 512], F32, "kg%d" % i) for i in range(2)]
    ktg = [p.sbuf([128, 4, 64], F32, "ktg%d" % i) for i in range(2)]
    vtg = [p.sbuf([128, 4, 64], F32, "vtg%d" % i) for i in range(2)]
    brg = [p.sbuf([1, 512], F32, "brg%d" % i) for i in range(2)]
    kbT = p.sbuf([64, 512], BF16, "kbT")
    kTb = p.sbuf([64, 512], BF16, "kTb")
    qTb = p.sbuf([64, 512], BF16, "qTb")
    gbc = p.sbuf([128, 128], F32, "gbc")
    tmp = p.sbuf([128, 128], F32, "tmp")
    tmp2 = p.sbuf([128, 128], F32, "tmp2")
    decay = p.sbuf([128, 128], F32, "decay")
    decayT = p.sbuf([128, 128], F32, "decayT")
    decay_s = p.sbuf([128, 128], F32, "decay_s")
    decayT_s = p.sbuf([128, 128], F32, "decayT_s")
    eg = p.sbuf([64, 130], F32, "eg")
    Ab = [p.sbuf([128, 128], F32, "Ab%d" % i) for i in range(2)]
    Bb = [p.sbuf([128, 128], F32, "Bb%d" % i) for i in range(2)]
    Xb = [p.sbuf([128, 128], F32, "Xb%d" % i) for i in range(2)]
    rhs_u = p.sbuf([128, 64], F32, "rhs_u")
    rhs_w = p.sbuf([128, 64], F32, "rhs_w")
    ktail = p.sbuf([128, 64], F32, "ktail")
    u_sb = p.sbuf([128, 64], F32, "u_sb")
    W3 = p.sbuf([64, 192], F32, "W3")
    QKT = p.sbuf([128, 128], F32, "QKT")
    qdT = p.sbuf([64, 128], F32, "qdT")
    vnew = p.sbuf([128, 64], F32, "vnew")
    Sb = [p.sbuf([64, 64], F32, "S%d" % i) for i in range(2)]
    ost = [p.sbuf([64, 8, 64], F32, "ost%d" % i) for i in range(2)]
    p.memset(W3[:, :], 0.0)
    p.memset(Sb[0][:, :], 0.0)
    si = 0

    for gp in range(NGP):
        T0 = 512 * gp
        Q, K, KT, VT, BR = qg[gp % 2], kg[gp % 2], ktg[gp % 2], vtg[gp % 2], brg[gp % 2]
        p.dma("sp", Q[:, :], qT[:, T0:T0 + 512])
        p.dma("sp", K[:, :], kT[:, T0:T0 + 512])
        p.dma("sp", KT[:, :, :], ktm.v(ktm.t[T0:T0 + 512, :].rearrange("(c p) d -> p c d", p=128)))
        p.dma("sp", VT[:, :, :], vtm.v(vtm.t[T0:T0 + 512, :].rearrange("(c p) d -> p c d", p=128)))
        p.dma("sp", BR[:, :], brow[:, T0:T0 + 512])
        p.mm(psbb[:, :], ones_f[0:1, 0:64], BR[:, :])
        p.tt(kbT[:, :], K[:, :], psbb[:, :], ALU.mult)
        p.copy(kTb[:, :], K[:, :], eng="pool")
        p.copy(qTb[:, :], Q[:, :], eng="pool")
        if stage == 2:
            return p.finish()
        OS = ost[gp % 2]
        for ci in range(4):
            cp = 4 * gp + ci
            t0 = 128 * ci
            gcc = gc_all[:, cp:cp + 1]
            p.ts(gbc[:, :], ones_f[:, :], g_all[:, cp:cp + 1], ALU.mult)
            p.mm(psGB[:, :], gbc[:, :], trisel[:, :])
            p.tt(tmp[:, :], MB[:, :], psGB[:, 0:128], ALU.subtract)
            p.act(decay[:, :], tmp[:, :], AF.Exp, bias=gcc)
            p.tt(tmp2[:, :], MBT[:, :], psGB[:, 0:128], ALU.add)
            p.act(decayT[:, :], tmp2[:, :], AF.Exp, bias=ngc_all[:, cp:cp + 1])
            p.tt(decay_s[:, :], decay[:, :], SM[:, :], ALU.mult, eng="pool")
            p.tt(decayT_s[:, :], decayT[:, :], SMT[:, :], ALU.mult, eng="pool")
            p.act(eg[:, :], psGB[0:64, :], AF.Exp)
            if stage == 3:
                return p.finish()
            p.mm(psL[:, :], kbT[:, t0:t0 + 128], kTb[:, t0:t0 + 128])
            p.mm(psU[:, :], kTb[:, t0:t0 + 128], kbT[:, t0:t0 + 128])
            A, B, X = Ab[0], Bb[0], Xb[0]
            p.tt(B[:, :], psL[:, :], decay_s[:, :], ALU.mult)
            p.tt(A[:, :], psU[:, :], decayT_s[:, :], ALU.mult)
            p.tt(X[:, :], ident[:, :], A[:, :], ALU.subtract)
            if stage == 31:
                return p.finish()
            for k in range(1, 6):
                A2, B2, X2 = Ab[k % 2], Bb[k % 2], Xb[k % 2]
                p.mm(psA[:, :], B[:, :], A[:, :])
                p.mm(psB[:, :], A[:, :], B[:, :])
                if stage == 32:
                    return p.finish()
                p.copy(A2[:, :], psA[:, :], eng="act")
                p.copy(B2[:, :], psB[:, :])
                if stage == 33:
                    return p.finish()
                p.mm(psX[:, :], B2[:, :], X[:, :])
                p.tt(X2[:, :], psX[:, :], X[:, :], ALU.add)
                A, B, X = A2, B2, X2
                if stage == 34:
                    return p.finish()
            TT = X
            if stage == 4:
                return p.finish()
            p.ts(rhs_u[:, :], VT[:, ci, :], b_all[:, cp:cp + 1], ALU.mult, eng="pool")
            p.ts(rhs_w[:, :], KT[:, ci, :], coefw[:, cp:cp + 1], ALU.mult, eng="pool")
            p.ts(ktail[:, :], KT[:, ci, :], coeft[:, cp:cp + 1], ALU.mult, eng="pool")
            p.mm(psu[:, :], TT[:, :], rhs_u[:, :])
            p.copy(u_sb[:, :], psu[:, :], eng="act")
            p.mm(psw[:, :], rhs_w[:, :], TT[:, :])
            p.copy(W3[:, 0:64], psw[:, 0:64], eng="act")
            p.copy(W3[:, 128:192], psw[:, 64:128], eng="act")
            p.mm(psq[:, :], kTb[:, t0:t0 + 128], qTb[:, t0:t0 + 128])
            p.tt(QKT[:, :], psq[:, :], decayT[:, :], ALU.mult)
            p.tt(qdT[:, :], Q[:, t0:t0 + 128], eg[:, 0:128], ALU.mult, eng="pool")
            if stage == 5:
                return p.finish()
            for half in range(2):
                lo = 64 * half
                S, S2 = Sb[si % 2], Sb[(si + 1) % 2]
                si += 1
                p.mm(pst[:, :], W3[:, lo:lo + 128], S[:, :])
                p.tt(vnew[lo:lo + 64, :], u_sb[lo:lo + 64, :], pst[lo:lo + 64, :], ALU.subtract)
                p.mm(pso[:, :], qdT[:, lo:lo + 64], S[:, :], start=True, stop=False)
                p.mm(pso[:, :], QKT[lo:lo + 64, lo:lo + 64], vnew[lo:lo + 64, :], start=False, stop=True)
                p.copy(OS[:, 2 * ci + half, :], pso[:, :], eng="act")
                p.mm(psS[:, :], ktail[lo:lo + 64, :], vnew[lo:lo + 64, :])
                p.stt(S2[:, :], S[:, :], eg[:, 128 + half:129 + half], psS[:, :], ALU.mult, ALU.add)
        p.dma("sp", o.v(o.t[T0:T0 + 512, :].rearrange("(c p) d -> p c d", p=64)), OS[:, :, :])
    return p.finish()


TOK = 2048
D = 1024
EPS = 1e-6


def build_k3():
    p = Prog()
    IN, OUT = "ExternalInput", "ExternalOutput"
    xT = p.dram("xT", [D, TOK], F32, IN)
    gn = p.dram("gn", [128, 8], F32, IN)
    wzg = p.dram("wzg", [D, 3328], F32, IN)
    gdng = p.dram("gdng", [64, 1], F32, IN)
    wb = p.dram("wb", [64, 16 * D], F32, IN)
    wout = p.dram("wout", [D, D], F32, IN)
    ynaT = p.dram("ynaT", [4, 64, TOK], F32, IN)
    yswaT = p.dram("yswaT", [8, 64, TOK], F32, IN)
    ofT = p.dram("ofT", [4, 64, TOK], F32, IN)
    obT = p.dram("obT", [4, 64, TOK], F32, IN)
    x1T = p.dram("x1T", [D, TOK], F32, OUT)

    ps = [p.psum([128, 512], F32, "ps%d" % i) for i in range(8)]
    wzg_sb = [p.sbuf([128, 3328], BF16, "wzg%d" % k) for k in range(8)]
    wb_sb = p.sbuf([64, 16, D], BF16, "wb_sb")
    wout_sb = [p.sbuf([128, D], BF16, "wout%d" % k) for k in range(8)]
    gn_sb = p.sbuf([128, 8], F32, "gn_sb")
    gdng_sb = p.sbuf([64, 1], F32, "gdng_sb")
    ones128 = p.sbuf([128, 128], BF16, "ones128")
    ones64 = p.sbuf([64, 64], BF16, "ones64")
    p.memset(ones128[:, :], 1.0 / D)
    p.memset(ones64[:, :], 1.0 / 64)
    p.dma("sp", gn_sb[:, :], gn[:, :])
    p.dma("sp", gdng_sb[:, :], gdng[:, :])
    for k in range(8):
        p.dma("pool", wzg_sb[k][:, :], wzg[k * 128:(k + 1) * 128, :], max_dma_last_dim=8192)
    for j in range(16):
        p.dma("pool", wb_sb[:, j, :], wb[:, j * D:(j + 1) * D])
    for k in range(8):
        p.dma("pool", wout_sb[k][:, :], wout[k * 128:(k + 1) * 128, :])

    X = p.sbuf([128, 8, 512], F32, "X")
    sq = p.sbuf([128, 8, 512], BF16, "sq")
    rstd = p.sbuf([128, 512], F32, "rstd")
    hT = p.sbuf([128, 8, 512], BF16, "hT")
    yna = p.sbuf([64, 4, 512], BF16, "yna")
    yswa = p.sbuf([64, 8, 512], BF16, "yswa")
    of = p.sbuf([64, 4, 512], F32, "of")
    ob = p.sbuf([64, 4, 512], F32, "ob")
    yg = p.sbuf([64, 4, 512], BF16, "yg")
    sz = p.sbuf([64, 512], F32, "sz")
    osum = p.sbuf([64, 512], F32, "osum")
    sq2 = p.sbuf([64, 512], BF16, "sq2")
    r2 = p.sbuf([64, 512], F32, "r2")
    sg = [p.sbuf([128, 512], F32, "sg%d" % i) for i in range(2)]
    macc = p.sbuf([128, 512], F32, "macc")
    mtmp = p.sbuf([128, 512], F32, "mtmp")
    merged = p.sbuf([128, 8, 512], BF16, "merged")
    xo = [p.sbuf([128, 512], F32, "xo%d" % i) for i in range(2)]

    for b in range(TOK // 512):
        c0 = 512 * b
        p.dma("sp", X[:, :, :], xT.v(xT.t[:, c0:c0 + 512].rearrange("(k p) t -> p k t", p=128)))
        p.dma("pool", yna[:, :, :], ynaT.v(ynaT.t[:, :, c0:c0 + 512].rearrange("h d t -> d h t")))
        p.dma("pool", yswa[:, :, :], yswaT.v(yswaT.t[:, :, c0:c0 + 512].rearrange("h d t -> d h t")))
        p.dma("sp", of[:, :, :], ofT.v(ofT.t[:, :, c0:c0 + 512].rearrange("h d t -> d h t")))
        p.dma("sp", ob[:, :, :], obT.v(obT.t[:, :, c0:c0 + 512].rearrange("h d t -> d h t")))
        for k in range(8):
            p.act(sq[:, k, :], X[:, k, :], AF.Square)
        for k in range(8):
            p.mm(ps[0][:, :], ones128[:, :], sq[:, k, :], start=(k == 0), stop=(k == 7))
        p.act(rstd[:, :], ps[0][:, :], AF.Sqrt, bias=EPS)
        p.recip(rstd[:, :], rstd[:, :])
        for k in range(8):
            p.stt(hT[:, k, :], X[:, k, :], gn_sb[:, k:k + 1], rstd[:, :], ALU.mult, ALU.mult)
        for h in range(4):
            pz = ps[1]
            for k in range(8):
                p.mm(pz[0:64, :], wzg_sb[k][:, h * 64:(h + 1) * 64], hT[:, k, :], start=(k == 0), stop=(k == 7))
            p.act(sz[:, :], pz[0:64, :], AF.Silu)
            p.tt(osum[:, :], of[:, h, :], ob[:, h, :], ALU.add, eng="pool")
            p.act(sq2[:, :], osum[:, :], AF.Square)
            p.mm(ps[2][0:64, :], ones64[:, :], sq2[:, :])
            p.act(r2[:, :], ps[2][0:64, :], AF.Sqrt, bias=EPS)
            p.recip(r2[:, :], r2[:, :])
            p.stt(osum[:, :], osum[:, :], gdng_sb[:, 0:1], r2[:, :], ALU.mult, ALU.mult)
            p.tt(yg[:, h, :], osum[:, :], sz[:, :], ALU.mult)
        for ot in range(8):
            n0 = ot * 128
            pbr = [ps[3], ps[4], ps[5]]
            for h in range(4):
                p.mm(pbr[0][:, :], wb_sb[:, h, n0:n0 + 128], yna[:, h, :], start=(h == 0), stop=(h == 3))
            for h in range(8):
                p.mm(pbr[1][:, :], wb_sb[:, 4 + h, n0:n0 + 128], yswa[:, h, :], start=(h == 0), stop=(h == 7))
            for h in range(4):
                p.mm(pbr[2][:, :], wb_sb[:, 12 + h, n0:n0 + 128], yg[:, h, :], start=(h == 0), stop=(h == 3))
            for br in range(3):
                pg = ps[6 + br % 2]
                gcol = 256 + br * D + n0
                for k in range(8):
                    p.mm(pg[:, :], wzg_sb[k][:, gcol:gcol + 128], hT[:, k, :], start=(k == 0), stop=(k == 7))
                S = sg[br % 2]
                p.act(S[:, :], pg[:, :], AF.Sigmoid)
                if br == 0:
                    p.tt(macc[:, :], S[:, :], pbr[0][:, :], ALU.mult)
                elif br == 1:
                    p.tt(mtmp[:, :], S[:, :], pbr[1][:, :], ALU.mult)
                    p.tt(macc[:, :], macc[:, :], mtmp[:, :], ALU.add, eng="pool")
                else:
                    p.tt(mtmp[:, :], S[:, :], pbr[2][:, :], ALU.mult)
                    p.tt(merged[:, ot, :], macc[:, :], mtmp[:, :], ALU.add, eng="pool")
        for ot in range(8):
            po = ps[1 + ot % 2]
            for k in range(8):
                p.mm(po[:, :], wout_sb[k][:, ot * 128:(ot + 1) * 128], merged[:, k, :], start=(k == 0), stop=(k == 7))
            O = xo[ot % 2]
            p.tt(O[:, :], po[:, :], X[:, ot, :], ALU.add)
            p.dma("sp", x1T[ot * 128:(ot + 1) * 128, c0:c0 + 512], O[:, :])
    return p.finish()


NB = 410
NBLK = 5
TOK = 2048
D = 1024
DFF = 2816
NCT = 44


def build_k4():
    p = Prog()
    xT = p.dram("xT", [D, TOK + 2], F32, "ExternalInput")
    gn = p.dram("gn", [128, 8], F32, "ExternalInput")
    wup = p.dram("wup", [D, 2 * DFF], F32, "ExternalInput")
    wdn = p.dram("wdn", [DFF, D], F32, "ExternalInput")
    cw = p.dram("cw", [128, NCT * 3], F32, "ExternalInput")
    cb = p.dram("cb", [128, NCT], F32, "ExternalInput")
    oT = p.dram("oT", [D, TOK], F32, "ExternalOutput")

    wup_sb = [p.sbuf([128, 2 * DFF], BF16, "wup%d" % k) for k in range(8)]
    gn_sb = p.sbuf([128, 8], F32, "gn_sb")
    cw_sb = p.sbuf([128, NCT * 3], F32, "cw_sb")
    cb_sb = p.sbuf([128, NCT], F32, "cb_sb")
    ones = p.sbuf([128, 128], BF16, "ones")
    prev2 = p.sbuf([128, NCT * 2], F32, "prev2")
    xb = [p.sbuf([128, 8, NB + 1], F32, "xb%d" % i) for i in range(2)]
    sq = p.sbuf([128, 8, NB], BF16, "sq")
    rstd = p.sbuf([128, NB], F32, "rstd")
    hT = p.sbuf([128, 8, NB], BF16, "hT")
    ub = [p.sbuf([128, NB + 2], F32, "ub%d" % i) for i in range(4)]
    ya = [p.sbuf([128, NB], F32, "ya%d" % i) for i in range(2)]
    yb = [p.sbuf([128, NB], F32, "yb%d" % i) for i in range(2)]
    mT = p.sbuf([128, 22, NB], BF16, "mT")
    wd_sb = [p.sbuf([128, 22, 128], BF16, "wd%d" % i) for i in range(3)]
    ob = [p.sbuf([128, NB], F32, "ob%d" % i) for i in range(2)]
    ps_u = [p.psum([128, 512], F32, "psu%d" % i) for i in range(4)]
    ps_o = [p.psum([128, 512], F32, "pso%d" % i) for i in range(2)]
    ps_s = p.psum([128, 512], F32, "pss")

    p.memset(ones[:, :], 1.0 / D)
    p.memset(prev2[:, :], 0.0)
    p.dma("sp", gn_sb[:, :], gn[:, :])
    p.dma("sp", cw_sb[:, :], cw[:, :])
    p.dma("sp", cb_sb[:, :], cb[:, :])
    for k in range(8):
        p.dma("pool", wup_sb[k][:, :], wup[k * 128:(k + 1) * 128, :], max_dma_last_dim=8192)

    wdi = 0
    for b in range(NBLK):
        c0 = NB * b
        start = max(c0 - 1, 0)
        off = c0 - start
        ncols = c0 + NB - start
        X = xb[b % 2]
        p.dma("sp", X[:, :, 0:ncols], xT.v(xT.t[:, start:c0 + NB].rearrange("(k p) t -> p k t", p=128)))
        for k in range(8):
            p.act(sq[:, k, :], X[:, k, off:off + NB], AF.Square)
        for k in range(8):
            p.mm(ps_s[:, 0:NB], ones[:, :], sq[:, k, :], start=(k == 0), stop=(k == 7))
        p.act(rstd[:, :], ps_s[:, 0:NB], AF.Sqrt, bias=1e-6)
        p.recip(rstd[:, :], rstd[:, :])
        for k in range(8):
            p.stt(hT[:, k, :], X[:, k, off:off + NB], gn_sb[:, k:k + 1], rstd[:, :], ALU.mult, ALU.mult)
        for c in range(22):
            ys = []
            for half in range(2):
                ct = c + 22 * half
                pu = ps_u[(2 * c + half) % 4]
                U = ub[(2 * c + half) % 4]
                for k in range(8):
                    p.mm(pu[:, 0:NB], wup_sb[k][:, ct * 128:(ct + 1) * 128], hT[:, k, :], start=(k == 0), stop=(k == 7))
                p.copy(U[:, 0:2], prev2[:, 2 * ct:2 * ct + 2], eng="pool")
                p.act(U[:, 2:NB + 2], pu[:, 0:NB], AF.Copy)
                p.copy(prev2[:, 2 * ct:2 * ct + 2], U[:, NB:NB + 2], eng="pool")
                Y = (ya if half == 0 else yb)[c % 2]
                p.ts(Y[:, :], U[:, 0:NB], cw_sb[:, 3 * ct:3 * ct + 1], ALU.mult, cb_sb[:, ct:ct + 1], ALU.add)
                p.stt(Y[:, :], U[:, 1:NB + 1], cw_sb[:, 3 * ct + 1:3 * ct + 2], Y[:, :], ALU.mult, ALU.add)
                p.stt(Y[:, :], U[:, 2:NB + 2], cw_sb[:, 3 * ct + 2:3 * ct + 3], Y[:, :], ALU.mult, ALU.add)
                ys.append(Y)
            p.act(ys[0][:, :], ys[0][:, :], AF.Silu)
            p.tt(mT[:, c, :], ys[0][:, :], ys[1][:, :], ALU.mult, eng="pool")
        i0 = 2 if b == 0 else 0
        for o in range(8):
            W = wd_sb[wdi % 3]
            wdi += 1
            p.dma("pool", W[:, :, :], wdn.v(wdn.t[:, o * 128:(o + 1) * 128].rearrange("(c p) n -> p c n", p=128)))
            po = ps_o[o % 2]
            for c in range(22):
                p.mm(po[:, 0:NB], W[:, c, :], mT[:, c, :], start=(c == 0), stop=(c == 21))
            O = ob[o % 2]
            xi = c0 - 1 + i0 - start
            p.tt(O[:, i0:NB], po[:, i0:NB], X[:, o, xi:xi + NB - i0], ALU.add)
            oc = c0 - 2 + i0
            p.dma("sp", oT[o * 128:(o + 1) * 128, oc:oc + NB - i0], O[:, i0:NB])
    return p.finish()


L = 16384
TOK = 2048
HALO = 256
THETA = 10000.0


def f32(a):
    return np.ascontiguousarray(a, dtype=np.float32)


def k1_consts():
    R = np.zeros((64, 64), np.float32)
    for i in range(32):
        R[i, i + 32] = -1.0
        R[i + 32, i] = 1.0
    k = np.arange(128)[:, None]
    q = np.arange(128)[None, :]
    mprev = np.tile((k >= q).astype(np.float32), (1, 4))
    mnext = np.tile((k <= q).astype(np.float32), (1, 4))
    qc = np.arange(64)
    cs = np.clip(qc - 8, 0, 48)
    kc = np.arange(64)
    col_in = (kc[None, :] >= cs[:, None]) & (kc[None, :] < cs[:, None] + 16)
    cmask = np.broadcast_to(col_in.T[:, None, :], (64, 4, 64)).astype(np.float32).reshape(64, 256)
    return dict(rotT=f32(R.T), mprev=f32(mprev), mnext=f32(mnext), cmask=f32(cmask))


def rope_tabs(pos):
    inv = (1.0 / (THETA ** (np.arange(0, 64, 2, dtype=np.float32) / np.float32(64)))).astype(np.float32)
    ang = pos.astype(np.float32)[:, None] * inv[None, :]
    c, s = np.cos(ang).astype(np.float32), np.sin(ang).astype(np.float32)
    return f32(np.concatenate([c, c], 1).T), f32(np.concatenate([s, s], 1).T)


def na_tables(rpb, c):
    kc = np.arange(64)[:, None]
    qc = np.arange(64)[None, :]
    dc = np.clip(kc - qc + 15, 0, 30)
    g = rpb[:, :, dc]
    g = np.transpose(g, (2, 1, 0, 3))
    shared = g[:, 3:11].reshape(64, 8 * 256)
    edge = np.full((64, 7, 12, 4, 64), -30000.0, np.float32)
    for e in range(7):
        qr = e if e < 4 else 29 + (e - 4)
        r = 32 * c + qr
        rs = min(max(r - 4, 0), 248)
        base = qr if e < 4 else 28
        for s in range(12):
            kr = base + s
            Rg = 32 * c - 4 + kr
            if rs <= Rg < rs + 8 and 0 <= Rg < 256:
                edge[:, e, s] = g[:, Rg - r + 7]
    return f32(shared), f32(edge.reshape(64, 7 * 12 * 256))


def prep_k1(inp, l, c, consts, xpad):
    s = c * TOK
    m = dict(consts)
    m["xT"] = f32(xpad[s:s + TOK + 2 * HALO].T)
    m["gn"] = f32(inp["attn_norm"][l].reshape(8, 128).T)
    w_in = inp["w_in"][l]
    m["w"] = f32(np.concatenate([w_in[:, 0:2304], w_in[:, 2560:2576]], 1))
    m["qkg"] = f32(inp["qk_norm"][l].T)
    pos = np.arange(s - HALO, s + TOK + HALO)
    m["cosT"], m["sinT"] = rope_tabs(pos)
    m["ebt"], m["ebe"] = na_tables(inp["na_rpb"][l], c)
    m["sinkb"] = f32(np.broadcast_to(inp["swa_sink"][l][None, :], (64, 8)))
    m["gcw"] = f32(inp["gdn_conv_w"][l].T.reshape(12, 64, 3).transpose(1, 0, 2).reshape(64, 36))
    m["alog"] = f32(inp["gdn_a_log"][l].reshape(8, 1))
    m["dtb"] = f32(inp["gdn_dt_bias"][l].reshape(8, 1))
    m["mprev0"] = consts["mprev"] if c > 0 else np.zeros_like(consts["mprev"])
    m["mnextL"] = consts["mnext"] if c < 7 else np.zeros_like(consts["mnext"])
    return m


def k2_consts():
    i = np.arange(128)[:, None]
    j = np.arange(128)[None, :]
    same = (i // 64) == (j // 64)
    low_incl = same & (j <= i)
    low_strict = same & (j < i)
    trisel = np.zeros((128, 130), np.float32)
    trisel[:, :128] = low_incl.T
    trisel[:64, 128] = 1.0
    trisel[64:, 129] = 1.0
    MB = np.where(low_incl, 0.0, -30000.0)
    return dict(ident=f32(np.eye(128)), trisel=f32(trisel), bones=f32(same), MB=f32(MB), MBT=f32(MB.T),
                SM=f32(low_strict), SMT=f32(low_strict.T))


def prep_k2(consts, qT, kT, vT, beta, g, flip):
    if flip:
        qT, kT, vT, beta, g = qT[:, ::-1], kT[:, ::-1], vT[:, ::-1], beta[::-1], g[::-1]
    m = dict(consts)
    m["qT"] = f32(qT)
    m["kT"] = f32(kT)
    m["ktm"] = f32(kT.T)
    m["vtm"] = f32(vT.T)
    m["brow"] = f32(beta.reshape(1, -1))
    m["gcp"] = f32(g.reshape(128, 128).T)
    m["bcp"] = f32(beta.reshape(128, 128).T)
    return m


def prep_k3_w(inp, l):
    w_in = inp["w_in"][l]
    m = {}
    m["gn"] = f32(inp["attn_norm"][l].reshape(8, 128).T)
    m["wzg"] = f32(np.concatenate([w_in[:, 2304:2560], w_in[:, 2576:5648]], 1))
    m["gdng"] = f32(inp["gdn_norm"][l].reshape(64, 1))
    wb = np.concatenate([inp["w_branch_na"][l].reshape(4, 64, 1024), inp["w_branch_swa"][l].reshape(8, 64, 1024),
                         inp["w_branch_gdn"][l].reshape(4, 64, 1024)], 0)
    m["wb"] = f32(wb.transpose(1, 0, 2).reshape(64, 16 * 1024))
    m["wout"] = f32(inp["w_out"][l])
    return m


def prep_k4_w(inp, l):
    return dict(gn=f32(inp["ffn_norm"][l].reshape(8, 128).T),
                cw=f32(inp["ffn_conv_w"][l].T.reshape(44, 128, 3).transpose(1, 0, 2).reshape(128, 132)),
                cb=f32(inp["ffn_conv_b"][l].reshape(44, 128).T),
                wup=f32(inp["w_up"][l]), wdn=f32(inp["w_down"][l]))


from concourse.bass_utils import run_bass_kernel_spmd

NCORES = 8


def _run(nc, maps):
    res = run_bass_kernel_spmd(nc, maps, core_ids=list(range(NCORES)))
    return res.results


def kernel(**inp):
    inp = {k: np.asarray(v) for k, v in inp.items()}
    x = f32(inp["x"][0])
    Lq = x.shape[0]
    c1 = k1_consts()
    c2 = k2_consts()
    T = 2048
    for l in range(4):
        z = np.zeros((256, 1024), np.float32)
        xpad = np.concatenate([z, x, z], 0)
        r1 = _run(build_k1(), [prep_k1(inp, l, c, c1, xpad) for c in range(NCORES)])
        gq = np.concatenate([r["gqkv"] for r in r1], 2)
        bT = np.concatenate([r["betaT"] for r in r1], 1)
        gT = np.concatenate([r["gT"] for r in r1], 1)
        maps = []
        for c in range(NCORES):
            hh = c % 4
            maps.append(prep_k2(c2, gq[hh], gq[4 + hh], gq[8 + hh], bT[c], gT[c], flip=(c >= 4)))
        r2 = _run(build_k2(), maps)
        o = [r2[c]["o"] if c < 4 else r2[c]["o"][::-1] for c in range(NCORES)]
        wm = prep_k3_w(inp, l)
        maps = []
        for c in range(NCORES):
            s, e = c * T, (c + 1) * T
            m = dict(wm)
            m["xT"] = f32(x[s:e].T)
            m["ynaT"] = r1[c]["ynaT"]
            m["yswaT"] = r1[c]["yswaT"]
            m["ofT"] = f32(np.stack([o[h][s:e].T for h in range(4)], 0))
            m["obT"] = f32(np.stack([o[4 + h][s:e].T for h in range(4)], 0))
            maps.append(m)
        r3 = _run(build_k3(), maps)
        x1 = np.concatenate([r["x1T"].T for r in r3], 0)
        z1 = np.zeros((1, 1024), np.float32)
        x1pad = np.concatenate([z1, x1, z1], 0)
        w4 = prep_k4_w(inp, l)
        maps = []
        for c in range(NCORES):
            m = dict(w4)
            m["xT"] = f32(x1pad[c * T:c * T + T + 2].T)
            maps.append(m)
        r4 = _run(build_k4(), maps)
        x = f32(np.concatenate([r["oT"].T for r in r4], 0))
    return x[None].astype(np.float32)
```

```python
from contextlib import ExitStack
import numpy as np
import concourse.bass as bass
import concourse.mybir as mybir

F32 = mybir.dt.float32
BF16 = mybir.dt.bfloat16
AF = mybir.ActivationFunctionType
ALU = mybir.AluOpType
NDMA = 24


class V:
    def __init__(self, buf, ap):
        self.buf = buf
        self.ap = ap


class St:
    def __init__(self):
        self.w = None
        self.r = {}
        self.excl = False


class Buf:
    def __init__(self, t, name, st=None):
        self.t = t
        self.name = name
        self.st = st if st is not None else St()

    def sub(self, ap, name):
        return Buf(ap, name, self.st)

    def __getitem__(self, idx):
        return V(self, self.t[idx])

    def v(self, ap):
        return V(self, ap)


class Prog:
    def __init__(self):
        self.nc = bass.Bass("TRN2", target_bir_lowering=False)
        self.es = ExitStack()
        nc = self.nc
        self.engs = {"pe": nc.tensor, "act": nc.scalar, "dve": nc.vector, "pool": nc.gpsimd, "sp": nc.sync}
        self.semh = {}
        for e in self.engs:
            self.semh[e] = self.es.enter_context(nc.semaphore("s_" + e))
        for i in range(NDMA):
            self.semh[("dma", i)] = self.es.enter_context(nc.semaphore("s_dma%d" % i))
        self.cnt = {e: 0 for e in self.engs}
        self.dcnt = [0] * NDMA
        self.drr = 0
        self.seen = {e: {} for e in self.engs}
        self.ops = {e: [] for e in self.engs}
        self.nbuf = 0

    def sbuf(self, shape, dt, name=None):
        self.nbuf += 1
        name = (name or "sb") + "_%d" % self.nbuf
        t = self.es.enter_context(self.nc.sbuf_tensor(name, list(shape), dt))
        return Buf(t, name)

    def psum(self, shape, dt=F32, name=None):
        self.nbuf += 1
        name = (name or "ps") + "_%d" % self.nbuf
        t = self.es.enter_context(self.nc.psum_tensor(name, list(shape), dt))
        b = Buf(t, name)
        b.st.excl = True
        return b

    def dram(self, name, shape, dt, kind):
        t = self.nc.dram_tensor(name, list(shape), dt, kind=kind)
        return Buf(t.ap(), name)

    def _waits(self, e, reads, writes):
        waits = {}
        seen = self.seen[e]

        def need(k, v):
            if k is None or v <= 0:
                return
            if e == "pe" and k == "pe":
                return
            if seen.get(k, 0) >= v:
                return
            if waits.get(k, 0) < v:
                waits[k] = v

        for b in reads:
            if b.w is not None:
                need(*b.w)
            if b.excl:
                for k, v in b.r.items():
                    if k != e:
                        need(k, v)
        for b in writes:
            if b.w is not None:
                need(*b.w)
            for k, v in b.r.items():
                need(k, v)
        for k, v in waits.items():
            seen[k] = v
        return list(waits.items())

    def op(self, e, fn, reads, writes):
        reads = [(x.buf if isinstance(x, V) else x).st for x in reads]
        writes = [(x.buf if isinstance(x, V) else x).st for x in writes]
        wl = self._waits(e, reads, writes)
        self.cnt[e] += 1
        tok = (e, self.cnt[e])
        semh = self.semh

        def emit(eng):
            for k, v in wl:
                eng.wait_ge(semh[k], v)
            fn(eng).then_inc(semh[e], 1)

        self.ops[e].append(emit)
        for b in writes:
            b.w = tok
            b.r = {}
        for b in reads:
            if b not in writes:
                b.r[e] = tok[1]

    def dma(self, q, out, in_, **kw):
        i = self.drr
        self.drr = (i + 1) % NDMA
        key = ("dma", i)
        prev = self.dcnt[i]
        self.dcnt[i] += 16
        tok = (key, self.dcnt[i])
        wl = self._waits(q, [in_.buf.st], [out.buf.st])
        if prev > 0 and self.seen[q].get(key, 0) < prev:
            wl.append((key, prev))
            self.seen[q][key] = prev
        semh = self.semh
        oa, ia = out.ap, in_.ap

        def emit(eng):
            for k, v in wl:
                eng.wait_ge(semh[k], v)
            eng.dma_start(out=oa, in_=ia, **kw).then_inc(semh[key], 16)

        self.ops[q].append(emit)
        out.buf.st.w = tok
        out.buf.st.r = {}
        if in_.buf.st is not out.buf.st:
            in_.buf.st.r[key] = tok[1]

    def finish(self):
        wl = [(e, c) for e, c in self.cnt.items() if c > 0]
        wl += [(("dma", i), c) for i, c in enumerate(self.dcnt) if c > 0]
        semh = self.semh

        def emit(eng):
            for k, v in wl:
                eng.wait_ge(semh[k], v)

        self.ops["sp"].append(emit)
        ops = self.ops
        with self.nc.Block() as block:
            @block.tensor
            def _(e):
                for f in ops["pe"]:
                    f(e)

            @block.scalar
            def _(e):
                for f in ops["act"]:
                    f(e)

            @block.vector
            def _(e):
                for f in ops["dve"]:
                    f(e)

            @block.gpsimd
            def _(e):
                for f in ops["pool"]:
                    f(e)

            @block.sync
            def _(e):
                for f in ops["sp"]:
                    f(e)
        self.es.close()
        return self.nc

    def mm(self, out, lhsT, rhs, start=True, stop=True):
        self.op("pe", lambda e: e.matmul(out.ap, lhsT.ap, rhs.ap, start=start, stop=stop),
                [lhsT, rhs] + ([] if start else [out]), [out])

    def act(self, out, in_, func, bias=None, scale=None, eng="act"):
        kw = {}
        rd = [in_]
        if bias is not None:
            if isinstance(bias, V):
                kw["bias"] = bias.ap
                rd.append(bias)
            else:
                kw["bias"] = bias
        if scale is not None:
            if isinstance(scale, V):
                kw["scale"] = scale.ap
                rd.append(scale)
            else:
                kw["scale"] = scale
        self.op(eng, lambda e: e.activation(out.ap, in_.ap, func, **kw), rd, [out])

    def tt(self, out, in0, in1, op, eng="dve"):
        self.op(eng, lambda e: e.tensor_tensor(out.ap, in0.ap, in1.ap, op), [in0, in1], [out])

    def ts(self, out, in0, s1, op0, s2=None, op1=None, eng="dve"):
        rd = [in0]
        a1 = s1.ap if isinstance(s1, V) else s1
        a2 = s2.ap if isinstance(s2, V) else s2
        if isinstance(s1, V):
            rd.append(s1)
        if isinstance(s2, V):
            rd.append(s2)
        if op1 is None:
            self.op(eng, lambda e: e.tensor_scalar(out.ap, in0.ap, a1, None, op0), rd, [out])
        else:
            self.op(eng, lambda e: e.tensor_scalar(out.ap, in0.ap, a1, a2, op0, op1), rd, [out])

    def stt(self, out, in0, s, in1, op0, op1, eng="dve"):
        rd = [in0, in1]
        a = s.ap if isinstance(s, V) else s
        if isinstance(s, V):
            rd.append(s)
        self.op(eng, lambda e: e.scalar_tensor_tensor(out.ap, in0.ap, a, in1.ap, op0, op1), rd, [out])

    def copy(self, out, in_, eng="dve"):
        if eng == "act":
            self.op("act", lambda e: e.copy(out.ap, in_.ap), [in_], [out])
        else:
            self.op(eng, lambda e: e.tensor_copy(out.ap, in_.ap), [in_], [out])

    def recip(self, out, in_, eng="dve"):
        self.op(eng, lambda e: e.reciprocal(out.ap, in_.ap), [in_], [out])

    def memset(self, out, val, eng="pool"):
        self.op(eng, lambda e: e.memset(out.ap, val), [], [out])


def _barrier(self):
    wl_all = [(e, c) for e, c in self.cnt.items() if c > 0]
    wl_all += [(("dma", i), c) for i, c in enumerate(self.dcnt) if c > 0]
    semh = self.semh
    for e in self.engs:
        wl = [(k, v) for k, v in wl_all if self.seen[e].get(k, 0) < v and not (k == e)]
        for k, v in wl:
            self.seen[e][k] = v

        def emit(eng, wl=wl):
            for k, v in wl:
                eng.wait_ge(semh[k], v)

        self.ops[e].append(emit)


Prog.barrier = _barrier

from contextlib import contextmanager


@contextmanager
def _scope(self):
    old = self.es
    self.es = ExitStack()
    try:
        yield
    finally:
        self.barrier()
        self.es.close()
        self.es = old


Prog.scope = _scope


TOK = 2048
HALO = 256
NL = TOK + 2 * HALO
D = 1024
EPS = 1e-6


def qknorm(p, C, ps_in, N, gain, out, rope=None):
    p.act(C["sq"][0:64, 0:N], ps_in, AF.Square)
    p.mm(C["psm"][0:64, 0:N], C["ones64"][:, :], C["sq"][0:64, 0:N])
    p.act(C["r"][0:64, 0:N], C["psm"][0:64, 0:N], AF.Ln, bias=C["eps"][0:64, 0:1])
    p.act(C["r"][0:64, 0:N], C["r"][0:64, 0:N], AF.Exp, scale=-0.5)
    if rope is None:
        p.stt(out, ps_in, gain, C["r"][0:64, 0:N], ALU.mult, ALU.mult)
        return
    cosv, sinv = rope
    p.stt(C["qn"][0:64, 0:N], ps_in, gain, C["r"][0:64, 0:N], ALU.mult, ALU.mult)
    p.mm(C["psr"][0:64, 0:N], C["rotT"][:, :], C["qn"][0:64, 0:N])
    p.tt(C["t1"][0:64, 0:N], C["qn"][0:64, 0:N], cosv, ALU.mult, eng="pool")
    p.tt(C["t2"][0:64, 0:N], C["psr"][0:64, 0:N], sinv, ALU.mult)
    p.tt(out, C["t1"][0:64, 0:N], C["t2"][0:64, 0:N], ALU.add, eng="pool")


def build_k1():
    p = Prog()
    IN, OUT = "ExternalInput", "ExternalOutput"
    xT = p.dram("xT", [D, NL], F32, IN)
    gn = p.dram("gn", [128, 8], F32, IN)
    w = p.dram("w", [D, 2320], F32, IN)
    qkg = p.dram("qkg", [64, 4], F32, IN)
    cosT = p.dram("cosT", [64, NL], F32, IN)
    sinT = p.dram("sinT", [64, NL], F32, IN)
    ebt = p.dram("ebt", [64, 8 * 256], F32, IN)
    ebe = p.dram("ebe", [64, 7 * 12 * 256], F32, IN)
    cmask = p.dram("cmask", [64, 256], F32, IN)
    sinkb = p.dram("sinkb", [64, 8], F32, IN)
    gcw = p.dram("gcw", [64, 36], F32, IN)
    alog = p.dram("alog", [8, 1], F32, IN)
    dtb = p.dram("dtb", [8, 1], F32, IN)
    rotT_d = p.dram("rotT", [64, 64], F32, IN)
    mprev = p.dram("mprev", [128, 512], F32, IN)
    mnext = p.dram("mnext", [128, 512], F32, IN)
    mprev0 = p.dram("mprev0", [128, 512], F32, IN)
    mnextL = p.dram("mnextL", [128, 512], F32, IN)
    ynaT = p.dram("ynaT", [4, 64, TOK], F32, OUT)
    yswaT = p.dram("yswaT", [8, 64, TOK], F32, OUT)
    gqkv = p.dram("gqkv", [12, 64, TOK], F32, OUT)
    betaT = p.dram("betaT", [8, TOK], F32, OUT)
    gT = p.dram("gT", [8, TOK], F32, OUT)

    ps = [p.psum([128, 512], F32, "ps%d" % i) for i in range(8)]
    hT = p.sbuf([128, 8, NL], BF16, "hT")
    gn_sb = p.sbuf([128, 8], F32, "gn_sb")
    qkg_sb = p.sbuf([64, 4], F32, "qkg_sb")
    ones128 = p.sbuf([128, 128], BF16, "ones128")
    ones64 = p.sbuf([64, 64], BF16, "ones64")
    one64 = p.sbuf([128, 64], BF16, "one64")
    rot_f = p.sbuf([64, 64], F32, "rot_f")
    rotT = p.sbuf([64, 64], BF16, "rotTb")
    p.memset(ones128[:, :], 1.0 / D)
    p.memset(ones64[:, :], 1.0 / 64)
    p.memset(one64[:, :], 1.0)
    epsc = p.sbuf([128, 1], F32, "epsc")
    p.memset(epsc[:, :], EPS)
    p.dma("sp", gn_sb[:, :], gn[:, :])
    p.dma("sp", qkg_sb[:, :], qkg[:, :])
    p.dma("sp", rot_f[:, :], rotT_d[:, :])
    p.copy(rotT[:, :], rot_f[:, :])

    with p.scope():
        xb = [p.sbuf([128, 8, 512], F32, "xb%d" % i) for i in range(2)]
        sq = p.sbuf([128, 8, 512], BF16, "sq1")
        rstd = p.sbuf([128, 512], F32, "rstd1")
        for b in range(NL // 512):
            X = xb[b % 2]
            c0 = 512 * b
            p.dma("sp", X[:, :, :], xT.v(xT.t[:, c0:c0 + 512].rearrange("(k p) t -> p k t", p=128)))
            for k in range(8):
                p.act(sq[:, k, :], X[:, k, :], AF.Square)
            for k in range(8):
                p.mm(ps[0][:, :], ones128[:, :], sq[:, k, :], start=(k == 0), stop=(k == 7))
            p.act(rstd[:, :], ps[0][:, :], AF.Ln, bias=epsc[:, 0:1])
            p.act(rstd[:, :], rstd[:, :], AF.Exp, scale=-0.5)
            for k in range(8):
                p.stt(hT[:, k, c0:c0 + 512], X[:, k, :], gn_sb[:, k:k + 1], rstd[:, :], ALU.mult, ALU.mult)

    with p.scope():
        NB = 410
        wg = [p.sbuf([128, 784], BF16, "wg%d" % k) for k in range(8)]
        for k in range(8):
            p.dma("pool", wg[k][:, :], w[k * 128:(k + 1) * 128, 1536:2320])
        gcw_sb = p.sbuf([64, 36], F32, "gcw_sb")
        p.dma("sp", gcw_sb[:, :], gcw[:, :])
        prev2 = p.sbuf([64, 24], F32, "prev2")
        p.memset(prev2[:, :], 0.0)
        ub = [p.sbuf([64, NB + 2], F32, "ub%d" % i) for i in range(3)]
        Y = [p.sbuf([64, NB], F32, "Y%d" % i) for i in range(3)]
        sqg = p.sbuf([64, NB], BF16, "sqg")
        rg = p.sbuf([64, NB], F32, "rg")
        it = 0
        for b in range(5):
            c0 = HALO - 1 + NB * b
            i0 = 2 if b == 0 else 0
            for t in range(12):
                pu = ps[it % 2]
                U = ub[it % 3]
                y = Y[it % 3]
                it += 1
                for k in range(8):
                    p.mm(pu[0:64, 0:NB], wg[k][:, t * 64:(t + 1) * 64], hT[:, k, c0:c0 + NB], start=(k == 0), stop=(k == 7))
                p.copy(U[:, 0:2], prev2[:, 2 * t:2 * t + 2], eng="pool")
                p.act(U[:, 2:NB + 2], pu[0:64, 0:NB], AF.Copy)
                p.copy(prev2[:, 2 * t:2 * t + 2], U[:, NB:NB + 2], eng="pool")
                p.ts(y[:, :], U[:, 0:NB], gcw_sb[:, 3 * t:3 * t + 1], ALU.mult)
                p.stt(y[:, :], U[:, 1:NB + 1], gcw_sb[:, 3 * t + 1:3 * t + 2], y[:, :], ALU.mult, ALU.add)
                p.stt(y[:, :], U[:, 2:NB + 2], gcw_sb[:, 3 * t + 2:3 * t + 3], y[:, :], ALU.mult, ALU.add)
                p.act(y[:, :], y[:, :], AF.Silu)
                if t < 8:
                    p.act(sqg[:, :], y[:, :], AF.Square)
                    p.mm(ps[2][0:64, 0:NB], one64[0:64, :], sqg[:, :])
                    p.act(rg[:, :], ps[2][0:64, 0:NB], AF.Ln, bias=epsc[0:64, 0:1])
                    p.act(rg[:, :], rg[:, :], AF.Exp, scale=-0.5)
                    if t < 4:
                        p.stt(y[:, :], y[:, :], 0.125, rg[:, :], ALU.mult, ALU.mult)
                    else:
                        p.tt(y[:, :], y[:, :], rg[:, :], ALU.mult)
                oc = NB * b - 2 + i0
                p.dma("sp", gqkv.v(gqkv.t[t, :, oc:oc + NB - i0]), y[:, i0:NB])
        alog_sb = p.sbuf([8, 1], F32, "alog_sb")
        dtb_sb = p.sbuf([8, 1], F32, "dtb_sb")
        negA = p.sbuf([8, 1], F32, "negA")
        p.dma("sp", alog_sb[:, :], alog[:, :])
        p.dma("sp", dtb_sb[:, :], dtb[:, :])
        p.act(negA[:, :], alog_sb[:, :], AF.Exp)
        p.ts(negA[:, :], negA[:, :], -1.0, ALU.mult)
        bo = [p.sbuf([8, 512], F32, "bo%d" % i) for i in range(2)]
        go = [p.sbuf([8, 512], F32, "go%d" % i) for i in range(2)]
        for b in range(4):
            c0 = HALO + 512 * b
            for k in range(8):
                p.mm(ps[3][0:8, :], wg[k][:, 768:776], hT[:, k, c0:c0 + 512], start=(k == 0), stop=(k == 7))
            for k in range(8):
                p.mm(ps[4][0:8, :], wg[k][:, 776:784], hT[:, k, c0:c0 + 512], start=(k == 0), stop=(k == 7))
            B_, G_ = bo[b % 2], go[b % 2]
            p.act(B_[:, :], ps[3][0:8, :], AF.Sigmoid)
            p.act(G_[:, :], ps[4][0:8, :], AF.Exp, bias=dtb_sb[:, 0:1])
            p.act(G_[:, :], G_[:, :], AF.Ln, bias=1.0)
            p.ts(G_[:, :], G_[:, :], negA[:, 0:1], ALU.mult)
            p.dma("sp", betaT[:, 512 * b:512 * b + 512], B_[:, :])
            p.dma("sp", gT[:, 512 * b:512 * b + 512], G_[:, :])

    def mk_scratch():
        C = dict(ones64=ones64, rotT=rotT, eps=epsc)
        C["sq"] = p.sbuf([64, 512], BF16, "qsq")
        C["r"] = p.sbuf([64, 512], F32, "qr")
        C["qn"] = p.sbuf([64, 512], BF16, "qqn")
        C["t1"] = p.sbuf([64, 512], F32, "qt1")
        C["t2"] = p.sbuf([64, 512], F32, "qt2")
        C["psm"] = ps[6]
        C["psr"] = ps[7]
        return C

    with p.scope():
        C = mk_scratch()
        wn = [p.sbuf([128, 768], BF16, "wn%d" % k) for k in range(8)]
        for k in range(8):
            p.dma("pool", wn[k][:, :], w[k * 128:(k + 1) * 128, 0:768])
        qaT = [p.sbuf([64, TOK], BF16, "qaT%d" % h) for h in range(4)]
        kaT = [p.sbuf([64, NL], BF16, "kaT%d" % h) for h in range(4)]
        va = p.sbuf([64, 40, 256], BF16, "va")
        for h in range(4):
            for nb in range(4):
                c0 = HALO + 512 * nb
                pq = ps[nb % 2]
                for k in range(8):
                    p.mm(pq[0:64, :], wn[k][:, h * 64:(h + 1) * 64], hT[:, k, c0:c0 + 512], start=(k == 0), stop=(k == 7))
                qknorm(p, C, pq[0:64, :], 512, qkg_sb[:, 0:1], qaT[h][:, 512 * nb:512 * nb + 512])
            for nb in range(5):
                c0 = 512 * nb
                pq = ps[nb % 2]
                for k in range(8):
                    p.mm(pq[0:64, :], wn[k][:, 256 + h * 64:256 + (h + 1) * 64], hT[:, k, c0:c0 + 512], start=(k == 0), stop=(k == 7))
                qknorm(p, C, pq[0:64, :], 512, qkg_sb[:, 1:2], kaT[h][:, c0:c0 + 512])
        for rr in range(40):
            pv = ps[2 + rr % 2]
            for k in range(8):
                p.mm(pv[0:64, 0:256], hT[:, k, 64 * rr:64 * rr + 64], wn[k][:, 512:768], start=(k == 0), stop=(k == 7))
            p.copy(va[:, rr, :], pv[0:64, 0:256], eng=("act" if rr % 2 else "dve"))
        cm = p.sbuf([64, 256], F32, "cm")
        p.dma("sp", cm[:, :], cmask[:, :])
        tmpt = p.sbuf([64, 12 * 256], F32, "tmpt")
        EBs = p.sbuf([64, 8, 256], BF16, "EBs")
        EBe = p.sbuf([64, 7 * 12, 256], BF16, "EBe")
        p.dma("sp", tmpt[:, 0:2048], ebt[:, :])
        p.act(tmpt[:, 0:2048], tmpt[:, 0:2048], AF.Exp)
        for j in range(8):
            p.tt(EBs[:, j, :], tmpt[:, 256 * j:256 * j + 256], cm[:, :], ALU.mult)
        for e_ in range(7):
            p.dma("sp", tmpt[:, :], ebe[:, e_ * 3072:(e_ + 1) * 3072])
            p.act(tmpt[:, :], tmpt[:, :], AF.Exp)
            for s_ in range(12):
                p.tt(EBe[:, e_ * 12 + s_, :], tmpt[:, 256 * s_:256 * s_ + 256], cm[:, :], ALU.mult)
        E1 = [p.sbuf([64, 256], F32, "E1_%d" % i) for i in range(3)]
        E2 = [p.sbuf([64, 256], BF16, "E2_%d" % i) for i in range(3)]
        rden = p.sbuf([64, 256], F32, "rden")
        yst = [p.sbuf([64, 256], F32, "yst%d" % i) for i in range(3)]
        ei = 0
        for qr in range(32):
            if qr < 4:
                slots = [(kr, EBe[:, qr * 12 + (kr - qr), :]) for kr in range(qr, 12)]
            elif qr >= 29:
                slots = [(kr, EBe[:, (qr - 29 + 4) * 12 + (kr - 28), :]) for kr in range(28, qr + 8)]
            else:
                slots = [(qr + j, EBs[:, j, :]) for j in range(8)]
            pacc = ps[4 + qr % 2]
            q0 = 64 * qr
            for si, (kr, tab) in enumerate(slots):
                pst = ps[si % 2]
                for h in range(4):
                    p.mm(pst[0:64, h * 64:(h + 1) * 64], kaT[h][:, 64 * kr:64 * kr + 64], qaT[h][:, q0:q0 + 64])
                e1, e2 = E1[ei % 3], E2[ei % 3]
                ei += 1
                p.act(e1[:, :], pst[0:64, 0:256], AF.Exp, scale=0.125)
                p.tt(e2[:, :], e1[:, :], tab, ALU.mult, eng=("pool" if ei % 2 else "dve"))
                for h in range(4):
                    p.mm(pacc[0:64, h * 64:(h + 1) * 64], va[:, kr, h * 64:(h + 1) * 64], e2[:, h * 64:(h + 1) * 64],
                         start=(si == 0 and h == 0), stop=(si == len(slots) - 1))
                p.mm(pacc[0:64, 256:512], one64[0:64, :], e2[:, :], start=False, stop=(si == len(slots) - 1))
            ys = yst[qr % 3]
            p.act(rden[:, :], pacc[0:64, 256:512], AF.Ln)
            p.act(rden[:, :], rden[:, :], AF.Exp, scale=-1.0)
            p.tt(ys[:, :], pacc[0:64, 0:256], rden[:, :], ALU.mult)
            p.dma("sp", ynaT.v(ynaT.t[:, :, q0:q0 + 64].rearrange("h d t -> d h t")),
                  ys.v(ys.t[:, :].rearrange("d (h t) -> d h t", h=4)))

    with p.scope():
        C = mk_scratch()
        wsw = [p.sbuf([128, 768], BF16, "wsw%d" % k) for k in range(8)]
        for k in range(8):
            p.dma("pool", wsw[k][:, :], w[k * 128:(k + 1) * 128, 768:1536])
        cs = p.sbuf([64, NL], F32, "cs")
        sn = p.sbuf([64, NL], F32, "sn")
        p.dma("sp", cs[:, :], cosT[:, :])
        p.dma("sp", sn[:, :], sinT[:, :])
        qsT = [p.sbuf([64, TOK], BF16, "qsT%d" % h) for h in range(8)]
        ksT = [p.sbuf([64, NL], BF16, "ksT%d" % g) for g in range(2)]
        vs = p.sbuf([128, 20, 128], BF16, "vs")
        for h in range(8):
            for nb in range(4):
                c0 = HALO + 512 * nb
                pq = ps[nb % 2]
                for k in range(8):
                    p.mm(pq[0:64, :], wsw[k][:, h * 64:(h + 1) * 64], hT[:, k, c0:c0 + 512], start=(k == 0), stop=(k == 7))
                qknorm(p, C, pq[0:64, :], 512, qkg_sb[:, 2:3], qsT[h][:, 512 * nb:512 * nb + 512],
                       rope=(cs[:, c0:c0 + 512], sn[:, c0:c0 + 512]))
        for g in range(2):
            for nb in range(5):
                c0 = 512 * nb
                pq = ps[nb % 2]
                for k in range(8):
                    p.mm(pq[0:64, :], wsw[k][:, 512 + g * 64:512 + (g + 1) * 64], hT[:, k, c0:c0 + 512], start=(k == 0), stop=(k == 7))
                qknorm(p, C, pq[0:64, :], 512, qkg_sb[:, 3:4], ksT[g][:, c0:c0 + 512],
                       rope=(cs[:, c0:c0 + 512], sn[:, c0:c0 + 512]))
        for tb in range(20):
            pv = ps[2 + tb % 2]
            for k in range(8):
                p.mm(pv[:, 0:128], hT[:, k, 128 * tb:128 * tb + 128], wsw[k][:, 640:768], start=(k == 0), stop=(k == 7))
            p.copy(vs[:, tb, :], pv[:, 0:128], eng=("act" if tb % 2 else "dve"))
        mtmp = p.sbuf([128, 512], F32, "mtmp")
        masks = {}
        for nm, src in (("prev", mprev), ("next", mnext), ("prev0", mprev0), ("nextL", mnextL)):
            m = p.sbuf([128, 512], BF16, "m_" + nm)
            p.dma("sp", mtmp[:, :], src[:, :])
            p.copy(m[:, :], mtmp[:, :])
            masks[nm] = m
        es = p.sbuf([64, 8], F32, "es")
        p.dma("sp", es[:, :], sinkb[:, :])
        p.act(es[:, :], es[:, :], AF.Exp)
        one_f = p.sbuf([64, 128], F32, "one_f")
        p.memset(one_f[:, :], 1.0)
        EsT = p.sbuf([64, 8, 128], F32, "EsT")
        for h in range(8):
            p.ts(EsT[:, h, :], one_f[:, :], es[:, h:h + 1], ALU.mult)
        E1 = [p.sbuf([128, 512], BF16, "S1_%d" % i) for i in range(3)]
        rden = p.sbuf([64, 512], F32, "srden")
        yst = [p.sbuf([64, 512], F32, "syst%d" % i) for i in range(3)]
        ei = 0
        for n in range(16):
            q0 = 128 * n
            for g in range(2):
                pnum = ps[2 + (2 * n + g) % 2]
                pden = ps[4 + (2 * n + g) % 2]
                for bi, kb in enumerate((-1, 0, 1)):
                    k0 = HALO + 128 * (n + kb)
                    pst = ps[bi % 2]
                    for hh in range(4):
                        p.mm(pst[:, hh * 128:(hh + 1) * 128], ksT[g][:, k0:k0 + 128], qsT[4 * g + hh][:, q0:q0 + 128])
                    e1 = E1[ei % 3]
                    ei += 1
                    p.act(e1[:, :], pst[:, :], AF.Exp, scale=0.125)
                    if kb == -1:
                        p.tt(e1[:, :], e1[:, :], masks["prev0" if n == 0 else "prev"][:, :], ALU.mult, eng="pool")
                    elif kb == 1:
                        p.tt(e1[:, :], e1[:, :], masks["nextL" if n == 15 else "next"][:, :], ALU.mult, eng="pool")
                    tb = (k0 // 128)
                    for hh in range(4):
                        p.mm(pnum[0:64, hh * 128:(hh + 1) * 128], vs[:, tb, g * 64:(g + 1) * 64], e1[:, hh * 128:(hh + 1) * 128],
                             start=(bi == 0 and hh == 0), stop=(bi == 2))
                    p.mm(pden[0:64, :], one64[:, :], e1[:, :], start=(bi == 0), stop=(bi == 2))
                ys = yst[(2 * n + g) % 3]
                p.tt(rden[:, :], pden[0:64, :], EsT.v(EsT.t[:, 4 * g:4 * g + 4, :].rearrange("d h t -> d (h t)")), ALU.add)
                p.act(rden[:, :], rden[:, :], AF.Ln)
                p.act(rden[:, :], rden[:, :], AF.Exp, scale=-1.0)
                p.tt(ys[:, :], pnum[0:64, :], rden[:, :], ALU.mult)
                p.dma("sp", yswaT.v(yswaT.t[4 * g:4 * g + 4, :, q0:q0 + 128].rearrange("h d t -> d h t")),
                      ys.v(ys.t[:, :].rearrange("d (h t) -> d h t", h=4)))
    return p.finish()


L = 16384
NCP = 128
NGP = 32


def build_k2(NGP=NGP, maxsteps=None):
    p = Prog()
    IN, OUT = "ExternalInput", "ExternalOutput"
    qT = p.dram("qT", [64, L], F32, IN)
    kT = p.dram("kT", [64, L], F32, IN)
    ktm = p.dram("ktm", [L, 64], F32, IN)
    vtm = p.dram("vtm", [L, 64], F32, IN)
    brow = p.dram("brow", [1, L], F32, IN)
    gcp = p.dram("gcp", [128, NCP], F32, IN)
    bcp = p.dram("bcp", [128, NCP], F32, IN)
    ident_d = p.dram("ident", [128, 512], F32, IN)
    trisel_d = p.dram("trisel", [128, 130], F32, IN)
    bones_d = p.dram("bones", [128, 128], F32, IN)
    MB_d = p.dram("MB", [128, 512], F32, IN)
    MBT_d = p.dram("MBT", [128, 512], F32, IN)
    SM_d = p.dram("SM", [128, 512], F32, IN)
    SMT_d = p.dram("SMT", [128, 512], F32, IN)
    o = p.dram("o", [L, 64], F32, OUT)

    def const(src, shape, nm):
        b = p.sbuf(shape, F32, nm)
        p.dma("sp", b[:, :], src[:, :])
        return b
    ident4 = const(ident_d, [128, 512], "ident4")
    trisel = const(trisel_d, [128, 130], "trisel")
    bones = const(bones_d, [128, 128], "bones")
    MB4 = const(MB_d, [128, 512], "MB4")
    MBT4 = const(MBT_d, [128, 512], "MBT4")
    SM4 = const(SM_d, [128, 512], "SM4")
    SMT4 = const(SMT_d, [128, 512], "SMT4")
    g_all = const(gcp, [128, NCP], "g_all")
    b_all = const(bcp, [128, NCP], "b_all")
    ones_f = p.sbuf([128, 128], F32, "ones_f")
    p.memset(ones_f[:, :], 1.0)

    bk = [p.psum([128, 512], F32, "bank%d" % i) for i in range(8)]
    gc_all = p.sbuf([128, NCP], F32, "gc_all")
    ngc_all = p.sbuf([128, NCP], F32, "ngc_all")
    coefw = p.sbuf([128, NCP], F32, "coefw")
    coeft = p.sbuf([128, NCP], F32, "coeft")
    p.mm(bk[0][:, 0:128], trisel[:, 0:128], g_all[:, :])
    p.copy(gc_all[:, :], bk[0][:, 0:128])
    p.ts(ngc_all[:, :], gc_all[:, :], -1.0, ALU.mult)
    p.act(coefw[:, :], gc_all[:, :], AF.Exp)
    p.tt(coefw[:, :], coefw[:, :], b_all[:, :], ALU.mult)
    p.mm(bk[1][:, 0:128], bones[:, :], g_all[:, :])
    p.tt(coeft[:, :], bk[1][:, 0:128], gc_all[:, :], ALU.subtract)
    p.act(coeft[:, :], coeft[:, :], AF.Exp)

    def mk_set(i):
        s = {}
        def sb(nm, shape, dt=F32):
            s[nm] = p.sbuf(shape, dt, "%s_%d" % (nm, i))
        sb("Q", [64, 512]); sb("K", [64, 512]); sb("KT", [128, 4, 64]); sb("VT", [128, 4, 64]); sb("BR", [1, 512])
        sb("kbT", [64, 512], BF16); sb("kTb", [64, 512], BF16); sb("qTb", [64, 512], BF16)
        sb("gbc", [128, 512]); sb("tmp", [128, 512]); sb("tmp2", [128, 512])
        sb("decay", [128, 512]); sb("decayT", [128, 512]); sb("decay_s", [128, 512]); sb("decayT_s", [128, 512])
        sb("eg", [64, 512]); sb("egS", [64, 8])
        for nm in ("A0", "A1", "B0", "B1", "X0", "X1"):
            sb(nm, [128, 512], BF16)
        sb("rhs_u", [128, 4, 64], BF16); sb("rhs_w", [128, 4, 64], BF16); sb("OS", [64, 8, 64])
        for ci in range(4):
            sb("ktail%d" % ci, [128, 64]); sb("u%d" % ci, [128, 64]); sb("W3%d" % ci, [64, 192])
            sb("QKT%d" % ci, [128, 128]); sb("qdT%d" % ci, [64, 128])
            p.memset(s["W3%d" % ci][:, :], 0.0)
        return s
    sets = [mk_set(0), mk_set(1)]
    vnew = p.sbuf([128, 64], F32, "vnew")
    Sb = [p.sbuf([64, 64], F32, "S%d" % i) for i in range(2)]
    p.memset(Sb[0][:, :], 0.0)
    state = {"si": 0}

    def pre(gp, s):
        T0 = 512 * gp
        Q, K, KT, VT, BR = s["Q"], s["K"], s["KT"], s["VT"], s["BR"]
        p.dma("sp", Q[:, :], qT[:, T0:T0 + 512])
        p.dma("sp", K[:, :], kT[:, T0:T0 + 512])
        p.dma("sp", KT[:, :, :], ktm.v(ktm.t[T0:T0 + 512, :].rearrange("(c p) d -> p c d", p=128)))
        p.dma("sp", VT[:, :, :], vtm.v(vtm.t[T0:T0 + 512, :].rearrange("(c p) d -> p c d", p=128)))
        p.dma("sp", BR[:, :], brow[:, T0:T0 + 512])
        yield
        p.mm(bk[3][0:64, :], ones_f[0:1, 0:64], BR[:, :])
        p.tt(s["kbT"][:, :], K[:, :], bk[3][0:64, :], ALU.mult)
        yield
        p.copy(s["kTb"][:, :], K[:, :], eng="pool")
        p.copy(s["qTb"][:, :], Q[:, :], eng="pool")
        yield
        for ci in range(4):
            cp = 4 * gp + ci
            p.ts(s["gbc"][:, 128 * ci:128 * ci + 128], ones_f[:, :], g_all[:, cp:cp + 1], ALU.mult, eng="pool")
        yield
        for ci in range(4):
            p.mm(bk[0][:, 128 * ci:128 * ci + 128], s["gbc"][:, 128 * ci:128 * ci + 128], trisel[:, 0:128])
        for ci in range(4):
            p.mm(bk[4][0:64, 2 * ci:2 * ci + 2], s["gbc"][:, 128 * ci:128 * ci + 64], trisel[:, 128:130])
        yield
        p.tt(s["tmp"][:, :], MB4[:, :], bk[0][:, :], ALU.subtract)
        p.tt(s["tmp2"][:, :], MBT4[:, :], bk[0][:, :], ALU.add)
        p.act(s["eg"][:, :], bk[0][0:64, :], AF.Exp)
        p.act(s["egS"][:, :], bk[4][0:64, 0:8], AF.Exp)
        yield
        for ci in range(4):
            cp = 4 * gp + ci
            sl = slice(128 * ci, 128 * ci + 128)
            p.act(s["decay"][:, sl], s["tmp"][:, sl], AF.Exp, bias=gc_all[:, cp:cp + 1])
            p.act(s["decayT"][:, sl], s["tmp2"][:, sl], AF.Exp, bias=ngc_all[:, cp:cp + 1])
            yield
        p.tt(s["decay_s"][:, :], s["decay"][:, :], SM4[:, :], ALU.mult, eng="pool")
        p.tt(s["decayT_s"][:, :], s["decayT"][:, :], SMT4[:, :], ALU.mult, eng="pool")
        yield
        for ci in range(4):
            sl = slice(128 * ci, 128 * ci + 128)
            p.mm(bk[1][:, sl], s["kbT"][:, sl], s["kTb"][:, sl])
        for ci in range(4):
            sl = slice(128 * ci, 128 * ci + 128)
            p.mm(bk[2][:, sl], s["kTb"][:, sl], s["kbT"][:, sl])
        yield
        A, B, X = s["A0"], s["B0"], s["X0"]
        p.tt(B[:, :], bk[1][:, :], s["decay_s"][:, :], ALU.mult)
        p.tt(A[:, :], bk[2][:, :], s["decayT_s"][:, :], ALU.mult)
        yield
        p.tt(X[:, :], ident4[:, :], A[:, :], ALU.subtract, eng="pool")
        yield
        for k in range(1, 6):
            A2, B2, X2 = s["A%d" % (k % 2)], s["B%d" % (k % 2)], s["X%d" % (k % 2)]
            for ci in range(4):
                sl = slice(128 * ci, 128 * ci + 128)
                p.mm(bk[3][:, sl], B[:, sl], A[:, sl])
            yield
            for ci in range(4):
                sl = slice(128 * ci, 128 * ci + 128)
                p.mm(bk[4][:, sl], A[:, sl], B[:, sl])
            yield
            p.copy(A2[:, :], bk[3][:, :], eng="act")
            p.copy(B2[:, :], bk[4][:, :])
            yield
            for ci in range(4):
                sl = slice(128 * ci, 128 * ci + 128)
                p.mm(bk[0][:, sl], B2[:, sl], X[:, sl])
            yield
            p.tt(X2[:, :], bk[0][:, :], X[:, :], ALU.add)
            yield
            A, B, X = A2, B2, X2
        TT = X
        for ci in range(4):
            cp = 4 * gp + ci
            p.ts(s["rhs_u"][:, ci, :], VT[:, ci, :], b_all[:, cp:cp + 1], ALU.mult, eng="pool")
            p.ts(s["rhs_w"][:, ci, :], KT[:, ci, :], coefw[:, cp:cp + 1], ALU.mult, eng="pool")
            p.ts(s["ktail%d" % ci][:, :], KT[:, ci, :], coeft[:, cp:cp + 1], ALU.mult, eng="pool")
            yield
        for ci in range(4):
            sl = slice(128 * ci, 128 * ci + 128)
            p.mm(bk[0][:, 64 * ci:64 * ci + 64], TT[:, sl], s["rhs_u"][:, ci, :])
        yield
        for ci in range(4):
            p.copy(s["u%d" % ci][:, :], bk[0][:, 64 * ci:64 * ci + 64], eng="act")
        yield
        for ci in range(4):
            sl = slice(128 * ci, 128 * ci + 128)
            p.mm(bk[1][0:64, sl], s["rhs_w"][:, ci, :], TT[:, sl])
        yield
        for ci in range(4):
            p.copy(s["W3%d" % ci][:, 0:64], bk[1][0:64, 128 * ci:128 * ci + 64], eng="act")
            p.copy(s["W3%d" % ci][:, 128:192], bk[1][0:64, 128 * ci + 64:128 * ci + 128], eng="act")
        yield
        for ci in range(4):
            sl = slice(128 * ci, 128 * ci + 128)
            p.mm(bk[2][:, sl], s["kTb"][:, sl], s["qTb"][:, sl])
        yield
        for ci in range(4):
            sl = slice(128 * ci, 128 * ci + 128)
            p.tt(s["QKT%d" % ci][:, :], bk[2][:, sl], s["decayT"][:, sl], ALU.mult)
            p.tt(s["qdT%d" % ci][:, :], Q[:, sl], s["eg"][:, sl], ALU.mult, eng="pool")
        yield

    def scan(gp, s):
        T0 = 512 * gp
        OS = s["OS"]
        for ci in range(4):
            c0 = 128 * ci
            for half in range(2):
                lo = 64 * half
                S, S2 = Sb[state["si"] % 2], Sb[(state["si"] + 1) % 2]
                state["si"] += 1
                p.mm(bk[5][:, 0:64], s["W3%d" % ci][:, lo:lo + 128], S[:, :])
                yield
                p.tt(vnew[lo:lo + 64, :], s["u%d" % ci][lo:lo + 64, :], bk[5][lo:lo + 64, 0:64], ALU.subtract)
                yield
                p.mm(bk[6][0:64, 0:64], s["qdT%d" % ci][:, lo:lo + 64], S[:, :], start=True, stop=False)
                p.mm(bk[6][0:64, 0:64], s["QKT%d" % ci][lo:lo + 64, lo:lo + 64], vnew[lo:lo + 64, :], start=False, stop=True)
                yield
                p.copy(OS[:, 2 * ci + half, :], bk[6][0:64, 0:64], eng="act")
                p.mm(bk[7][0:64, 0:64], s["ktail%d" % ci][lo:lo + 64, :], vnew[lo:lo + 64, :])
                yield
                p.stt(S2[:, :], S[:, :], s["egS"][:, 2 * ci + half:2 * ci + half + 1], bk[7][0:64, 0:64], ALU.mult, ALU.add)
                yield
        p.dma("sp", o.v(o.t[T0:T0 + 512, :].rearrange("(c p) d -> p c d", p=64)), OS[:, :, :])
        yield

    for gp in range(NGP):
        for _ in pre(gp, sets[gp % 2]):
            pass
        p.barrier()
        for _ in scan(gp, sets[gp % 2]):
            pass
        p.barrier()
    return p.finish()


TOK = 2048
D = 1024
EPS = 1e-6


def build_k3():
    p = Prog()
    IN, OUT = "ExternalInput", "ExternalOutput"
    xT = p.dram("xT", [D, TOK], F32, IN)
    gn = p.dram("gn", [128, 8], F32, IN)
    wzg = p.dram("wzg", [D, 3328], F32, IN)
    gdng = p.dram("gdng", [64, 1], F32, IN)
    wb = p.dram("wb", [64, 16 * D], F32, IN)
    wout = p.dram("wout", [D, D], F32, IN)
    ynaT = p.dram("ynaT", [4, 64, TOK], F32, IN)
    yswaT = p.dram("yswaT", [8, 64, TOK], F32, IN)
    ofT = p.dram("ofT", [4, 64, TOK], F32, IN)
    obT = p.dram("obT", [4, 64, TOK], F32, IN)
    x1T = p.dram("x1T", [D, TOK], F32, OUT)

    ps = [p.psum([128, 512], F32, "ps%d" % i) for i in range(8)]
    wzg_sb = [p.sbuf([128, 3328], BF16, "wzg%d" % k) for k in range(8)]
    wb_sb = p.sbuf([64, 16, D], BF16, "wb_sb")
    wout_sb = [p.sbuf([128, D], BF16, "wout%d" % k) for k in range(8)]
    gn_sb = p.sbuf([128, 8], F32, "gn_sb")
    gdng_sb = p.sbuf([64, 1], F32, "gdng_sb")
    ones128 = p.sbuf([128, 128], BF16, "ones128")
    ones64 = p.sbuf([64, 64], BF16, "ones64")
    p.memset(ones128[:, :], 1.0 / D)
    p.memset(ones64[:, :], 1.0 / 64)
    epsc = p.sbuf([128, 1], F32, "epsc")
    p.memset(epsc[:, :], EPS)
    p.dma("sp", gn_sb[:, :], gn[:, :])
    p.dma("sp", gdng_sb[:, :], gdng[:, :])
    for k in range(8):
        p.dma("pool", wzg_sb[k][:, :], wzg[k * 128:(k + 1) * 128, :], max_dma_last_dim=8192)
    for j in range(16):
        p.dma("pool", wb_sb[:, j, :], wb[:, j * D:(j + 1) * D])
    for k in range(8):
        p.dma("pool", wout_sb[k][:, :], wout[k * 128:(k + 1) * 128, :])

    X = p.sbuf([128, 8, 512], F32, "X")
    sq = p.sbuf([128, 8, 512], BF16, "sq")
    rstd = p.sbuf([128, 512], F32, "rstd")
    hT = p.sbuf([128, 8, 512], BF16, "hT")
    yna = p.sbuf([64, 4, 512], BF16, "yna")
    yswa = p.sbuf([64, 8, 512], BF16, "yswa")
    of = p.sbuf([64, 4, 512], F32, "of")
    ob = p.sbuf([64, 4, 512], F32, "ob")
    yg = p.sbuf([64, 4, 512], BF16, "yg")
    sz = p.sbuf([64, 512], F32, "sz")
    osum = p.sbuf([64, 512], F32, "osum")
    sq2 = p.sbuf([64, 512], BF16, "sq2")
    r2 = p.sbuf([64, 512], F32, "r2")
    sg = [p.sbuf([128, 512], F32, "sg%d" % i) for i in range(2)]
    macc = p.sbuf([128, 512], F32, "macc")
    mtmp = p.sbuf([128, 512], F32, "mtmp")
    merged = p.sbuf([128, 8, 512], BF16, "merged")
    xo = [p.sbuf([128, 512], F32, "xo%d" % i) for i in range(2)]

    for b in range(TOK // 512):
        c0 = 512 * b
        p.dma("sp", X[:, :, :], xT.v(xT.t[:, c0:c0 + 512].rearrange("(k p) t -> p k t", p=128)))
        p.dma("pool", yna[:, :, :], ynaT.v(ynaT.t[:, :, c0:c0 + 512].rearrange("h d t -> d h t")))
        p.dma("pool", yswa[:, :, :], yswaT.v(yswaT.t[:, :, c0:c0 + 512].rearrange("h d t -> d h t")))
        p.dma("sp", of[:, :, :], ofT.v(ofT.t[:, :, c0:c0 + 512].rearrange("h d t -> d h t")))
        p.dma("sp", ob[:, :, :], obT.v(obT.t[:, :, c0:c0 + 512].rearrange("h d t -> d h t")))
        for k in range(8):
            p.act(sq[:, k, :], X[:, k, :], AF.Square)
        for k in range(8):
            p.mm(ps[0][:, :], ones128[:, :], sq[:, k, :], start=(k == 0), stop=(k == 7))
        p.act(rstd[:, :], ps[0][:, :], AF.Ln, bias=epsc[:, 0:1])
        p.act(rstd[:, :], rstd[:, :], AF.Exp, scale=-0.5)
        for k in range(8):
            p.stt(hT[:, k, :], X[:, k, :], gn_sb[:, k:k + 1], rstd[:, :], ALU.mult, ALU.mult)
        for h in range(4):
            pz = ps[1]
            for k in range(8):
                p.mm(pz[0:64, :], wzg_sb[k][:, h * 64:(h + 1) * 64], hT[:, k, :], start=(k == 0), stop=(k == 7))
            p.act(sz[:, :], pz[0:64, :], AF.Silu)
            p.tt(osum[:, :], of[:, h, :], ob[:, h, :], ALU.add, eng="pool")
            p.act(sq2[:, :], osum[:, :], AF.Square)
            p.mm(ps[2][0:64, :], ones64[:, :], sq2[:, :])
            p.act(r2[:, :], ps[2][0:64, :], AF.Ln, bias=epsc[0:64, 0:1])
            p.act(r2[:, :], r2[:, :], AF.Exp, scale=-0.5)
            p.stt(osum[:, :], osum[:, :], gdng_sb[:, 0:1], r2[:, :], ALU.mult, ALU.mult)
            p.tt(yg[:, h, :], osum[:, :], sz[:, :], ALU.mult)
        for ot in range(8):
            n0 = ot * 128
            pbr = [ps[3], ps[4], ps[5]]
            for h in range(4):
                p.mm(pbr[0][:, :], wb_sb[:, h, n0:n0 + 128], yna[:, h, :], start=(h == 0), stop=(h == 3))
            for h in range(8):
                p.mm(pbr[1][:, :], wb_sb[:, 4 + h, n0:n0 + 128], yswa[:, h, :], start=(h == 0), stop=(h == 7))
            for h in range(4):
                p.mm(pbr[2][:, :], wb_sb[:, 12 + h, n0:n0 + 128], yg[:, h, :], start=(h == 0), stop=(h == 3))
            for br in range(3):
                pg = ps[6 + br % 2]
                gcol = 256 + br * D + n0
                for k in range(8):
                    p.mm(pg[:, :], wzg_sb[k][:, gcol:gcol + 128], hT[:, k, :], start=(k == 0), stop=(k == 7))
                S = sg[br % 2]
                p.act(S[:, :], pg[:, :], AF.Sigmoid)
                if br == 0:
                    p.tt(macc[:, :], S[:, :], pbr[0][:, :], ALU.mult)
                elif br == 1:
                    p.tt(mtmp[:, :], S[:, :], pbr[1][:, :], ALU.mult)
                    p.tt(macc[:, :], macc[:, :], mtmp[:, :], ALU.add, eng="pool")
                else:
                    p.tt(mtmp[:, :], S[:, :], pbr[2][:, :], ALU.mult)
                    p.tt(merged[:, ot, :], macc[:, :], mtmp[:, :], ALU.add, eng="pool")
        for ot in range(8):
            po = ps[1 + ot % 2]
            for k in range(8):
                p.mm(po[:, :], wout_sb[k][:, ot * 128:(ot + 1) * 128], merged[:, k, :], start=(k == 0), stop=(k == 7))
            O = xo[ot % 2]
            p.tt(O[:, :], po[:, :], X[:, ot, :], ALU.add)
            p.dma("sp", x1T[ot * 128:(ot + 1) * 128, c0:c0 + 512], O[:, :])
    return p.finish()


NB = 410
NBLK = 5
TOK = 2048
D = 1024
DFF = 2816
NCT = 44


def build_k4():
    p = Prog()
    xT = p.dram("xT", [D, TOK + 2], F32, "ExternalInput")
    gn = p.dram("gn", [128, 8], F32, "ExternalInput")
    wup = p.dram("wup", [D, 2 * DFF], F32, "ExternalInput")
    wdn = p.dram("wdn", [DFF, D], F32, "ExternalInput")
    cw = p.dram("cw", [128, NCT * 3], F32, "ExternalInput")
    cb = p.dram("cb", [128, NCT], F32, "ExternalInput")
    oT = p.dram("oT", [D, TOK], F32, "ExternalOutput")

    wup_sb = [p.sbuf([128, 2 * DFF], BF16, "wup%d" % k) for k in range(8)]
    gn_sb = p.sbuf([128, 8], F32, "gn_sb")
    cw_sb = p.sbuf([128, NCT * 3], F32, "cw_sb")
    cb_sb = p.sbuf([128, NCT], F32, "cb_sb")
    ones = p.sbuf([128, 128], BF16, "ones")
    prev2 = p.sbuf([128, NCT * 2], F32, "prev2")
    xb = [p.sbuf([128, 8, NB + 1], F32, "xb%d" % i) for i in range(2)]
    sq = p.sbuf([128, 8, NB], BF16, "sq")
    rstd = p.sbuf([128, NB], F32, "rstd")
    hT = p.sbuf([128, 8, NB], BF16, "hT")
    ub = [p.sbuf([128, NB + 2], F32, "ub%d" % i) for i in range(4)]
    ya = [p.sbuf([128, NB], F32, "ya%d" % i) for i in range(2)]
    yb = [p.sbuf([128, NB], F32, "yb%d" % i) for i in range(2)]
    mT = p.sbuf([128, 22, NB], BF16, "mT")
    wd_sb = [p.sbuf([128, 22, 128], BF16, "wd%d" % i) for i in range(3)]
    ob = [p.sbuf([128, NB], F32, "ob%d" % i) for i in range(2)]
    ps_u = [p.psum([128, 512], F32, "psu%d" % i) for i in range(4)]
    ps_o = [p.psum([128, 512], F32, "pso%d" % i) for i in range(2)]
    ps_s = p.psum([128, 512], F32, "pss")

    p.memset(ones[:, :], 1.0 / D)
    p.memset(prev2[:, :], 0.0)
    epsc = p.sbuf([128, 1], F32, "epsc")
    p.memset(epsc[:, :], 1e-6)
    p.dma("sp", gn_sb[:, :], gn[:, :])
    p.dma("sp", cw_sb[:, :], cw[:, :])
    p.dma("sp", cb_sb[:, :], cb[:, :])
    for k in range(8):
        p.dma("pool", wup_sb[k][:, :], wup[k * 128:(k + 1) * 128, :], max_dma_last_dim=8192)

    wdi = 0
    for b in range(NBLK):
        c0 = NB * b
        start = max(c0 - 1, 0)
        off = c0 - start
        ncols = c0 + NB - start
        X = xb[b % 2]
        p.dma("sp", X[:, :, 0:ncols], xT.v(xT.t[:, start:c0 + NB].rearrange("(k p) t -> p k t", p=128)))
        for k in range(8):
            p.act(sq[:, k, :], X[:, k, off:off + NB], AF.Square)
        for k in range(8):
            p.mm(ps_s[:, 0:NB], ones[:, :], sq[:, k, :], start=(k == 0), stop=(k == 7))
        p.act(rstd[:, :], ps_s[:, 0:NB], AF.Ln, bias=epsc[:, 0:1])
        p.act(rstd[:, :], rstd[:, :], AF.Exp, scale=-0.5)
        for k in range(8):
            p.stt(hT[:, k, :], X[:, k, off:off + NB], gn_sb[:, k:k + 1], rstd[:, :], ALU.mult, ALU.mult)
        for c in range(22):
            ys = []
            for half in range(2):
                ct = c + 22 * half
                pu = ps_u[(2 * c + half) % 4]
                U = ub[(2 * c + half) % 4]
                for k in range(8):
                    p.mm(pu[:, 0:NB], wup_sb[k][:, ct * 128:(ct + 1) * 128], hT[:, k, :], start=(k == 0), stop=(k == 7))
                p.copy(U[:, 0:2], prev2[:, 2 * ct:2 * ct + 2], eng="pool")
                p.act(U[:, 2:NB + 2], pu[:, 0:NB], AF.Copy)
                p.copy(prev2[:, 2 * ct:2 * ct + 2], U[:, NB:NB + 2], eng="pool")
                Y = (ya if half == 0 else yb)[c % 2]
                p.ts(Y[:, :], U[:, 0:NB], cw_sb[:, 3 * ct:3 * ct + 1], ALU.mult, cb_sb[:, ct:ct + 1], ALU.add)
                p.stt(Y[:, :], U[:, 1:NB + 1], cw_sb[:, 3 * ct + 1:3 * ct + 2], Y[:, :], ALU.mult, ALU.add)
                p.stt(Y[:, :], U[:, 2:NB + 2], cw_sb[:, 3 * ct + 2:3 * ct + 3], Y[:, :], ALU.mult, ALU.add)
                ys.append(Y)
            p.act(ys[0][:, :], ys[0][:, :], AF.Silu)
            p.tt(mT[:, c, :], ys[0][:, :], ys[1][:, :], ALU.mult, eng="pool")
        i0 = 2 if b == 0 else 0
        for o in range(8):
            W = wd_sb[wdi % 3]
            wdi += 1
            p.dma("pool", W[:, :, :], wdn.v(wdn.t[:, o * 128:(o + 1) * 128].rearrange("(c p) n -> p c n", p=128)))
            po = ps_o[o % 2]
            for c in range(22):
                p.mm(po[:, 0:NB], W[:, c, :], mT[:, c, :], start=(c == 0), stop=(c == 21))
            O = ob[o % 2]
            xi = c0 - 1 + i0 - start
            p.tt(O[:, i0:NB], po[:, i0:NB], X[:, o, xi:xi + NB - i0], ALU.add)
            oc = c0 - 2 + i0
            p.dma("sp", oT[o * 128:(o + 1) * 128, oc:oc + NB - i0], O[:, i0:NB])
    return p.finish()


L = 16384
TOK = 2048
HALO = 256
THETA = 10000.0


def f32(a):
    return np.ascontiguousarray(a, dtype=np.float32)


def k1_consts():
    R = np.zeros((64, 64), np.float32)
    for i in range(32):
        R[i, i + 32] = -1.0
        R[i + 32, i] = 1.0
    k = np.arange(128)[:, None]
    q = np.arange(128)[None, :]
    mprev = np.tile((k >= q).astype(np.float32), (1, 4))
    mnext = np.tile((k <= q).astype(np.float32), (1, 4))
    qc = np.arange(64)
    cs = np.clip(qc - 8, 0, 48)
    kc = np.arange(64)
    col_in = (kc[None, :] >= cs[:, None]) & (kc[None, :] < cs[:, None] + 16)
    cmask = np.broadcast_to(col_in.T[:, None, :], (64, 4, 64)).astype(np.float32).reshape(64, 256)
    return dict(rotT=f32(R.T), mprev=f32(mprev), mnext=f32(mnext), cmask=f32(cmask))


def rope_tabs(pos):
    inv = (1.0 / (THETA ** (np.arange(0, 64, 2, dtype=np.float32) / np.float32(64)))).astype(np.float32)
    ang = pos.astype(np.float32)[:, None] * inv[None, :]
    c, s = np.cos(ang).astype(np.float32), np.sin(ang).astype(np.float32)
    return f32(np.concatenate([c, c], 1).T), f32(np.concatenate([s, s], 1).T)


def na_tables(rpb, c):
    kc = np.arange(64)[:, None]
    qc = np.arange(64)[None, :]
    dc = np.clip(kc - qc + 15, 0, 30)
    g = rpb[:, :, dc]
    g = np.transpose(g, (2, 1, 0, 3))
    shared = g[:, 3:11].reshape(64, 8 * 256)
    edge = np.full((64, 7, 12, 4, 64), -30000.0, np.float32)
    for e in range(7):
        qr = e if e < 4 else 29 + (e - 4)
        r = 32 * c + qr
        rs = min(max(r - 4, 0), 248)
        base = qr if e < 4 else 28
        for s in range(12):
            kr = base + s
            Rg = 32 * c - 4 + kr
            if rs <= Rg < rs + 8 and 0 <= Rg < 256:
                edge[:, e, s] = g[:, Rg - r + 7]
    return f32(shared), f32(edge.reshape(64, 7 * 12 * 256))


def prep_k1(inp, l, c, consts, xpad):
    s = c * TOK
    m = dict(consts)
    m["xT"] = f32(xpad[s:s + TOK + 2 * HALO].T)
    m["gn"] = f32(inp["attn_norm"][l].reshape(8, 128).T)
    w_in = inp["w_in"][l]
    m["w"] = f32(np.concatenate([w_in[:, 0:2304], w_in[:, 2560:2576]], 1))
    m["qkg"] = f32(inp["qk_norm"][l].T)
    pos = np.arange(s - HALO, s + TOK + HALO)
    m["cosT"], m["sinT"] = rope_tabs(pos)
    m["ebt"], m["ebe"] = na_tables(inp["na_rpb"][l], c)
    m["sinkb"] = f32(np.broadcast_to(inp["swa_sink"][l][None, :], (64, 8)))
    m["gcw"] = f32(inp["gdn_conv_w"][l].T.reshape(12, 64, 3).transpose(1, 0, 2).reshape(64, 36))
    m["alog"] = f32(inp["gdn_a_log"][l].reshape(8, 1))
    m["dtb"] = f32(inp["gdn_dt_bias"][l].reshape(8, 1))
    m["mprev0"] = consts["mprev"] if c > 0 else np.zeros_like(consts["mprev"])
    m["mnextL"] = consts["mnext"] if c < 7 else np.zeros_like(consts["mnext"])
    return m


def k2_consts():
    i = np.arange(128)[:, None]
    j = np.arange(128)[None, :]
    same = (i // 64) == (j // 64)
    low_incl = same & (j <= i)
    low_strict = same & (j < i)
    trisel = np.zeros((128, 130), np.float32)
    trisel[:, :128] = low_incl.T
    trisel[:64, 128] = 1.0
    trisel[64:, 129] = 1.0
    MB = np.where(low_incl, 0.0, -30000.0)
    t4 = lambda a: f32(np.tile(a, (1, 4)))
    return dict(ident=t4(np.eye(128)), trisel=f32(trisel), bones=f32(same), MB=t4(MB), MBT=t4(MB.T),
                SM=t4(low_strict), SMT=t4(low_strict.T))


def prep_k2(consts, qT, kT, vT, beta, g, flip):
    if flip:
        qT, kT, vT, beta, g = qT[:, ::-1], kT[:, ::-1], vT[:, ::-1], beta[::-1], g[::-1]
    m = dict(consts)
    m["qT"] = f32(qT)
    m["kT"] = f32(kT)
    m["ktm"] = f32(kT.T)
    m["vtm"] = f32(vT.T)
    m["brow"] = f32(beta.reshape(1, -1))
    m["gcp"] = f32(g.reshape(128, 128).T)
    m["bcp"] = f32(beta.reshape(128, 128).T)
    return m


def prep_k3_w(inp, l):
    w_in = inp["w_in"][l]
    m = {}
    m["gn"] = f32(inp["attn_norm"][l].reshape(8, 128).T)
    m["wzg"] = f32(np.concatenate([w_in[:, 2304:2560], w_in[:, 2576:5648]], 1))
    m["gdng"] = f32(inp["gdn_norm"][l].reshape(64, 1))
    wb = np.concatenate([inp["w_branch_na"][l].reshape(4, 64, 1024), inp["w_branch_swa"][l].reshape(8, 64, 1024),
                         inp["w_branch_gdn"][l].reshape(4, 64, 1024)], 0)
    m["wb"] = f32(wb.transpose(1, 0, 2).reshape(64, 16 * 1024))
    m["wout"] = f32(inp["w_out"][l])
    return m


def prep_k4_w(inp, l):
    return dict(gn=f32(inp["ffn_norm"][l].reshape(8, 128).T),
                cw=f32(inp["ffn_conv_w"][l].T.reshape(44, 128, 3).transpose(1, 0, 2).reshape(128, 132)),
                cb=f32(inp["ffn_conv_b"][l].reshape(44, 128).T),
                wup=f32(inp["w_up"][l]), wdn=f32(inp["w_down"][l]))


from concourse.bass_utils import run_bass_kernel_spmd

NCORES = 8


def _run(nc, maps):
    res = run_bass_kernel_spmd(nc, maps, core_ids=list(range(NCORES)))
    return res.results


def kernel(**inp):
    inp = {k: np.asarray(v) for k, v in inp.items()}
    x = f32(inp["x"][0])
    Lq = x.shape[0]
    c1 = k1_consts()
    c2 = k2_consts()
    T = 2048
    for l in range(4):
        z = np.zeros((256, 1024), np.float32)
        xpad = np.concatenate([z, x, z], 0)
        r1 = _run(build_k1(), [prep_k1(inp, l, c, c1, xpad) for c in range(NCORES)])
        gq = np.concatenate([r["gqkv"] for r in r1], 2)
        bT = np.concatenate([r["betaT"] for r in r1], 1)
        gT = np.concatenate([r["gT"] for r in r1], 1)
        maps = []
        for c in range(NCORES):
            hh = c % 4
            maps.append(prep_k2(c2, gq[hh], gq[4 + hh], gq[8 + hh], bT[c], gT[c], flip=(c >= 4)))
        r2 = _run(build_k2(), maps)
        o = [r2[c]["o"] if c < 4 else r2[c]["o"][::-1] for c in range(NCORES)]
        wm = prep_k3_w(inp, l)
        maps = []
        for c in range(NCORES):
            s, e = c * T, (c + 1) * T
            m = dict(wm)
            m["xT"] = f32(x[s:e].T)
            m["ynaT"] = r1[c]["ynaT"]
            m["yswaT"] = r1[c]["yswaT"]
            m["ofT"] = f32(np.stack([o[h][s:e].T for h in range(4)], 0))
            m["obT"] = f32(np.stack([o[4 + h][s:e].T for h in range(4)], 0))
            maps.append(m)
        r3 = _run(build_k3(), maps)
        x1 = np.concatenate([r["x1T"].T for r in r3], 0)
        z1 = np.zeros((1, 1024), np.float32)
        x1pad = np.concatenate([z1, x1, z1], 0)
        w4 = prep_k4_w(inp, l)
        maps = []
        for c in range(NCORES):
            m = dict(w4)
            m["xT"] = f32(x1pad[c * T:c * T + T + 2].T)
            maps.append(m)
        r4 = _run(build_k4(), maps)
        x = f32(np.concatenate([r["oT"].T for r in r4], 0))
    return x[None].astype(np.float32)
```

```python
from contextlib import ExitStack
import numpy as np
import concourse.bass as bass
import concourse.mybir as mybir

F32 = mybir.dt.float32
BF16 = mybir.dt.bfloat16
AF = mybir.ActivationFunctionType
ALU = mybir.AluOpType
NDMA = 24


class V:
    def __init__(self, buf, ap):
        self.buf = buf
        self.ap = ap


class St:
    def __init__(self):
        self.w = None
        self.r = {}
        self.excl = False


class Buf:
    def __init__(self, t, name, st=None):
        self.t = t
        self.name = name
        self.st = st if st is not None else St()

    def sub(self, ap, name):
        return Buf(ap, name, self.st)

    def __getitem__(self, idx):
        return V(self, self.t[idx])

    def v(self, ap):
        return V(self, ap)


class Prog:
    def __init__(self):
        self.nc = bass.Bass("TRN2", target_bir_lowering=False)
        self.es = ExitStack()
        nc = self.nc
        self.engs = {"pe": nc.tensor, "act": nc.scalar, "dve": nc.vector, "pool": nc.gpsimd, "sp": nc.sync}
        self.semh = {}
        for e in self.engs:
            self.semh[e] = self.es.enter_context(nc.semaphore("s_" + e))
        for i in range(NDMA):
            self.semh[("dma", i)] = self.es.enter_context(nc.semaphore("s_dma%d" % i))
        self.cnt = {e: 0 for e in self.engs}
        self.dcnt = [0] * NDMA
        self.drr = 0
        self.seen = {e: {} for e in self.engs}
        self.ops = {e: [] for e in self.engs}
        self.nbuf = 0

    def sbuf(self, shape, dt, name=None):
        self.nbuf += 1
        name = (name or "sb") + "_%d" % self.nbuf
        t = self.es.enter_context(self.nc.sbuf_tensor(name, list(shape), dt))
        return Buf(t, name)

    def psum(self, shape, dt=F32, name=None):
        self.nbuf += 1
        name = (name or "ps") + "_%d" % self.nbuf
        t = self.es.enter_context(self.nc.psum_tensor(name, list(shape), dt))
        b = Buf(t, name)
        b.st.excl = True
        return b

    def dram(self, name, shape, dt, kind):
        t = self.nc.dram_tensor(name, list(shape), dt, kind=kind)
        return Buf(t.ap(), name)

    def _waits(self, e, reads, writes):
        waits = {}
        seen = self.seen[e]

        def need(k, v):
            if k is None or v <= 0:
                return
            if e == "pe" and k == "pe":
                return
            if seen.get(k, 0) >= v:
                return
            if waits.get(k, 0) < v:
                waits[k] = v

        for b in reads:
            if b.w is not None:
                need(*b.w)
            if b.excl:
                for k, v in b.r.items():
                    if k != e:
                        need(k, v)
        for b in writes:
            if b.w is not None:
                need(*b.w)
            for k, v in b.r.items():
                need(k, v)
        for k, v in waits.items():
            seen[k] = v
        return list(waits.items())

    def op(self, e, fn, reads, writes):
        reads = [(x.buf if isinstance(x, V) else x).st for x in reads]
        writes = [(x.buf if isinstance(x, V) else x).st for x in writes]
        wl = self._waits(e, reads, writes)
        self.cnt[e] += 1
        tok = (e, self.cnt[e])
        semh = self.semh

        def emit(eng):
            for k, v in wl:
                eng.wait_ge(semh[k], v)
            fn(eng).then_inc(semh[e], 1)

        self.ops[e].append(emit)
        for b in writes:
            b.w = tok
            b.r = {}
        for b in reads:
            if b not in writes:
                b.r[e] = tok[1]

    def dma(self, q, out, in_, **kw):
        i = self.drr
        self.drr = (i + 1) % NDMA
        key = ("dma", i)
        prev = self.dcnt[i]
        self.dcnt[i] += 16
        tok = (key, self.dcnt[i])
        wl = self._waits(q, [in_.buf.st], [out.buf.st])
        if prev > 0 and self.seen[q].get(key, 0) < prev:
            wl.append((key, prev))
            self.seen[q][key] = prev
        semh = self.semh
        oa, ia = out.ap, in_.ap

        def emit(eng):
            for k, v in wl:
                eng.wait_ge(semh[k], v)
            eng.dma_start(out=oa, in_=ia, **kw).then_inc(semh[key], 16)

        self.ops[q].append(emit)
        out.buf.st.w = tok
        out.buf.st.r = {}
        if in_.buf.st is not out.buf.st:
            in_.buf.st.r[key] = tok[1]

    def finish(self):
        wl = [(e, c) for e, c in self.cnt.items() if c > 0]
        wl += [(("dma", i), c) for i, c in enumerate(self.dcnt) if c > 0]
        semh = self.semh

        def emit(eng):
            for k, v in wl:
                eng.wait_ge(semh[k], v)

        self.ops["sp"].append(emit)
        ops = self.ops
        with self.nc.Block() as block:
            @block.tensor
            def _(e):
                for f in ops["pe"]:
                    f(e)

            @block.scalar
            def _(e):
                for f in ops["act"]:
                    f(e)

            @block.vector
            def _(e):
                for f in ops["dve"]:
                    f(e)

            @block.gpsimd
            def _(e):
                for f in ops["pool"]:
                    f(e)

            @block.sync
            def _(e):
                for f in ops["sp"]:
                    f(e)
        self.es.close()
        return self.nc

    def mm(self, out, lhsT, rhs, start=True, stop=True):
        self.op("pe", lambda e: e.matmul(out.ap, lhsT.ap, rhs.ap, start=start, stop=stop),
                [lhsT, rhs] + ([] if start else [out]), [out])

    def act(self, out, in_, func, bias=None, scale=None, eng="act"):
        kw = {}
        rd = [in_]
        if bias is not None:
            if isinstance(bias, V):
                kw["bias"] = bias.ap
                rd.append(bias)
            else:
                kw["bias"] = bias
        if scale is not None:
            if isinstance(scale, V):
                kw["scale"] = scale.ap
                rd.append(scale)
            else:
                kw["scale"] = scale
        self.op(eng, lambda e: e.activation(out.ap, in_.ap, func, **kw), rd, [out])

    def tt(self, out, in0, in1, op, eng="dve"):
        self.op(eng, lambda e: e.tensor_tensor(out.ap, in0.ap, in1.ap, op), [in0, in1], [out])

    def ts(self, out, in0, s1, op0, s2=None, op1=None, eng="dve"):
        rd = [in0]
        a1 = s1.ap if isinstance(s1, V) else s1
        a2 = s2.ap if isinstance(s2, V) else s2
        if isinstance(s1, V):
            rd.append(s1)
        if isinstance(s2, V):
            rd.append(s2)
        if op1 is None:
            self.op(eng, lambda e: e.tensor_scalar(out.ap, in0.ap, a1, None, op0), rd, [out])
        else:
            self.op(eng, lambda e: e.tensor_scalar(out.ap, in0.ap, a1, a2, op0, op1), rd, [out])

    def stt(self, out, in0, s, in1, op0, op1, eng="dve"):
        rd = [in0, in1]
        a = s.ap if isinstance(s, V) else s
        if isinstance(s, V):
            rd.append(s)
        self.op(eng, lambda e: e.scalar_tensor_tensor(out.ap, in0.ap, a, in1.ap, op0, op1), rd, [out])

    def copy(self, out, in_, eng="dve"):
        if eng == "act":
            self.op("act", lambda e: e.copy(out.ap, in_.ap), [in_], [out])
        else:
            self.op(eng, lambda e: e.tensor_copy(out.ap, in_.ap), [in_], [out])

    def recip(self, out, in_, eng="dve"):
        self.op(eng, lambda e: e.reciprocal(out.ap, in_.ap), [in_], [out])

    def memset(self, out, val, eng="pool"):
        self.op(eng, lambda e: e.memset(out.ap, val), [], [out])


def _barrier(self):
    wl_all = [(e, c) for e, c in self.cnt.items() if c > 0]
    wl_all += [(("dma", i), c) for i, c in enumerate(self.dcnt) if c > 0]
    semh = self.semh
    for e in self.engs:
        wl = [(k, v) for k, v in wl_all if self.seen[e].get(k, 0) < v and not (k == e)]
        for k, v in wl:
            self.seen[e][k] = v

        def emit(eng, wl=wl):
            for k, v in wl:
                eng.wait_ge(semh[k], v)

        self.ops[e].append(emit)


Prog.barrier = _barrier

from contextlib import contextmanager


@contextmanager
def _scope(self):
    old = self.es
    self.es = ExitStack()
    try:
        yield
    finally:
        self.barrier()
        self.es.close()
        self.es = old


Prog.scope = _scope


TOK = 2048
HALO = 256
NL = TOK + 2 * HALO
D = 1024
EPS = 1e-6


def qknorm(p, C, ps_in, N, gain, out, rope=None):
    p.act(C["sq"][0:64, 0:N], ps_in, AF.Square)
    p.mm(C["psm"][0:64, 0:N], C["ones64"][:, :], C["sq"][0:64, 0:N])
    p.act(C["r"][0:64, 0:N], C["psm"][0:64, 0:N], AF.Ln, bias=C["eps"][0:64, 0:1])
    p.act(C["r"][0:64, 0:N], C["r"][0:64, 0:N], AF.Exp, scale=-0.5)
    if rope is None:
        p.stt(out, ps_in, gain, C["r"][0:64, 0:N], ALU.mult, ALU.mult)
        return
    cosv, sinv = rope
    p.stt(C["qn"][0:64, 0:N], ps_in, gain, C["r"][0:64, 0:N], ALU.mult, ALU.mult)
    p.mm(C["psr"][0:64, 0:N], C["rotT"][:, :], C["qn"][0:64, 0:N])
    p.tt(C["t1"][0:64, 0:N], C["qn"][0:64, 0:N], cosv, ALU.mult, eng="pool")
    p.tt(C["t2"][0:64, 0:N], C["psr"][0:64, 0:N], sinv, ALU.mult)
    p.tt(out, C["t1"][0:64, 0:N], C["t2"][0:64, 0:N], ALU.add, eng="pool")


def build_k1():
    p = Prog()
    IN, OUT = "ExternalInput", "ExternalOutput"
    xT = p.dram("xT", [D, NL], F32, IN)
    gn = p.dram("gn", [128, 8], F32, IN)
    w = p.dram("w", [D, 2320], F32, IN)
    qkg = p.dram("qkg", [64, 4], F32, IN)
    cosT = p.dram("cosT", [64, NL], F32, IN)
    sinT = p.dram("sinT", [64, NL], F32, IN)
    ebt = p.dram("ebt", [64, 8 * 256], F32, IN)
    ebe = p.dram("ebe", [64, 7 * 12 * 256], F32, IN)
    cmask = p.dram("cmask", [64, 256], F32, IN)
    sinkb = p.dram("sinkb", [64, 8], F32, IN)
    gcw = p.dram("gcw", [64, 36], F32, IN)
    alog = p.dram("alog", [8, 1], F32, IN)
    dtb = p.dram("dtb", [8, 1], F32, IN)
    rotT_d = p.dram("rotT", [64, 64], F32, IN)
    mprev = p.dram("mprev", [128, 512], F32, IN)
    mnext = p.dram("mnext", [128, 512], F32, IN)
    mprev0 = p.dram("mprev0", [128, 512], F32, IN)
    mnextL = p.dram("mnextL", [128, 512], F32, IN)
    ynaT = p.dram("ynaT", [4, 64, TOK], F32, OUT)
    yswaT = p.dram("yswaT", [8, 64, TOK], F32, OUT)
    gqkv = p.dram("gqkv", [12, 64, TOK], F32, OUT)
    betaT = p.dram("betaT", [8, TOK], F32, OUT)
    gT = p.dram("gT", [8, TOK], F32, OUT)

    ps = [p.psum([128, 512], F32, "ps%d" % i) for i in range(8)]
    hT = p.sbuf([128, 8, NL], BF16, "hT")
    gn_sb = p.sbuf([128, 8], F32, "gn_sb")
    qkg_sb = p.sbuf([64, 4], F32, "qkg_sb")
    ones128 = p.sbuf([128, 128], BF16, "ones128")
    ones64 = p.sbuf([64, 64], BF16, "ones64")
    one64 = p.sbuf([128, 64], BF16, "one64")
    rot_f = p.sbuf([64, 64], F32, "rot_f")
    rotT = p.sbuf([64, 64], BF16, "rotTb")
    p.memset(ones128[:, :], 1.0 / D)
    p.memset(ones64[:, :], 1.0 / 64)
    p.memset(one64[:, :], 1.0)
    epsc = p.sbuf([128, 1], F32, "epsc")
    p.memset(epsc[:, :], EPS)
    p.dma("sp", gn_sb[:, :], gn[:, :])
    p.dma("sp", qkg_sb[:, :], qkg[:, :])
    p.dma("sp", rot_f[:, :], rotT_d[:, :])
    p.copy(rotT[:, :], rot_f[:, :])

    with p.scope():
        xb = [p.sbuf([128, 8, 512], F32, "xb%d" % i) for i in range(2)]
        sq = p.sbuf([128, 8, 512], BF16, "sq1")
        rstd = p.sbuf([128, 512], F32, "rstd1")
        for b in range(NL // 512):
            X = xb[b % 2]
            c0 = 512 * b
            p.dma("sp", X[:, :, :], xT.v(xT.t[:, c0:c0 + 512].rearrange("(k p) t -> p k t", p=128)))
            for k in range(8):
                p.act(sq[:, k, :], X[:, k, :], AF.Square)
            for k in range(8):
                p.mm(ps[0][:, :], ones128[:, :], sq[:, k, :], start=(k == 0), stop=(k == 7))
            p.act(rstd[:, :], ps[0][:, :], AF.Ln, bias=epsc[:, 0:1])
            p.act(rstd[:, :], rstd[:, :], AF.Exp, scale=-0.5)
            for k in range(8):
                p.stt(hT[:, k, c0:c0 + 512], X[:, k, :], gn_sb[:, k:k + 1], rstd[:, :], ALU.mult, ALU.mult)

    with p.scope():
        NB = 410
        wg = [p.sbuf([128, 784], BF16, "wg%d" % k) for k in range(8)]
        for k in range(8):
            p.dma("pool", wg[k][:, :], w[k * 128:(k + 1) * 128, 1536:2320])
        gcw_sb = p.sbuf([64, 36], F32, "gcw_sb")
        p.dma("sp", gcw_sb[:, :], gcw[:, :])
        prev2 = p.sbuf([64, 24], F32, "prev2")
        p.memset(prev2[:, :], 0.0)
        ub = [p.sbuf([64, NB + 2], F32, "ub%d" % i) for i in range(3)]
        Y = [p.sbuf([64, NB], F32, "Y%d" % i) for i in range(3)]
        sqg = p.sbuf([64, NB], BF16, "sqg")
        rg = p.sbuf([64, NB], F32, "rg")
        it = 0
        for b in range(5):
            c0 = HALO - 1 + NB * b
            i0 = 2 if b == 0 else 0
            for t in range(12):
                pu = ps[it % 2]
                U = ub[it % 3]
                y = Y[it % 3]
                it += 1
                for k in range(8):
                    p.mm(pu[0:64, 0:NB], wg[k][:, t * 64:(t + 1) * 64], hT[:, k, c0:c0 + NB], start=(k == 0), stop=(k == 7))
                p.copy(U[:, 0:2], prev2[:, 2 * t:2 * t + 2], eng="pool")
                p.act(U[:, 2:NB + 2], pu[0:64, 0:NB], AF.Copy)
                p.copy(prev2[:, 2 * t:2 * t + 2], U[:, NB:NB + 2], eng="pool")
                p.ts(y[:, :], U[:, 0:NB], gcw_sb[:, 3 * t:3 * t + 1], ALU.mult)
                p.stt(y[:, :], U[:, 1:NB + 1], gcw_sb[:, 3 * t + 1:3 * t + 2], y[:, :], ALU.mult, ALU.add)
                p.stt(y[:, :], U[:, 2:NB + 2], gcw_sb[:, 3 * t + 2:3 * t + 3], y[:, :], ALU.mult, ALU.add)
                p.act(y[:, :], y[:, :], AF.Silu)
                if t < 8:
                    p.act(sqg[:, :], y[:, :], AF.Square)
                    p.mm(ps[2][0:64, 0:NB], one64[0:64, :], sqg[:, :])
                    p.act(rg[:, :], ps[2][0:64, 0:NB], AF.Ln, bias=epsc[0:64, 0:1])
                    p.act(rg[:, :], rg[:, :], AF.Exp, scale=-0.5)
                    if t < 4:
                        p.stt(y[:, :], y[:, :], 0.125, rg[:, :], ALU.mult, ALU.mult)
                    else:
                        p.tt(y[:, :], y[:, :], rg[:, :], ALU.mult)
                oc = NB * b - 2 + i0
                p.dma("sp", gqkv.v(gqkv.t[t, :, oc:oc + NB - i0]), y[:, i0:NB])
        alog_sb = p.sbuf([8, 1], F32, "alog_sb")
        dtb_sb = p.sbuf([8, 1], F32, "dtb_sb")
        negA = p.sbuf([8, 1], F32, "negA")
        p.dma("sp", alog_sb[:, :], alog[:, :])
        p.dma("sp", dtb_sb[:, :], dtb[:, :])
        p.act(negA[:, :], alog_sb[:, :], AF.Exp)
        p.ts(negA[:, :], negA[:, :], -1.0, ALU.mult)
        bo = [p.sbuf([8, 512], F32, "bo%d" % i) for i in range(2)]
        go = [p.sbuf([8, 512], F32, "go%d" % i) for i in range(2)]
        for b in range(4):
            c0 = HALO + 512 * b
            for k in range(8):
                p.mm(ps[3][0:8, :], wg[k][:, 768:776], hT[:, k, c0:c0 + 512], start=(k == 0), stop=(k == 7))
            for k in range(8):
                p.mm(ps[4][0:8, :], wg[k][:, 776:784], hT[:, k, c0:c0 + 512], start=(k == 0), stop=(k == 7))
            B_, G_ = bo[b % 2], go[b % 2]
            p.act(B_[:, :], ps[3][0:8, :], AF.Sigmoid)
            p.act(G_[:, :], ps[4][0:8, :], AF.Exp, bias=dtb_sb[:, 0:1])
            p.act(G_[:, :], G_[:, :], AF.Ln, bias=1.0)
            p.ts(G_[:, :], G_[:, :], negA[:, 0:1], ALU.mult)
            p.dma("sp", betaT[:, 512 * b:512 * b + 512], B_[:, :])
            p.dma("sp", gT[:, 512 * b:512 * b + 512], G_[:, :])

    def mk_scratch():
        C = dict(ones64=ones64, rotT=rotT, eps=epsc)
        C["sq"] = p.sbuf([64, 512], BF16, "qsq")
        C["r"] = p.sbuf([64, 512], F32, "qr")
        C["qn"] = p.sbuf([64, 512], BF16, "qqn")
        C["t1"] = p.sbuf([64, 512], F32, "qt1")
        C["t2"] = p.sbuf([64, 512], F32, "qt2")
        C["psm"] = ps[6]
        C["psr"] = ps[7]
        return C

    with p.scope():
        C = mk_scratch()
        wn = [p.sbuf([128, 768], BF16, "wn%d" % k) for k in range(8)]
        for k in range(8):
            p.dma("pool", wn[k][:, :], w[k * 128:(k + 1) * 128, 0:768])
        qaT = [p.sbuf([64, TOK], BF16, "qaT%d" % h) for h in range(4)]
        kaT = [p.sbuf([64, NL], BF16, "kaT%d" % h) for h in range(4)]
        va = p.sbuf([64, 40, 256], BF16, "va")
        for h in range(4):
            for nb in range(4):
                c0 = HALO + 512 * nb
                pq = ps[nb % 2]
                for k in range(8):
                    p.mm(pq[0:64, :], wn[k][:, h * 64:(h + 1) * 64], hT[:, k, c0:c0 + 512], start=(k == 0), stop=(k == 7))
                qknorm(p, C, pq[0:64, :], 512, qkg_sb[:, 0:1], qaT[h][:, 512 * nb:512 * nb + 512])
            for nb in range(5):
                c0 = 512 * nb
                pq = ps[nb % 2]
                for k in range(8):
                    p.mm(pq[0:64, :], wn[k][:, 256 + h * 64:256 + (h + 1) * 64], hT[:, k, c0:c0 + 512], start=(k == 0), stop=(k == 7))
                qknorm(p, C, pq[0:64, :], 512, qkg_sb[:, 1:2], kaT[h][:, c0:c0 + 512])
        for rr in range(40):
            pv = ps[2 + rr % 2]
            for k in range(8):
                p.mm(pv[0:64, 0:256], hT[:, k, 64 * rr:64 * rr + 64], wn[k][:, 512:768], start=(k == 0), stop=(k == 7))
            p.copy(va[:, rr, :], pv[0:64, 0:256], eng=("act" if rr % 2 else "dve"))
        cm = p.sbuf([64, 256], F32, "cm")
        p.dma("sp", cm[:, :], cmask[:, :])
        tmpt = p.sbuf([64, 12 * 256], F32, "tmpt")
        EBs = p.sbuf([64, 8, 256], BF16, "EBs")
        EBe = p.sbuf([64, 7 * 12, 256], BF16, "EBe")
        p.dma("sp", tmpt[:, 0:2048], ebt[:, :])
        p.act(tmpt[:, 0:2048], tmpt[:, 0:2048], AF.Exp)
        for j in range(8):
            p.tt(EBs[:, j, :], tmpt[:, 256 * j:256 * j + 256], cm[:, :], ALU.mult)
        for e_ in range(7):
            p.dma("sp", tmpt[:, :], ebe[:, e_ * 3072:(e_ + 1) * 3072])
            p.act(tmpt[:, :], tmpt[:, :], AF.Exp)
            for s_ in range(12):
                p.tt(EBe[:, e_ * 12 + s_, :], tmpt[:, 256 * s_:256 * s_ + 256], cm[:, :], ALU.mult)
        E1 = [p.sbuf([64, 256], F32, "E1_%d" % i) for i in range(3)]
        E2 = [p.sbuf([64, 256], BF16, "E2_%d" % i) for i in range(3)]
        rden = p.sbuf([64, 256], F32, "rden")
        yst = [p.sbuf([64, 256], F32, "yst%d" % i) for i in range(3)]
        ei = 0
        for qr in range(32):
            if qr < 4:
                slots = [(kr, EBe[:, qr * 12 + (kr - qr), :]) for kr in range(qr, 12)]
            elif qr >= 29:
                slots = [(kr, EBe[:, (qr - 29 + 4) * 12 + (kr - 28), :]) for kr in range(28, qr + 8)]
            else:
                slots = [(qr + j, EBs[:, j, :]) for j in range(8)]
            pacc = ps[4 + qr % 2]
            q0 = 64 * qr
            def qk_na(si_):
                kr_ = slots[si_][0]
                pst_ = ps[si_ % 2]
                for h in range(4):
                    p.mm(pst_[0:64, h * 64:(h + 1) * 64], kaT[h][:, 64 * kr_:64 * kr_ + 64], qaT[h][:, q0:q0 + 64])
            qk_na(0)
            for si, (kr, tab) in enumerate(slots):
                pst = ps[si % 2]
                e1, e2 = E1[ei % 3], E2[ei % 3]
                ei += 1
                p.act(e1[:, :], pst[0:64, 0:256], AF.Exp, scale=0.125)
                if si + 1 < len(slots):
                    qk_na(si + 1)
                p.tt(e2[:, :], e1[:, :], tab, ALU.mult, eng=("pool" if ei % 2 else "dve"))
                for h in range(4):
                    p.mm(pacc[0:64, h * 64:(h + 1) * 64], va[:, kr, h * 64:(h + 1) * 64], e2[:, h * 64:(h + 1) * 64],
                         start=(si == 0 and h == 0), stop=(si == len(slots) - 1))
                p.mm(pacc[0:64, 256:512], one64[0:64, :], e2[:, :], start=False, stop=(si == len(slots) - 1))
            ys = yst[qr % 3]
            p.act(rden[:, :], pacc[0:64, 256:512], AF.Ln)
            p.act(rden[:, :], rden[:, :], AF.Exp, scale=-1.0)
            p.tt(ys[:, :], pacc[0:64, 0:256], rden[:, :], ALU.mult)
            p.dma("sp", ynaT.v(ynaT.t[:, :, q0:q0 + 64].rearrange("h d t -> d h t")),
                  ys.v(ys.t[:, :].rearrange("d (h t) -> d h t", h=4)))

    with p.scope():
        C = mk_scratch()
        wsw = [p.sbuf([128, 768], BF16, "wsw%d" % k) for k in range(8)]
        for k in range(8):
            p.dma("pool", wsw[k][:, :], w[k * 128:(k + 1) * 128, 768:1536])
        cs = p.sbuf([64, NL], F32, "cs")
        sn = p.sbuf([64, NL], F32, "sn")
        p.dma("sp", cs[:, :], cosT[:, :])
        p.dma("sp", sn[:, :], sinT[:, :])
        qsT = [p.sbuf([64, TOK], BF16, "qsT%d" % h) for h in range(8)]
        ksT = [p.sbuf([64, NL], BF16, "ksT%d" % g) for g in range(2)]
        vs = p.sbuf([128, 20, 128], BF16, "vs")
        for h in range(8):
            for nb in range(4):
                c0 = HALO + 512 * nb
                pq = ps[nb % 2]
                for k in range(8):
                    p.mm(pq[0:64, :], wsw[k][:, h * 64:(h + 1) * 64], hT[:, k, c0:c0 + 512], start=(k == 0), stop=(k == 7))
                qknorm(p, C, pq[0:64, :], 512, qkg_sb[:, 2:3], qsT[h][:, 512 * nb:512 * nb + 512],
                       rope=(cs[:, c0:c0 + 512], sn[:, c0:c0 + 512]))
        for g in range(2):
            for nb in range(5):
                c0 = 512 * nb
                pq = ps[nb % 2]
                for k in range(8):
                    p.mm(pq[0:64, :], wsw[k][:, 512 + g * 64:512 + (g + 1) * 64], hT[:, k, c0:c0 + 512], start=(k == 0), stop=(k == 7))
                qknorm(p, C, pq[0:64, :], 512, qkg_sb[:, 3:4], ksT[g][:, c0:c0 + 512],
                       rope=(cs[:, c0:c0 + 512], sn[:, c0:c0 + 512]))
        for tb in range(20):
            pv = ps[2 + tb % 2]
            for k in range(8):
                p.mm(pv[:, 0:128], hT[:, k, 128 * tb:128 * tb + 128], wsw[k][:, 640:768], start=(k == 0), stop=(k == 7))
            p.copy(vs[:, tb, :], pv[:, 0:128], eng=("act" if tb % 2 else "dve"))
        mtmp = p.sbuf([128, 512], F32, "mtmp")
        masks = {}
        for nm, src in (("prev", mprev), ("next", mnext), ("prev0", mprev0), ("nextL", mnextL)):
            m = p.sbuf([128, 512], BF16, "m_" + nm)
            p.dma("sp", mtmp[:, :], src[:, :])
            p.copy(m[:, :], mtmp[:, :])
            masks[nm] = m
        es = p.sbuf([64, 8], F32, "es")
        p.dma("sp", es[:, :], sinkb[:, :])
        p.act(es[:, :], es[:, :], AF.Exp)
        one_f = p.sbuf([64, 128], F32, "one_f")
        p.memset(one_f[:, :], 1.0)
        EsT = p.sbuf([64, 8, 128], F32, "EsT")
        for h in range(8):
            p.ts(EsT[:, h, :], one_f[:, :], es[:, h:h + 1], ALU.mult)
        E1 = [p.sbuf([128, 512], BF16, "S1_%d" % i) for i in range(3)]
        rden = p.sbuf([64, 512], F32, "srden")
        yst = [p.sbuf([64, 512], F32, "syst%d" % i) for i in range(3)]
        ei = 0
        for n in range(16):
            q0 = 128 * n
            for g in range(2):
                pnum = ps[2 + (2 * n + g) % 2]
                pden = ps[4 + (2 * n + g) % 2]
                def qk_sw(bi_):
                    k0_ = HALO + 128 * (n + bi_ - 1)
                    pst_ = ps[bi_ % 2]
                    for hh in range(4):
                        p.mm(pst_[:, hh * 128:(hh + 1) * 128], ksT[g][:, k0_:k0_ + 128], qsT[4 * g + hh][:, q0:q0 + 128])
                qk_sw(0)
                for bi, kb in enumerate((-1, 0, 1)):
                    k0 = HALO + 128 * (n + kb)
                    pst = ps[bi % 2]
                    e1 = E1[ei % 3]
                    ei += 1
                    p.act(e1[:, :], pst[:, :], AF.Exp, scale=0.125)
                    if bi < 2:
                        qk_sw(bi + 1)
                    if kb == -1:
                        p.tt(e1[:, :], e1[:, :], masks["prev0" if n == 0 else "prev"][:, :], ALU.mult, eng="pool")
                    elif kb == 1:
                        p.tt(e1[:, :], e1[:, :], masks["nextL" if n == 15 else "next"][:, :], ALU.mult, eng="pool")
                    tb = (k0 // 128)
                    for hh in range(4):
                        p.mm(pnum[0:64, hh * 128:(hh + 1) * 128], vs[:, tb, g * 64:(g + 1) * 64], e1[:, hh * 128:(hh + 1) * 128],
                             start=(bi == 0 and hh == 0), stop=(bi == 2))
                    p.mm(pden[0:64, :], one64[:, :], e1[:, :], start=(bi == 0), stop=(bi == 2))
                ys = yst[(2 * n + g) % 3]
                p.tt(rden[:, :], pden[0:64, :], EsT.v(EsT.t[:, 4 * g:4 * g + 4, :].rearrange("d h t -> d (h t)")), ALU.add)
                p.act(rden[:, :], rden[:, :], AF.Ln)
                p.act(rden[:, :], rden[:, :], AF.Exp, scale=-1.0)
                p.tt(ys[:, :], pnum[0:64, :], rden[:, :], ALU.mult)
                p.dma("sp", yswaT.v(yswaT.t[4 * g:4 * g + 4, :, q0:q0 + 128].rearrange("h d t -> d h t")),
                      ys.v(ys.t[:, :].rearrange("d (h t) -> d h t", h=4)))
    return p.finish()


L = 16384
NCP = 128
NGP = 32


def build_k2(NGP=NGP, maxsteps=None):
    p = Prog()
    IN, OUT = "ExternalInput", "ExternalOutput"
    qT = p.dram("qT", [64, L], F32, IN)
    kT = p.dram("kT", [64, L], F32, IN)
    ktm = p.dram("ktm", [L, 64], F32, IN)
    vtm = p.dram("vtm", [L, 64], F32, IN)
    brow = p.dram("brow", [1, L], F32, IN)
    gcp = p.dram("gcp", [128, NCP], F32, IN)
    bcp = p.dram("bcp", [128, NCP], F32, IN)
    ident_d = p.dram("ident", [128, 512], F32, IN)
    trisel_d = p.dram("trisel", [128, 130], F32, IN)
    bones_d = p.dram("bones", [128, 128], F32, IN)
    MB_d = p.dram("MB", [128, 512], F32, IN)
    MBT_d = p.dram("MBT", [128, 512], F32, IN)
    SM_d = p.dram("SM", [128, 512], F32, IN)
    SMT_d = p.dram("SMT", [128, 512], F32, IN)
    o = p.dram("o", [L, 64], F32, OUT)

    def const(src, shape, nm):
        b = p.sbuf(shape, F32, nm)
        p.dma("sp", b[:, :], src[:, :])
        return b
    ident4 = const(ident_d, [128, 512], "ident4")
    trisel = const(trisel_d, [128, 130], "trisel")
    bones = const(bones_d, [128, 128], "bones")
    MB4 = const(MB_d, [128, 512], "MB4")
    MBT4 = const(MBT_d, [128, 512], "MBT4")
    SM4 = const(SM_d, [128, 512], "SM4")
    SMT4 = const(SMT_d, [128, 512], "SMT4")
    g_all = const(gcp, [128, NCP], "g_all")
    b_all = const(bcp, [128, NCP], "b_all")
    ones_f = p.sbuf([128, 128], F32, "ones_f")
    p.memset(ones_f[:, :], 1.0)

    bk = [p.psum([128, 512], F32, "bank%d" % i) for i in range(8)]
    gc_all = p.sbuf([128, NCP], F32, "gc_all")
    ngc_all = p.sbuf([128, NCP], F32, "ngc_all")
    coefw = p.sbuf([128, NCP], F32, "coefw")
    coeft = p.sbuf([128, NCP], F32, "coeft")
    p.mm(bk[0][:, 0:128], trisel[:, 0:128], g_all[:, :])
    p.copy(gc_all[:, :], bk[0][:, 0:128])
    p.ts(ngc_all[:, :], gc_all[:, :], -1.0, ALU.mult)
    p.act(coefw[:, :], gc_all[:, :], AF.Exp)
    p.tt(coefw[:, :], coefw[:, :], b_all[:, :], ALU.mult)
    p.mm(bk[1][:, 0:128], bones[:, :], g_all[:, :])
    p.tt(coeft[:, :], bk[1][:, 0:128], gc_all[:, :], ALU.subtract)
    p.act(coeft[:, :], coeft[:, :], AF.Exp)

    def mk_set(i):
        s = {}
        def sb(nm, shape, dt=F32):
            s[nm] = p.sbuf(shape, dt, "%s_%d" % (nm, i))
        sb("Q", [64, 512]); sb("K", [64, 512]); sb("KT", [128, 4, 64]); sb("VT", [128, 4, 64]); sb("BR", [1, 512])
        sb("kbT", [64, 512], BF16); sb("kTb", [64, 512], BF16); sb("qTb", [64, 512], BF16)
        sb("gbc", [128, 512]); sb("tmp", [128, 512]); sb("tmp2", [128, 512])
        sb("decay", [128, 512]); sb("decayT", [128, 512]); sb("decay_s", [128, 512]); sb("decayT_s", [128, 512])
        sb("eg", [64, 512]); sb("egS", [64, 8])
        for nm in ("A0", "A1", "B0", "B1", "X0", "X1"):
            sb(nm, [128, 512], BF16)
        sb("rhs_u", [128, 4, 64], BF16); sb("rhs_w", [128, 4, 64], BF16); sb("OS", [64, 8, 64])
        for ci in range(4):
            sb("ktail%d" % ci, [128, 64]); sb("u%d" % ci, [128, 64]); sb("W3%d" % ci, [64, 192])
            sb("QKT%d" % ci, [128, 128]); sb("qdT%d" % ci, [64, 128])
            p.memset(s["W3%d" % ci][:, :], 0.0)
        return s
    sets = [mk_set(0), mk_set(1)]
    vnew = p.sbuf([128, 64], F32, "vnew")
    Sb = [p.sbuf([64, 64], F32, "S%d" % i) for i in range(2)]
    p.memset(Sb[0][:, :], 0.0)
    state = {"si": 0}

    def pre(gp, s):
        T0 = 512 * gp
        Q, K, KT, VT, BR = s["Q"], s["K"], s["KT"], s["VT"], s["BR"]
        p.dma("sp", Q[:, :], qT[:, T0:T0 + 512])
        p.dma("sp", K[:, :], kT[:, T0:T0 + 512])
        p.dma("sp", KT[:, :, :], ktm.v(ktm.t[T0:T0 + 512, :].rearrange("(c p) d -> p c d", p=128)))
        p.dma("sp", VT[:, :, :], vtm.v(vtm.t[T0:T0 + 512, :].rearrange("(c p) d -> p c d", p=128)))
        p.dma("sp", BR[:, :], brow[:, T0:T0 + 512])
        yield
        p.mm(bk[3][0:64, :], ones_f[0:1, 0:64], BR[:, :])
        p.tt(s["kbT"][:, :], K[:, :], bk[3][0:64, :], ALU.mult)
        yield
        p.copy(s["kTb"][:, :], K[:, :], eng="act")
        p.copy(s["qTb"][:, :], Q[:, :], eng="act")
        yield
        for ci in range(4):
            cp = 4 * gp + ci
            p.ts(s["gbc"][:, 128 * ci:128 * ci + 128], ones_f[:, :], g_all[:, cp:cp + 1], ALU.mult, eng="pool")
        yield
        for ci in range(4):
            p.mm(bk[0][:, 128 * ci:128 * ci + 128], s["gbc"][:, 128 * ci:128 * ci + 128], trisel[:, 0:128])
        for ci in range(4):
            p.mm(bk[4][0:64, 2 * ci:2 * ci + 2], s["gbc"][:, 128 * ci:128 * ci + 64], trisel[:, 128:130])
        yield
        p.tt(s["tmp"][:, :], MB4[:, :], bk[0][:, :], ALU.subtract)
        p.tt(s["tmp2"][:, :], MBT4[:, :], bk[0][:, :], ALU.add)
        p.act(s["eg"][:, :], bk[0][0:64, :], AF.Exp)
        p.act(s["egS"][:, :], bk[4][0:64, 0:8], AF.Exp)
        yield
        for ci in range(4):
            cp = 4 * gp + ci
            sl = slice(128 * ci, 128 * ci + 128)
            p.act(s["decay"][:, sl], s["tmp"][:, sl], AF.Exp, bias=gc_all[:, cp:cp + 1])
            p.act(s["decayT"][:, sl], s["tmp2"][:, sl], AF.Exp, bias=ngc_all[:, cp:cp + 1])
            yield
        p.tt(s["decay_s"][:, :], s["decay"][:, :], SM4[:, :], ALU.mult)
        p.tt(s["decayT_s"][:, :], s["decayT"][:, :], SMT4[:, :], ALU.mult)
        yield
        for ci in range(4):
            sl = slice(128 * ci, 128 * ci + 128)
            p.mm(bk[1][:, sl], s["kbT"][:, sl], s["kTb"][:, sl])
        for ci in range(4):
            sl = slice(128 * ci, 128 * ci + 128)
            p.mm(bk[2][:, sl], s["kTb"][:, sl], s["kbT"][:, sl])
        yield
        A, B, X = s["A0"], s["B0"], s["X0"]
        p.tt(B[:, :], bk[1][:, :], s["decay_s"][:, :], ALU.mult)
        p.tt(A[:, :], bk[2][:, :], s["decayT_s"][:, :], ALU.mult)
        yield
        p.tt(X[:, :], ident4[:, :], A[:, :], ALU.subtract)
        yield
        for k in range(1, 6):
            A2, B2, X2 = s["A%d" % (k % 2)], s["B%d" % (k % 2)], s["X%d" % (k % 2)]
            for ci in range(4):
                sl = slice(128 * ci, 128 * ci + 128)
                p.mm(bk[3][:, sl], B[:, sl], A[:, sl])
            yield
            for ci in range(4):
                sl = slice(128 * ci, 128 * ci + 128)
                p.mm(bk[4][:, sl], A[:, sl], B[:, sl])
            yield
            p.copy(A2[:, :], bk[3][:, :], eng="act")
            p.copy(B2[:, :], bk[4][:, :])
            yield
            for ci in range(4):
                sl = slice(128 * ci, 128 * ci + 128)
                p.mm(bk[0][:, sl], B2[:, sl], X[:, sl])
            yield
            p.tt(X2[:, :], bk[0][:, :], X[:, :], ALU.add)
            yield
            A, B, X = A2, B2, X2
        TT = X
        for ci in range(4):
            cp = 4 * gp + ci
            p.ts(s["rhs_u"][:, ci, :], VT[:, ci, :], b_all[:, cp:cp + 1], ALU.mult, eng="pool")
            p.ts(s["rhs_w"][:, ci, :], KT[:, ci, :], coefw[:, cp:cp + 1], ALU.mult, eng="pool")
            p.ts(s["ktail%d" % ci][:, :], KT[:, ci, :], coeft[:, cp:cp + 1], ALU.mult, eng="pool")
            yield
        for ci in range(4):
            sl = slice(128 * ci, 128 * ci + 128)
            p.mm(bk[0][:, 64 * ci:64 * ci + 64], TT[:, sl], s["rhs_u"][:, ci, :])
        yield
        for ci in range(4):
            p.copy(s["u%d" % ci][:, :], bk[0][:, 64 * ci:64 * ci + 64], eng="act")
        yield
        for ci in range(4):
            sl = slice(128 * ci, 128 * ci + 128)
            p.mm(bk[1][0:64, sl], s["rhs_w"][:, ci, :], TT[:, sl])
        yield
        for ci in range(4):
            p.copy(s["W3%d" % ci][:, 0:64], bk[1][0:64, 128 * ci:128 * ci + 64], eng="act")
            p.copy(s["W3%d" % ci][:, 128:192], bk[1][0:64, 128 * ci + 64:128 * ci + 128], eng="act")
        yield
        for ci in range(4):
            sl = slice(128 * ci, 128 * ci + 128)
            p.mm(bk[2][:, sl], s["kTb"][:, sl], s["qTb"][:, sl])
        yield
        for ci in range(4):
            sl = slice(128 * ci, 128 * ci + 128)
            p.tt(s["QKT%d" % ci][:, :], bk[2][:, sl], s["decayT"][:, sl], ALU.mult)
            p.tt(s["qdT%d" % ci][:, :], Q[:, sl], s["eg"][:, sl], ALU.mult, eng="pool")
        yield

    def scan(gp, s):
        T0 = 512 * gp
        OS = s["OS"]
        for ci in range(4):
            c0 = 128 * ci
            for half in range(2):
                lo = 64 * half
                S, S2 = Sb[state["si"] % 2], Sb[(state["si"] + 1) % 2]
                state["si"] += 1
                p.mm(bk[5][:, 0:64], s["W3%d" % ci][:, lo:lo + 128], S[:, :])
                yield
                p.tt(vnew[lo:lo + 64, :], s["u%d" % ci][lo:lo + 64, :], bk[5][lo:lo + 64, 0:64], ALU.subtract)
                yield
                p.mm(bk[6][0:64, 0:64], s["qdT%d" % ci][:, lo:lo + 64], S[:, :], start=True, stop=False)
                p.mm(bk[6][0:64, 0:64], s["QKT%d" % ci][lo:lo + 64, lo:lo + 64], vnew[lo:lo + 64, :], start=False, stop=True)
                yield
                p.copy(OS[:, 2 * ci + half, :], bk[6][0:64, 0:64], eng="act")
                p.mm(bk[7][0:64, 0:64], s["ktail%d" % ci][lo:lo + 64, :], vnew[lo:lo + 64, :])
                yield
                p.stt(S2[:, :], S[:, :], s["egS"][:, 2 * ci + half:2 * ci + half + 1], bk[7][0:64, 0:64], ALU.mult, ALU.add)
                yield
        p.dma("sp", o.v(o.t[T0:T0 + 512, :].rearrange("(c p) d -> p c d", p=64)), OS[:, :, :])
        yield

    gens = {0: pre(0, sets[0])}
    next(gens[0])
    for gp in range(NGP):
        for _ in gens.pop(gp):
            pass
        p.barrier()
        if gp + 1 < NGP:
            gens[gp + 1] = pre(gp + 1, sets[(gp + 1) % 2])
            next(gens[gp + 1])
        for _ in scan(gp, sets[gp % 2]):
            pass
        p.barrier()
    return p.finish()


TOK = 2048
D = 1024
EPS = 1e-6


def build_k3():
    p = Prog()
    IN, OUT = "ExternalInput", "ExternalOutput"
    xT = p.dram("xT", [D, TOK], F32, IN)
    gn = p.dram("gn", [128, 8], F32, IN)
    wzg = p.dram("wzg", [D, 3328], F32, IN)
    gdng = p.dram("gdng", [64, 1], F32, IN)
    wb = p.dram("wb", [64, 16 * D], F32, IN)
    wout = p.dram("wout", [D, D], F32, IN)
    ynaT = p.dram("ynaT", [4, 64, TOK], F32, IN)
    yswaT = p.dram("yswaT", [8, 64, TOK], F32, IN)
    ofT = p.dram("ofT", [4, 64, TOK], F32, IN)
    obT = p.dram("obT", [4, 64, TOK], F32, IN)
    x1T = p.dram("x1T", [D, TOK], F32, OUT)

    ps = [p.psum([128, 512], F32, "ps%d" % i) for i in range(8)]
    wzg_sb = [p.sbuf([128, 3328], BF16, "wzg%d" % k) for k in range(8)]
    wb_sb = p.sbuf([64, 16, D], BF16, "wb_sb")
    wout_sb = [p.sbuf([128, D], BF16, "wout%d" % k) for k in range(8)]
    gn_sb = p.sbuf([128, 8], F32, "gn_sb")
    gdng_sb = p.sbuf([64, 1], F32, "gdng_sb")
    ones128 = p.sbuf([128, 128], BF16, "ones128")
    ones64 = p.sbuf([64, 64], BF16, "ones64")
    p.memset(ones128[:, :], 1.0 / D)
    p.memset(ones64[:, :], 1.0 / 64)
    epsc = p.sbuf([128, 1], F32, "epsc")
    p.memset(epsc[:, :], EPS)
    p.dma("sp", gn_sb[:, :], gn[:, :])
    p.dma("sp", gdng_sb[:, :], gdng[:, :])
    for k in range(8):
        p.dma("pool", wzg_sb[k][:, :], wzg[k * 128:(k + 1) * 128, :], max_dma_last_dim=8192)
    for j in range(16):
        p.dma("pool", wb_sb[:, j, :], wb[:, j * D:(j + 1) * D])
    for k in range(8):
        p.dma("pool", wout_sb[k][:, :], wout[k * 128:(k + 1) * 128, :])

    X = p.sbuf([128, 8, 512], F32, "X")
    sq = p.sbuf([128, 8, 512], BF16, "sq")
    rstd = p.sbuf([128, 512], F32, "rstd")
    hT = p.sbuf([128, 8, 512], BF16, "hT")
    yna = p.sbuf([64, 4, 512], BF16, "yna")
    yswa = p.sbuf([64, 8, 512], BF16, "yswa")
    of = p.sbuf([64, 4, 512], F32, "of")
    ob = p.sbuf([64, 4, 512], F32, "ob")
    yg = p.sbuf([64, 4, 512], BF16, "yg")
    sz = p.sbuf([64, 512], F32, "sz")
    osum = p.sbuf([64, 512], F32, "osum")
    sq2 = p.sbuf([64, 512], BF16, "sq2")
    r2 = p.sbuf([64, 512], F32, "r2")
    sg = [p.sbuf([128, 512], F32, "sg%d" % i) for i in range(2)]
    macc = p.sbuf([128, 512], F32, "macc")
    mtmp = p.sbuf([128, 512], F32, "mtmp")
    merged = p.sbuf([128, 8, 512], BF16, "merged")
    xo = [p.sbuf([128, 512], F32, "xo%d" % i) for i in range(2)]

    for b in range(TOK // 512):
        c0 = 512 * b
        p.dma("sp", X[:, :, :], xT.v(xT.t[:, c0:c0 + 512].rearrange("(k p) t -> p k t", p=128)))
        p.dma("pool", yna[:, :, :], ynaT.v(ynaT.t[:, :, c0:c0 + 512].rearrange("h d t -> d h t")))
        p.dma("pool", yswa[:, :, :], yswaT.v(yswaT.t[:, :, c0:c0 + 512].rearrange("h d t -> d h t")))
        p.dma("sp", of[:, :, :], ofT.v(ofT.t[:, :, c0:c0 + 512].rearrange("h d t -> d h t")))
        p.dma("sp", ob[:, :, :], obT.v(obT.t[:, :, c0:c0 + 512].rearrange("h d t -> d h t")))
        for k in range(8):
            p.act(sq[:, k, :], X[:, k, :], AF.Square)
        for k in range(8):
            p.mm(ps[0][:, :], ones128[:, :], sq[:, k, :], start=(k == 0), stop=(k == 7))
        p.act(rstd[:, :], ps[0][:, :], AF.Ln, bias=epsc[:, 0:1])
        p.act(rstd[:, :], rstd[:, :], AF.Exp, scale=-0.5)
        for k in range(8):
            p.stt(hT[:, k, :], X[:, k, :], gn_sb[:, k:k + 1], rstd[:, :], ALU.mult, ALU.mult)
        for h in range(4):
            pz = ps[1]
            for k in range(8):
                p.mm(pz[0:64, :], wzg_sb[k][:, h * 64:(h + 1) * 64], hT[:, k, :], start=(k == 0), stop=(k == 7))
            p.act(sz[:, :], pz[0:64, :], AF.Silu)
            p.tt(osum[:, :], of[:, h, :], ob[:, h, :], ALU.add, eng="pool")
            p.act(sq2[:, :], osum[:, :], AF.Square)
            p.mm(ps[2][0:64, :], ones64[:, :], sq2[:, :])
            p.act(r2[:, :], ps[2][0:64, :], AF.Ln, bias=epsc[0:64, 0:1])
            p.act(r2[:, :], r2[:, :], AF.Exp, scale=-0.5)
            p.stt(osum[:, :], osum[:, :], gdng_sb[:, 0:1], r2[:, :], ALU.mult, ALU.mult)
            p.tt(yg[:, h, :], osum[:, :], sz[:, :], ALU.mult)
        for ot in range(8):
            n0 = ot * 128
            pbr = [ps[3], ps[4], ps[5]]
            for h in range(4):
                p.mm(pbr[0][:, :], wb_sb[:, h, n0:n0 + 128], yna[:, h, :], start=(h == 0), stop=(h == 3))
            for h in range(8):
                p.mm(pbr[1][:, :], wb_sb[:, 4 + h, n0:n0 + 128], yswa[:, h, :], start=(h == 0), stop=(h == 7))
            for h in range(4):
                p.mm(pbr[2][:, :], wb_sb[:, 12 + h, n0:n0 + 128], yg[:, h, :], start=(h == 0), stop=(h == 3))
            for br in range(3):
                pg = ps[6 + br % 2]
                gcol = 256 + br * D + n0
                for k in range(8):
                    p.mm(pg[:, :], wzg_sb[k][:, gcol:gcol + 128], hT[:, k, :], start=(k == 0), stop=(k == 7))
                S = sg[br % 2]
                p.act(S[:, :], pg[:, :], AF.Sigmoid)
                if br == 0:
                    p.tt(macc[:, :], S[:, :], pbr[0][:, :], ALU.mult)
                elif br == 1:
                    p.tt(mtmp[:, :], S[:, :], pbr[1][:, :], ALU.mult)
                    p.tt(macc[:, :], macc[:, :], mtmp[:, :], ALU.add, eng="pool")
                else:
                    p.tt(mtmp[:, :], S[:, :], pbr[2][:, :], ALU.mult)
                    p.tt(merged[:, ot, :], macc[:, :], mtmp[:, :], ALU.add, eng="pool")
        for ot in range(8):
            po = ps[1 + ot % 2]
            for k in range(8):
                p.mm(po[:, :], wout_sb[k][:, ot * 128:(ot + 1) * 128], merged[:, k, :], start=(k == 0), stop=(k == 7))
            O = xo[ot % 2]
            p.tt(O[:, :], po[:, :], X[:, ot, :], ALU.add)
            p.dma("sp", x1T[ot * 128:(ot + 1) * 128, c0:c0 + 512], O[:, :])
    return p.finish()


NB = 410
NBLK = 5
TOK = 2048
D = 1024
DFF = 2816
NCT = 44


def build_k4():
    p = Prog()
    xT = p.dram("xT", [D, TOK + 2], F32, "ExternalInput")
    gn = p.dram("gn", [128, 8], F32, "ExternalInput")
    wup = p.dram("wup", [D, 2 * DFF], F32, "ExternalInput")
    wdn = p.dram("wdn", [DFF, D], F32, "ExternalInput")
    cw = p.dram("cw", [128, NCT * 3], F32, "ExternalInput")
    cb = p.dram("cb", [128, NCT], F32, "ExternalInput")
    oT = p.dram("oT", [D, TOK], F32, "ExternalOutput")

    wup_sb = [p.sbuf([128, 2 * DFF], BF16, "wup%d" % k) for k in range(8)]
    gn_sb = p.sbuf([128, 8], F32, "gn_sb")
    cw_sb = p.sbuf([128, NCT * 3], F32, "cw_sb")
    cb_sb = p.sbuf([128, NCT], F32, "cb_sb")
    ones = p.sbuf([128, 128], BF16, "ones")
    prev2 = p.sbuf([128, NCT * 2], F32, "prev2")
    xb = [p.sbuf([128, 8, NB + 1], F32, "xb%d" % i) for i in range(2)]
    sq = p.sbuf([128, 8, NB], BF16, "sq")
    rstd = p.sbuf([128, NB], F32, "rstd")
    hT = p.sbuf([128, 8, NB], BF16, "hT")
    ub = [p.sbuf([128, NB + 2], F32, "ub%d" % i) for i in range(4)]
    ya = [p.sbuf([128, NB], F32, "ya%d" % i) for i in range(2)]
    yb = [p.sbuf([128, NB], F32, "yb%d" % i) for i in range(2)]
    mT = p.sbuf([128, 22, NB], BF16, "mT")
    wd_sb = [p.sbuf([128, 22, 128], BF16, "wd%d" % i) for i in range(3)]
    ob = [p.sbuf([128, NB], F32, "ob%d" % i) for i in range(2)]
    ps_u = [p.psum([128, 512], F32, "psu%d" % i) for i in range(4)]
    ps_o = [p.psum([128, 512], F32, "pso%d" % i) for i in range(2)]
    ps_s = p.psum([128, 512], F32, "pss")

    p.memset(ones[:, :], 1.0 / D)
    p.memset(prev2[:, :], 0.0)
    epsc = p.sbuf([128, 1], F32, "epsc")
    p.memset(epsc[:, :], 1e-6)
    p.dma("sp", gn_sb[:, :], gn[:, :])
    p.dma("sp", cw_sb[:, :], cw[:, :])
    p.dma("sp", cb_sb[:, :], cb[:, :])
    for k in range(8):
        p.dma("pool", wup_sb[k][:, :], wup[k * 128:(k + 1) * 128, :], max_dma_last_dim=8192)

    wdi = 0
    for b in range(NBLK):
        c0 = NB * b
        start = max(c0 - 1, 0)
        off = c0 - start
        ncols = c0 + NB - start
        X = xb[b % 2]
        p.dma("sp", X[:, :, 0:ncols], xT.v(xT.t[:, start:c0 + NB].rearrange("(k p) t -> p k t", p=128)))
        for k in range(8):
            p.act(sq[:, k, :], X[:, k, off:off + NB], AF.Square)
        for k in range(8):
            p.mm(ps_s[:, 0:NB], ones[:, :], sq[:, k, :], start=(k == 0), stop=(k == 7))
        p.act(rstd[:, :], ps_s[:, 0:NB], AF.Ln, bias=epsc[:, 0:1])
        p.act(rstd[:, :], rstd[:, :], AF.Exp, scale=-0.5)
        for k in range(8):
            p.stt(hT[:, k, :], X[:, k, off:off + NB], gn_sb[:, k:k + 1], rstd[:, :], ALU.mult, ALU.mult)
        for c in range(22):
            ys = []
            for half in range(2):
                ct = c + 22 * half
                pu = ps_u[(2 * c + half) % 4]
                U = ub[(2 * c + half) % 4]
                for k in range(8):
                    p.mm(pu[:, 0:NB], wup_sb[k][:, ct * 128:(ct + 1) * 128], hT[:, k, :], start=(k == 0), stop=(k == 7))
                p.copy(U[:, 0:2], prev2[:, 2 * ct:2 * ct + 2], eng="pool")
                p.act(U[:, 2:NB + 2], pu[:, 0:NB], AF.Copy)
                p.copy(prev2[:, 2 * ct:2 * ct + 2], U[:, NB:NB + 2], eng="pool")
                Y = (ya if half == 0 else yb)[c % 2]
                p.ts(Y[:, :], U[:, 0:NB], cw_sb[:, 3 * ct:3 * ct + 1], ALU.mult, cb_sb[:, ct:ct + 1], ALU.add)
                p.stt(Y[:, :], U[:, 1:NB + 1], cw_sb[:, 3 * ct + 1:3 * ct + 2], Y[:, :], ALU.mult, ALU.add)
                p.stt(Y[:, :], U[:, 2:NB + 2], cw_sb[:, 3 * ct + 2:3 * ct + 3], Y[:, :], ALU.mult, ALU.add)
                ys.append(Y)
            p.act(ys[0][:, :], ys[0][:, :], AF.Silu)
            p.tt(mT[:, c, :], ys[0][:, :], ys[1][:, :], ALU.mult, eng="pool")
        i0 = 2 if b == 0 else 0
        for o in range(8):
            W = wd_sb[wdi % 3]
            wdi += 1
            p.dma("pool", W[:, :, :], wdn.v(wdn.t[:, o * 128:(o + 1) * 128].rearrange("(c p) n -> p c n", p=128)))
            po = ps_o[o % 2]
            for c in range(22):
                p.mm(po[:, 0:NB], W[:, c, :], mT[:, c, :], start=(c == 0), stop=(c == 21))
            O = ob[o % 2]
            xi = c0 - 1 + i0 - start
            p.tt(O[:, i0:NB], po[:, i0:NB], X[:, o, xi:xi + NB - i0], ALU.add)
            oc = c0 - 2 + i0
            p.dma("sp", oT[o * 128:(o + 1) * 128, oc:oc + NB - i0], O[:, i0:NB])
    return p.finish()


L = 16384
TOK = 2048
HALO = 256
THETA = 10000.0


def f32(a):
    return np.ascontiguousarray(a, dtype=np.float32)


def k1_consts():
    R = np.zeros((64, 64), np.float32)
    for i in range(32):
        R[i, i + 32] = -1.0
        R[i + 32, i] = 1.0
    k = np.arange(128)[:, None]
    q = np.arange(128)[None, :]
    mprev = np.tile((k >= q).astype(np.float32), (1, 4))
    mnext = np.tile((k <= q).astype(np.float32), (1, 4))
    qc = np.arange(64)
    cs = np.clip(qc - 8, 0, 48)
    kc = np.arange(64)
    col_in = (kc[None, :] >= cs[:, None]) & (kc[None, :] < cs[:, None] + 16)
    cmask = np.broadcast_to(col_in.T[:, None, :], (64, 4, 64)).astype(np.float32).reshape(64, 256)
    return dict(rotT=f32(R.T), mprev=f32(mprev), mnext=f32(mnext), cmask=f32(cmask))


def rope_tabs(pos):
    inv = (1.0 / (THETA ** (np.arange(0, 64, 2, dtype=np.float32) / np.float32(64)))).astype(np.float32)
    ang = pos.astype(np.float32)[:, None] * inv[None, :]
    c, s = np.cos(ang).astype(np.float32), np.sin(ang).astype(np.float32)
    return f32(np.concatenate([c, c], 1).T), f32(np.concatenate([s, s], 1).T)


def na_tables(rpb, c):
    kc = np.arange(64)[:, None]
    qc = np.arange(64)[None, :]
    dc = np.clip(kc - qc + 15, 0, 30)
    g = rpb[:, :, dc]
    g = np.transpose(g, (2, 1, 0, 3))
    shared = g[:, 3:11].reshape(64, 8 * 256)
    edge = np.full((64, 7, 12, 4, 64), -30000.0, np.float32)
    for e in range(7):
        qr = e if e < 4 else 29 + (e - 4)
        r = 32 * c + qr
        rs = min(max(r - 4, 0), 248)
        base = qr if e < 4 else 28
        for s in range(12):
            kr = base + s
            Rg = 32 * c - 4 + kr
            if rs <= Rg < rs + 8 and 0 <= Rg < 256:
                edge[:, e, s] = g[:, Rg - r + 7]
    return f32(shared), f32(edge.reshape(64, 7 * 12 * 256))


def prep_k1(inp, l, c, consts, xpad):
    s = c * TOK
    m = dict(consts)
    m["xT"] = f32(xpad[s:s + TOK + 2 * HALO].T)
    m["gn"] = f32(inp["attn_norm"][l].reshape(8, 128).T)
    w_in = inp["w_in"][l]
    m["w"] = f32(np.concatenate([w_in[:, 0:2304], w_in[:, 2560:2576]], 1))
    m["qkg"] = f32(inp["qk_norm"][l].T)
    pos = np.arange(s - HALO, s + TOK + HALO)
    m["cosT"], m["sinT"] = rope_tabs(pos)
    m["ebt"], m["ebe"] = na_tables(inp["na_rpb"][l], c)
    m["sinkb"] = f32(np.broadcast_to(inp["swa_sink"][l][None, :], (64, 8)))
    m["gcw"] = f32(inp["gdn_conv_w"][l].T.reshape(12, 64, 3).transpose(1, 0, 2).reshape(64, 36))
    m["alog"] = f32(inp["gdn_a_log"][l].reshape(8, 1))
    m["dtb"] = f32(inp["gdn_dt_bias"][l].reshape(8, 1))
    m["mprev0"] = consts["mprev"] if c > 0 else np.zeros_like(consts["mprev"])
    m["mnextL"] = consts["mnext"] if c < 7 else np.zeros_like(consts["mnext"])
    return m


def k2_consts():
    i = np.arange(128)[:, None]
    j = np.arange(128)[None, :]
    same = (i // 64) == (j // 64)
    low_incl = same & (j <= i)
    low_strict = same & (j < i)
    trisel = np.zeros((128, 130), np.float32)
    trisel[:, :128] = low_incl.T
    trisel[:64, 128] = 1.0
    trisel[64:, 129] = 1.0
    MB = np.where(low_incl, 0.0, -30000.0)
    t4 = lambda a: f32(np.tile(a, (1, 4)))
    return dict(ident=t4(np.eye(128)), trisel=f32(trisel), bones=f32(same), MB=t4(MB), MBT=t4(MB.T),
                SM=t4(low_strict), SMT=t4(low_strict.T))


def prep_k2(consts, qT, kT, vT, beta, g, flip):
    if flip:
        qT, kT, vT, beta, g = qT[:, ::-1], kT[:, ::-1], vT[:, ::-1], beta[::-1], g[::-1]
    m = dict(consts)
    m["qT"] = f32(qT)
    m["kT"] = f32(kT)
    m["ktm"] = f32(kT.T)
    m["vtm"] = f32(vT.T)
    m["brow"] = f32(beta.reshape(1, -1))
    m["gcp"] = f32(g.reshape(128, 128).T)
    m["bcp"] = f32(beta.reshape(128, 128).T)
    return m


def prep_k3_w(inp, l):
    w_in = inp["w_in"][l]
    m = {}
    m["gn"] = f32(inp["attn_norm"][l].reshape(8, 128).T)
    m["wzg"] = f32(np.concatenate([w_in[:, 2304:2560], w_in[:, 2576:5648]], 1))
    m["gdng"] = f32(inp["gdn_norm"][l].reshape(64, 1))
    wb = np.concatenate([inp["w_branch_na"][l].reshape(4, 64, 1024), inp["w_branch_swa"][l].reshape(8, 64, 1024),
                         inp["w_branch_gdn"][l].reshape(4, 64, 1024)], 0)
    m["wb"] = f32(wb.transpose(1, 0, 2).reshape(64, 16 * 1024))
    m["wout"] = f32(inp["w_out"][l])
    return m


def prep_k4_w(inp, l):
    return dict(gn=f32(inp["ffn_norm"][l].reshape(8, 128).T),
                cw=f32(inp["ffn_conv_w"][l].T.reshape(44, 128, 3).transpose(1, 0, 2).reshape(128, 132)),
                cb=f32(inp["ffn_conv_b"][l].reshape(44, 128).T),
                wup=f32(inp["w_up"][l]), wdn=f32(inp["w_down"][l]))


from concourse.bass_utils import run_bass_kernel_spmd

NCORES = 8


def _run(nc, maps):
    res = run_bass_kernel_spmd(nc, maps, core_ids=list(range(NCORES)))
    return res.results


def kernel(**inp):
    inp = {k: np.asarray(v) for k, v in inp.items()}
    x = f32(inp["x"][0])
    Lq = x.shape[0]
    c1 = k1_consts()
    c2 = k2_consts()
    T = 2048
    for l in range(4):
        z = np.zeros((256, 1024), np.float32)
        xpad = np.concatenate([z, x, z], 0)
        r1 = _run(build_k1(), [prep_k1(inp, l, c, c1, xpad) for c in range(NCORES)])
        gq = np.concatenate([r["gqkv"] for r in r1], 2)
        bT = np.concatenate([r["betaT"] for r in r1], 1)
        gT = np.concatenate([r["gT"] for r in r1], 1)
        maps = []
        for c in range(NCORES):
            hh = c % 4
            maps.append(prep_k2(c2, gq[hh], gq[4 + hh], gq[8 + hh], bT[c], gT[c], flip=(c >= 4)))
        r2 = _run(build_k2(), maps)
        o = [r2[c]["o"] if c < 4 else r2[c]["o"][::-1] for c in range(NCORES)]
        wm = prep_k3_w(inp, l)
        maps = []
        for c in range(NCORES):
            s, e = c * T, (c + 1) * T
            m = dict(wm)
            m["xT"] = f32(x[s:e].T)
            m["ynaT"] = r1[c]["ynaT"]
            m["yswaT"] = r1[c]["yswaT"]
            m["ofT"] = f32(np.stack([o[h][s:e].T for h in range(4)], 0))
            m["obT"] = f32(np.stack([o[4 + h][s:e].T for h in range(4)], 0))
            maps.append(m)
        r3 = _run(build_k3(), maps)
        x1 = np.concatenate([r["x1T"].T for r in r3], 0)
        z1 = np.zeros((1, 1024), np.float32)
        x1pad = np.concatenate([z1, x1, z1], 0)
        w4 = prep_k4_w(inp, l)
        maps = []
        for c in range(NCORES):
            m = dict(w4)
            m["xT"] = f32(x1pad[c * T:c * T + T + 2].T)
            maps.append(m)
        r4 = _run(build_k4(), maps)
        x = f32(np.concatenate([r["oT"].T for r in r4], 0))
    return x[None].astype(np.float32)
```
